# Optimizing a Trainium2 kernel written in Bass

```python
import math
import functools
import jax
import jax.numpy as jnp
from jax import lax
import numpy as np

D_MODEL = 2048
BATCH = 4
SEQ = 4096
DEPTH = 2

GRID_W = 64
CTX_LEN = 256
CHUNK = 64
CONV_K = 3
N_MOD = 9
N_BRANCH = 3
D_FF = 11 * D_MODEL // 4
SSD_HEAD_DIM = 64
SSD_INNER = D_MODEL // 2
SSD_HEADS = SSD_INNER // SSD_HEAD_DIM
SSD_GROUPS = 4
SSD_HPG = SSD_HEADS // SSD_GROUPS
SSD_STATE = 64
SSD_XBC = SSD_INNER + 2 * SSD_GROUPS * SSD_STATE
ML_WIDTH = D_MODEL // 4
ML_HEADS = 4
ML_HEAD_DIM = ML_WIDTH // ML_HEADS
GLA_WIDTH = D_MODEL // 4
GLA_HEADS = 4
GLA_DV = GLA_WIDTH // GLA_HEADS
GLA_KEY_WIDTH = GLA_WIDTH // 2
GLA_DK = GLA_KEY_WIDTH // GLA_HEADS
GLA_RANK = 16
GLA_TAU = 16.0
DEEPNORM_ALPHA = (2 * DEPTH) ** 0.25
DEEPNORM_BETA = (8 * DEPTH) ** -0.25
EPS = 1e-5
IN_SIZES = (SSD_INNER, SSD_XBC, 2 * SSD_HEADS,
            ML_WIDTH, ML_WIDTH, ML_WIDTH, ML_WIDTH, 4 * ML_HEADS,
            GLA_KEY_WIDTH, GLA_KEY_WIDTH, GLA_WIDTH, GLA_WIDTH, 2 * GLA_RANK,
            N_BRANCH * D_MODEL)
IN_TOTAL = sum(IN_SIZES)

kernel_name = 'hybrid_ssd_mlstm_gla_diffusion_block'


def _split(t, sizes):
    idx = [int(s) for s in np.cumsum(sizes)[:-1]]
    return jnp.split(t, idx, axis=-1)


def _chunks(t):
    return t.reshape(t.shape[0], t.shape[1] // CHUNK, CHUNK, *t.shape[2:])


def _causal_mask():
    return jnp.tril(jnp.ones((CHUNK, CHUNK), dtype=bool))


def layer_norm(x, g, b):
    xf = x.astype(jnp.float32)
    mu = jnp.mean(xf, axis=-1, keepdims=True)
    var = jnp.mean(jnp.square(xf - mu), axis=-1, keepdims=True)
    return ((xf - mu) * lax.rsqrt(var + EPS)).astype(x.dtype) * g + b


def group_norm(y, w, groups, center):
    shp = y.shape
    yf = y.astype(jnp.float32).reshape(*shp[:-1], groups, shp[-1] // groups)
    if center:
        yf = yf - jnp.mean(yf, axis=-1, keepdims=True)
    yf = yf * lax.rsqrt(jnp.mean(jnp.square(yf), axis=-1, keepdims=True) + EPS)
    return yf.reshape(shp).astype(y.dtype) * w


def swiglu(u, w_in, w_out):
    a, g = jnp.split(u @ w_in, 2, axis=-1)
    return (jax.nn.silu(g) * a) @ w_out


def short_conv(u, uc, w, b):
    bsz, T, ch = u.shape
    rows = T // GRID_W
    grid = u.reshape(bsz, rows, GRID_W, ch)
    y = lax.conv_general_dilated(grid, w[:, :, None, :], window_strides=(1, 1), padding='SAME',
                                 dimension_numbers=('NHWC', 'HWIO', 'NHWC'), feature_group_count=ch)
    yc = lax.conv_general_dilated(uc, w[1][:, None, :], window_strides=(1,), padding='SAME',
                                  dimension_numbers=('NWC', 'WIO', 'NWC'), feature_group_count=ch)
    return y.reshape(bsz, T, ch) + b, yc + b


def ssd_chunked(x, dt, bm, cm, state, need_out, a):
    out_dtype = x.dtype
    bsz, T = x.shape[:2]
    x, dt, bm, cm = (_chunks(t.astype(jnp.float32)) for t in (x, dt, bm, cm))
    acum = jnp.cumsum(dt * a, axis=2)
    a_end = acum[:, :, -1]
    w_end = jnp.exp(a_end[:, :, None] - acum) * dt
    s_loc = jnp.einsum('bclgn,bclgh,bclghp->bcghpn', bm, w_end, x)

    def step(h, inp):
        dec, s = inp
        return h * jnp.exp(dec)[..., None, None] + s, h

    h_fin, h_in = lax.scan(step, state, (jnp.moveaxis(a_end, 1, 0), jnp.moveaxis(s_loc, 1, 0)))
    if not need_out:
        return None, h_fin
    h_in = jnp.moveaxis(h_in, 0, 1)
    seg = acum[:, :, :, None] - acum[:, :, None, :]
    decay = jnp.exp(jnp.where(_causal_mask()[:, :, None, None], seg, -jnp.inf))
    cb = jnp.einsum('bclgn,bcsgn->bclsg', cm, bm)
    y = jnp.einsum('bclsg,bclsgh,bcsgh,bcsghp->bclghp', cb, decay, dt, x)
    y = y + jnp.einsum('bclgn,bcghpn,bclgh->bclghp', cm, h_in, jnp.exp(acum))
    return y.reshape(bsz, T, *y.shape[3:]).astype(out_dtype), h_fin


def mlstm_chunked(q, k, v, i_pre, f_pre, state, need_out):
    out_dtype = v.dtype
    bsz, T = v.shape[:2]
    q, k, v, ig, fg = (_chunks(t.astype(jnp.float32)) for t in (q, k, v, i_pre, f_pre))
    q = q * (ML_HEAD_DIM ** -0.5)
    b = jnp.cumsum(jax.nn.log_sigmoid(fg), axis=2)
    b_end = b[:, :, -1]
    a = b_end[:, :, None] - b + ig
    m_loc = jnp.max(a, axis=2)
    w = jnp.exp(a - m_loc[:, :, None])
    c_loc = jnp.einsum('bcjh,bcjhv,bcjhk->bchvk', w, v, k)
    n_loc = jnp.einsum('bcjh,bcjhk->bchk', w, k)

    def step(carry, inp):
        c_st, n_st, m_st = carry
        be, ml, cl, nl = inp
        m_new = jnp.maximum(be + m_st, ml)
        f_old = jnp.exp(be + m_st - m_new)
        f_new = jnp.exp(ml - m_new)
        new = (f_old[..., None, None] * c_st + f_new[..., None, None] * cl,
               f_old[..., None] * n_st + f_new[..., None] * nl,
               m_new)
        return new, carry

    fin, ent = lax.scan(step, state, tuple(jnp.moveaxis(t, 1, 0) for t in (b_end, m_loc, c_loc, n_loc)))
    if not need_out:
        return None, fin
    c_in, n_in, m_in = (jnp.moveaxis(t, 0, 1) for t in ent)
    dmat = b[:, :, :, None, :] - b[:, :, None, :, :] + ig[:, :, None, :, :]
    dmat = jnp.where(_causal_mask()[:, :, None], dmat, -jnp.inf)
    g = b + m_in[:, :, None, :]
    m_t = jnp.maximum(jnp.max(dmat, axis=3), g)
    s = jnp.exp(dmat - m_t[:, :, :, None, :]) * jnp.einsum('bcthk,bcjhk->bctjh', q, k)
    g_w = jnp.exp(g - m_t)
    num = jnp.einsum('bctjh,bcjhv->bcthv', s, v) + g_w[..., None] * jnp.einsum('bcthk,bchvk->bcthv', q, c_in)
    den = jnp.sum(s, axis=3) + g_w * jnp.einsum('bcthk,bchk->bcth', q, n_in)
    h = num / jnp.maximum(jnp.abs(den), jnp.exp(-m_t))[..., None]
    return h.reshape(bsz, T, *h.shape[3:]).astype(out_dtype), fin


def gla_chunked(q, k, v, log_a, state, need_out):
    out_dtype = v.dtype
    bsz, T = v.shape[:2]
    q, k, v, log_a = (_chunks(t.astype(jnp.float32)) for t in (q, k, v, log_a))
    q = q * (GLA_DK ** -0.5)
    b = jnp.cumsum(log_a, axis=2)
    b_end = b[:, :, -1]
    s_loc = jnp.einsum('bcjhk,bcjhv->bchkv', k * jnp.exp(b_end[:, :, None] - b), v)

    def step(s_st, inp):
        be, sl = inp
        return jnp.exp(be)[..., None] * s_st + sl, s_st

    s_fin, s_in = lax.scan(step, state, (jnp.moveaxis(b_end, 1, 0), jnp.moveaxis(s_loc, 1, 0)))
    if not need_out:
        return None, s_fin
    s_in = jnp.moveaxis(s_in, 0, 1)
    q_d = q * jnp.exp(b)
    k_d = k * jnp.exp(-b)
    att = jnp.where(_causal_mask(), jnp.einsum('bcthk,bcjhk->bchtj', q_d, k_d), 0.0)
    o = jnp.einsum('bchtj,bcjhv->bcthv', att, v) + jnp.einsum('bcthk,bchkv->bcthv', q_d, s_in)
    return o.reshape(bsz, T, *o.shape[3:]).astype(out_dtype), s_fin


def bidirectional(fn_f, fn_b, ctx_f, lat_f, ctx_b, lat_b, state0, need_ctx_out):
    rev = lambda ts: tuple(jnp.flip(t, axis=1) for t in ts)
    yc_f, s_f = fn_f(*ctx_f, state0, need_ctx_out)
    yl_f, _ = fn_f(*lat_f, s_f, True)
    yc_b, s_b = fn_b(*rev(ctx_b), state0, need_ctx_out)
    yl_b, _ = fn_b(*rev(lat_b), s_b, True)
    y_lat = yl_f + jnp.flip(yl_b, axis=1)
    y_ctx = (yc_f + jnp.flip(yc_b, axis=1)) if need_ctx_out else None
    return y_lat, y_ctx


def ssd_branch(lat, ctx, need_ctx_out, conv_w, conv_b, dt_bias, a_log, d_skip, norm_w, w_br):
    (z, xbc, dt), (zc, xbcc, dtc) = lat, ctx
    xbc, xbcc = short_conv(xbc, xbcc, conv_w, conv_b)

    def prep(t_xbc, t_dt):
        bsz, T = t_xbc.shape[:2]
        xs, bm, cm = _split(jax.nn.silu(t_xbc), (SSD_INNER, SSD_GROUPS * SSD_STATE, SSD_GROUPS * SSD_STATE))
        dts = jax.nn.softplus(t_dt.reshape(bsz, T, 2, SSD_GROUPS, SSD_HPG).astype(jnp.float32)
                              + dt_bias.reshape(2, SSD_GROUPS, SSD_HPG).astype(jnp.float32))
        return (xs.reshape(bsz, T, SSD_GROUPS, SSD_HPG, SSD_HEAD_DIM),
                bm.reshape(bsz, T, SSD_GROUPS, SSD_STATE),
                cm.reshape(bsz, T, SSD_GROUPS, SSD_STATE),
                dts[:, :, 0], dts[:, :, 1])

    xs, bm, cm, dt_f, dt_b = prep(xbc, dt)
    xsc, bmc, cmc, dtc_f, dtc_b = prep(xbcc, dtc)
    a = -jnp.exp(a_log.astype(jnp.float32)).reshape(2, SSD_GROUPS, SSD_HPG)
    state0 = jnp.zeros((xs.shape[0], SSD_GROUPS, SSD_HPG, SSD_HEAD_DIM, SSD_STATE), jnp.float32)
    y, yc = bidirectional(functools.partial(ssd_chunked, a=a[0]), functools.partial(ssd_chunked, a=a[1]),
                          (xsc, dtc_f, bmc, cmc), (xs, dt_f, bm, cm),
                          (xsc, dtc_b, bmc, cmc), (xs, dt_b, bm, cm), state0, need_ctx_out)

    def finish(yy, xx, zz):
        yy = (yy + d_skip.reshape(SSD_GROUPS, SSD_HPG, 1) * xx).reshape(zz.shape)
        return group_norm(yy * jax.nn.silu(zz), norm_w, SSD_GROUPS, False) @ w_br

    return finish(y, xs, z), (finish(yc, xsc, zc) if need_ctx_out else None)


def mlstm_branch(lat, ctx, need_ctx_out, conv_w, conv_b, gate_b, norm_w, w_br):
    (q, k, v, o, g), (qc, kc, vc, oc, gc) = lat, ctx
    qk, qkc = short_conv(jnp.concatenate([q, k], axis=-1), jnp.concatenate([qc, kc], axis=-1), conv_w, conv_b)

    def prep(t_qk, t_v, t_g):
        bsz, T = t_v.shape[:2]
        hq, hk = jnp.split(jax.nn.silu(t_qk), 2, axis=-1)
        hd = (bsz, T, ML_HEADS, ML_HEAD_DIM)
        gates = t_g.reshape(bsz, T, 2, 2, ML_HEADS) + gate_b
        return hq.reshape(hd), hk.reshape(hd), t_v.reshape(hd), gates

    hq, hk, hv, gt = prep(qk, v, g)
    hqc, hkc, hvc, gtc = prep(qkc, vc, gc)
    bsz = hq.shape[0]
    state0 = (jnp.zeros((bsz, ML_HEADS, ML_HEAD_DIM, ML_HEAD_DIM), jnp.float32),
              jnp.zeros((bsz, ML_HEADS, ML_HEAD_DIM), jnp.float32),
              jnp.zeros((bsz, ML_HEADS), jnp.float32))
    y, yc = bidirectional(mlstm_chunked, mlstm_chunked,
                          (hqc, hkc, hvc, gtc[:, :, 0, 0], gtc[:, :, 0, 1]),
                          (hq, hk, hv, gt[:, :, 0, 0], gt[:, :, 0, 1]),
                          (hqc, hkc, hvc, gtc[:, :, 1, 0], gtc[:, :, 1, 1]),
                          (hq, hk, hv, gt[:, :, 1, 0], gt[:, :, 1, 1]), state0, need_ctx_out)

    def finish(hh, oo):
        hh = jax.nn.sigmoid(oo) * hh.reshape(oo.shape)
        return group_norm(hh, norm_w, ML_HEADS, True) @ w_br

    return finish(y, o), (finish(yc, oc) if need_ctx_out else None)


def gla_branch(lat, ctx, need_ctx_out, w2, b2, norm_w, w_br):
    (q, k, v, r, lr), (qc, kc, vc, rc, lrc) = lat, ctx

    def prep(tq, tk, tv, tlr):
        bsz, T = tq.shape[:2]
        kd = (bsz, T, GLA_HEADS, GLA_DK)
        lr2 = tlr.reshape(bsz, T, 2, GLA_RANK)
        log_a = [(jax.nn.log_sigmoid((lr2[:, :, d] @ w2[d] + b2[d]).astype(jnp.float32)) / GLA_TAU).reshape(kd)
                 for d in range(2)]
        return tq.reshape(kd), tk.reshape(kd), tv.reshape(bsz, T, GLA_HEADS, GLA_DV), log_a[0], log_a[1]

    gq, gk, gv, la_f, la_b = prep(q, k, v, lr)
    gqc, gkc, gvc, lac_f, lac_b = prep(qc, kc, vc, lrc)
    state0 = jnp.zeros((gq.shape[0], GLA_HEADS, GLA_DK, GLA_DV), jnp.float32)
    y, yc = bidirectional(gla_chunked, gla_chunked,
                          (gqc, gkc, gvc, lac_f), (gq, gk, gv, la_f),
                          (gqc, gkc, gvc, lac_b), (gq, gk, gv, la_b), state0, need_ctx_out)

    def finish(oo, rr):
        return (group_norm(oo.reshape(rr.shape), norm_w, GLA_HEADS, True) * jax.nn.silu(rr)) @ w_br

    return finish(y, r), (finish(yc, rc) if need_ctx_out else None)


def _merge(b_ssd, b_ml, b_gla, gate_pre, merge_b, w_out):
    g = jax.nn.sigmoid(gate_pre.reshape(*gate_pre.shape[:-1], N_BRANCH, D_MODEL) + merge_b)
    return (g[..., 0, :] * b_ssd + g[..., 1, :] * b_ml + g[..., 2, :] * b_gla) @ w_out


def token_mixer(u, uc, need_ctx_out, w_in, merge_b, ssd_conv_w, ssd_conv_b, ssd_dt_bias, ssd_a_log, ssd_d,
                ssd_norm_w, ml_conv_w, ml_conv_b, ml_gate_b, ml_norm_w, gla_w2, gla_b2, gla_norm_w,
                w_br_ssd, w_br_ml, w_br_gla, w_out):
    pl = _split(u @ w_in, IN_SIZES)
    pc = _split(uc @ w_in, IN_SIZES)
    s_lat, s_ctx = ssd_branch(pl[0:3], pc[0:3], need_ctx_out, ssd_conv_w, ssd_conv_b, ssd_dt_bias,
                              ssd_a_log, ssd_d, ssd_norm_w, w_br_ssd)
    m_lat, m_ctx = mlstm_branch(pl[3:8], pc[3:8], need_ctx_out, ml_conv_w, ml_conv_b, ml_gate_b,
                                ml_norm_w, w_br_ml)
    g_lat, g_ctx = gla_branch(pl[8:13], pc[8:13], need_ctx_out, gla_w2, gla_b2, gla_norm_w, w_br_gla)
    y = _merge(s_lat, m_lat, g_lat, pl[13], merge_b, w_out)
    yc = _merge(s_ctx, m_ctx, g_ctx, pc[13], merge_b, w_out) if need_ctx_out else None
    return y, yc


def _sub_in(h, m, k):
    return h * (1.0 + m[3 * k + 1]) + m[3 * k]


def _sub_out(h, y, m, k, g, b):
    return layer_norm(DEEPNORM_ALPHA * h + m[3 * k + 2] * y, g, b)


def setup_inputs(seed: int = 0) -> dict:
    key = jax.random.key(seed)
    ks = list(jax.random.split(key, 32))

    def nrm(i, shape, scale):
        return jax.random.normal(ks[i], shape, jnp.float32) * scale

    D, L = D_MODEL, DEPTH
    u_dt = jax.random.uniform(ks[30], (L, 2, SSD_HEADS), jnp.float32)
    dt0 = jnp.exp(u_dt * (math.log(0.1) - math.log(0.001)) + math.log(0.001))
    f_bias = jnp.stack([jnp.zeros((ML_HEADS,), jnp.float32), jnp.linspace(3.0, 6.0, ML_HEADS, dtype=jnp.float32)])
    return {
        'x': nrm(0, (BATCH, SEQ, D), 1.0),
        'c': nrm(1, (BATCH, D), 1.0),
        'ctx': nrm(2, (BATCH, CTX_LEN, D), 1.0),
        'c_ctx': nrm(3, (D,), 1.0),
        'w_mod': nrm(4, (L, D, N_MOD * D), 0.5 * D ** -0.5),
        'b_mod': nrm(5, (L, N_MOD * D), 0.02),
        'ln_g': 1.0 + nrm(6, (L, 3, D), 0.02),
        'ln_b': nrm(7, (L, 3, D), 0.02),
        'ffn_w_in': nrm(8, (L, 2, D, 2 * D_FF), D ** -0.5),
        'ffn_w_out': nrm(9, (L, 2, D_FF, D), DEEPNORM_BETA * D_FF ** -0.5),
        'w_in': nrm(10, (L, D, IN_TOTAL), D ** -0.5),
        'merge_b': nrm(11, (L, N_BRANCH, D), 0.02),
        'ssd_conv_w': nrm(12, (L, CONV_K, CONV_K, SSD_XBC), 1.0 / CONV_K),
        'ssd_conv_b': nrm(13, (L, SSD_XBC), 0.02),
        'ssd_dt_bias': dt0 + jnp.log(-jnp.expm1(-dt0)),
        'ssd_a_log': jnp.log(jax.random.uniform(ks[14], (L, 2, SSD_HEADS), jnp.float32, 1.0, 16.0)),
        'ssd_d': 1.0 + nrm(15, (L, SSD_HEADS), 0.1),
        'ssd_norm_w': 1.0 + nrm(16, (L, SSD_INNER), 0.02),
        'ml_conv_w': nrm(17, (L, CONV_K, CONV_K, 2 * ML_WIDTH), 1.0 / CONV_K),
        'ml_conv_b': nrm(18, (L, 2 * ML_WIDTH), 0.02),
        'ml_gate_b': nrm(19, (L, 2, 2, ML_HEADS), 0.1) + f_bias,
        'ml_norm_w': 1.0 + nrm(20, (L, ML_WIDTH), 0.02),
        'gla_w2': nrm(21, (L, 2, GLA_RANK, GLA_KEY_WIDTH), GLA_RANK ** -0.5),
        'gla_b2': nrm(22, (L, 2, GLA_KEY_WIDTH), 0.1),
        'gla_norm_w': 1.0 + nrm(23, (L, GLA_WIDTH), 0.02),
        'w_br_ssd': nrm(24, (L, SSD_INNER, D), DEEPNORM_BETA * SSD_INNER ** -0.5),
        'w_br_ml': nrm(25, (L, ML_WIDTH, D), DEEPNORM_BETA * ML_WIDTH ** -0.5),
        'w_br_gla': nrm(26, (L, GLA_WIDTH, D), DEEPNORM_BETA * GLA_WIDTH ** -0.5),
        'w_out': nrm(27, (L, D, D), DEEPNORM_BETA * D ** -0.5),
    }


def reference(x, c, ctx, c_ctx, w_mod, b_mod, ln_g, ln_b, ffn_w_in, ffn_w_out, w_in, merge_b,
              ssd_conv_w, ssd_conv_b, ssd_dt_bias, ssd_a_log, ssd_d, ssd_norm_w,
              ml_conv_w, ml_conv_b, ml_gate_b, ml_norm_w, gla_w2, gla_b2, gla_norm_w,
              w_br_ssd, w_br_ml, w_br_gla, w_out):
    h, hc = x, ctx
    for l in range(DEPTH):
        need_ctx_out = l < DEPTH - 1
        m_lat = jnp.moveaxis((jax.nn.silu(c) @ w_mod[l] + b_mod[l]).reshape(-1, N_MOD, D_MODEL), 1, 0)[:, :, None, :]
        m_ctx = (jax.nn.silu(c_ctx) @ w_mod[l] + b_mod[l]).reshape(N_MOD, D_MODEL)
        h = _sub_out(h, 0.5 * swiglu(_sub_in(h, m_lat, 0), ffn_w_in[l, 0], ffn_w_out[l, 0]), m_lat, 0, ln_g[l, 0], ln_b[l, 0])
        hc = _sub_out(hc, 0.5 * swiglu(_sub_in(hc, m_ctx, 0), ffn_w_in[l, 0], ffn_w_out[l, 0]), m_ctx, 0, ln_g[l, 0], ln_b[l, 0])
        y, yc = token_mixer(_sub_in(h, m_lat, 1), _sub_in(hc, m_ctx, 1), need_ctx_out, w_in[l], merge_b[l],
                            ssd_conv_w[l], ssd_conv_b[l], ssd_dt_bias[l], ssd_a_log[l], ssd_d[l], ssd_norm_w[l],
                            ml_conv_w[l], ml_conv_b[l], ml_gate_b[l], ml_norm_w[l],
                            gla_w2[l], gla_b2[l], gla_norm_w[l], w_br_ssd[l], w_br_ml[l], w_br_gla[l], w_out[l])
        h = _sub_out(h, y, m_lat, 1, ln_g[l, 1], ln_b[l, 1])
        h = _sub_out(h, 0.5 * swiglu(_sub_in(h, m_lat, 2), ffn_w_in[l, 1], ffn_w_out[l, 1]), m_lat, 2, ln_g[l, 2], ln_b[l, 2])
        if need_ctx_out:
            hc = _sub_out(hc, yc, m_ctx, 1, ln_g[l, 1], ln_b[l, 1])
            hc = _sub_out(hc, 0.5 * swiglu(_sub_in(hc, m_ctx, 2), ffn_w_in[l, 1], ffn_w_out[l, 1]), m_ctx, 2, ln_g[l, 2], ln_b[l, 2])
    return h
```

```python
import numpy as np
from contextlib import ExitStack
import concourse.bass as bass
import concourse.mybir as mybir
from concourse.bass_utils import run_bass_kernel_spmd

F32 = mybir.dt.float32
BF16 = mybir.dt.bfloat16
ALU = mybir.AluOpType
AF = mybir.ActivationFunctionType

D = 2048
DFF = 5632
NFC = 16
NCT = 2
NLT = 16
NCH = NCT + NLT
T = NCH * 128
DEPTH = 2
ALPHA = (2 * DEPTH) ** 0.25
EPS = 1e-5
INTOT = 12368
OFF = dict(z=0, xbc=1024, dt=2560, mq=2592, mk=3104, mv=3616, mo=4128, mg=4640,
           gq=4656, gk=4912, gv=5168, gr=5680, lr=6192, gate=6224)
NEG = -30000.0
_UQ = [0]


def uniq(n):
    _UQ[0] += 1
    return "%s_%d" % (n, _UQ[0])


class Buf:
    __slots__ = ("w", "r", "ex")

    def __init__(self, ex=False):
        self.w = None
        self.r = {}
        self.ex = ex


class DT:
    def __init__(self, nc, name, shape, dt, kind="Internal"):
        self.t = nc.dram_tensor(name, list(shape), dt, kind=kind)
        self.ap = self.t.ap()
        self.b = {}

    def bufs(self, c0, c1):
        return [self.b.setdefault(c, Buf()) for c in range(c0, c1)]

    def fm(self):
        return self.ap.rearrange("(c p) t -> p c t", p=128)


class Prog:
    def __init__(self, nc, es):
        self.nc = nc
        self.E = {"pe": nc.tensor, "act": nc.scalar, "dve": nc.vector, "pool": nc.gpsimd, "sp": nc.sync}
        self.R = 4
        self.EP = 700
        self.sem = {e: [es.enter_context(nc.semaphore("s_%s%d" % (e, i))) for i in range(self.R)] for e in self.E}
        self.cnt = {e: 0 for e in self.E}
        self.known = {e: {f: 0 for f in self.E} for e in self.E}
        self.ND = 24
        self.dsem = [es.enter_context(nc.semaphore("d%d" % i)) for i in range(self.ND)]
        self.dval = [0] * self.ND
        self.dseq = 0
        self.dknown = {e: [0] * self.ND for e in self.E}
        self.nins = 0
        self.csems = [es.enter_context(nc.semaphore("cc%d" % i)) for i in range(4)]

    def _wait(self, x, tok):
        if tok[0] == "c":
            _, e, k = tok
            if x == "pe" and e == "pe":
                return
            if self.known[x][e] >= k:
                return
            self.known[x][e] = k
            ep = (k - 1) // self.EP
            val = (ep // self.R) * self.EP + ((k - 1) % self.EP) + 1
            self.E[x].wait_ge(self.sem[e][ep % self.R], val)
        else:
            _, i, v = tok
            if self.dknown[x][i] >= v:
                return
            self.dknown[x][i] = v
            self.E[x].wait_ge(self.dsem[i], v)

    def _deps(self, reads, writes):
        deps = set()
        for b in reads:
            if b.w is not None:
                deps.add(b.w)
        for b in writes:
            if b.w is not None:
                deps.add(b.w)
            deps.update(b.r.values())
        return deps

    def op(self, e, fn, reads=(), writes=(), inc=True):
        if e != "pe":
            rx = [b for b in reads if b.ex]
            if rx:
                writes = list(writes) + rx
        for t in self._deps(reads, writes):
            self._wait(e, t)
        self.nins += 1
        if inc:
            self.cnt[e] += 1
            k = self.cnt[e]
            ep = (k - 1) // self.EP
            fn(self.E[e]).then_inc(self.sem[e][ep % self.R], 1)
        else:
            k = self.cnt[e] + 1
            fn(self.E[e])
        tok = ("c", e, k)
        for b in reads:
            b.r[e] = tok
        for b in writes:
            b.w = tok
            b.r = {}

    def dma(self, q, out, in_, reads=(), writes=()):
        i = self.dseq % self.ND
        self.dseq += 1
        if self.dval[i] > 0:
            self._wait(q, ("d", i, self.dval[i]))
        for t in self._deps(reads, writes):
            self._wait(q, t)
        self.dval[i] += 16
        tok = ("d", i, self.dval[i])
        self.nins += 1
        self.E[q].dma_start(out=out, in_=in_).then_inc(self.dsem[i], 16)
        for b in reads:
            b.r[("d", i)] = tok
        for b in writes:
            b.w = tok
            b.r = {}

    def collective(self, src, dst, reads, writes, dummy, dummyb):
        for t in self._deps(reads, writes):
            self._wait("pool", t)
        csem = self.csems.pop()
        self.nc.gpsimd.collective_compute("AllGather", ALU.bypass, replica_groups=REPLICA_GROUPS,
                                          ins=[src.opt()], outs=[dst.opt()]).then_inc(csem)
        self.nc.gpsimd.wait_ge(csem, 1)
        self.op("pool", lambda e: e.memset(dummy, 0.0), reads=reads, writes=list(writes) + [dummyb])

    def barrier(self, engines=None):
        for x in (engines or self.E):
            for e in self.E:
                if e != x and self.cnt[e] > 0:
                    self._wait(x, ("c", e, self.cnt[e]))
            for i in range(self.ND):
                if self.dval[i] > 0:
                    self._wait(x, ("d", i, self.dval[i]))


REPLICA_GROUPS = [[0, 1], [2, 3], [4, 5], [6, 7]]


def merge_into(dst, srcs):
    for s in srcs:
        if s.w is not None:
            dst.r[("m", id(s))] = s.w
        for k, v in s.r.items():
            dst.r[("m", id(s), k)] = v


CO = {}
_o = 0
for _n, _w in (("ident", 128), ("ones", 128), ("trif", 128), ("trib", 128), ("mbf", 128), ("mbb", 128),
               ("m01f", 128), ("m01b", 128), ("self", 128), ("selb", 128), ("esel", 16 * 128), ("anti", 128)):
    CO[_n] = (_o, _w)
    _o += _w
NCONST = _o


def make_consts():
    c = np.zeros((128, NCONST), np.float32)
    i = np.arange(128)
    J, Tt = np.meshgrid(i, i, indexing="ij")

    def put(n, m):
        o, w = CO[n]
        c[:, o:o + w] = m

    put("ident", (J == Tt).astype(np.float32))
    put("ones", np.ones((128, 128), np.float32))
    put("trif", (J <= Tt).astype(np.float32))
    put("trib", (J >= Tt).astype(np.float32))
    put("mbf", np.where(J <= Tt, 0.0, NEG).astype(np.float32))
    put("mbb", np.where(J >= Tt, 0.0, NEG).astype(np.float32))
    put("m01f", (J <= Tt).astype(np.float32))
    put("m01b", (J >= Tt).astype(np.float32))
    sf = np.zeros((128, 128), np.float32); sf[127, :] = 1.0
    sb = np.zeros((128, 128), np.float32); sb[0, :] = 1.0
    put("self", sf)
    put("selb", sb)
    es = np.zeros((128, 16, 128), np.float32)
    for h in range(16):
        es[h, h, :] = 1.0
    put("esel", es.reshape(128, 16 * 128))
    put("anti", (J + Tt == 127).astype(np.float32))
    return c


PO = {}
_o = 0
for _n, _w in (("bmod", 144), ("lng", 48), ("lnb", 48), ("mergeb", 48), ("scw", 12 * 9), ("scb", 12),
               ("mcw", 8 * 9), ("mcb", 8)):
    PO[_n] = (_o, _w)
    _o += _w
NPAR = _o
RO = {}
_o = 0
for _n, _w in (("dtb", 32), ("alog", 32), ("dsk", 16), ("snw", 1024), ("mnw", 512), ("gnw", 512), ("mgb", 16)):
    RO[_n] = (_o, _w)
    _o += _w
NROW = _o


def fm_vec(v):
    return np.ascontiguousarray(v.reshape(-1, 128).T)


def make_par(inp, l):
    p = np.zeros((128, NPAR), np.float32)

    def put(n, m):
        o, w = PO[n]
        p[:, o:o + w] = m.reshape(128, w)

    put("bmod", fm_vec(inp["b_mod"][l]))
    put("lng", np.concatenate([fm_vec(inp["ln_g"][l, k]) for k in range(3)], axis=1))
    put("lnb", np.concatenate([fm_vec(inp["ln_b"][l, k]) for k in range(3)], axis=1))
    put("mergeb", np.concatenate([fm_vec(inp["merge_b"][l, k]) for k in range(3)], axis=1))
    w = inp["ssd_conv_w"][l].reshape(9, 1536)
    put("scw", np.ascontiguousarray(w.T.reshape(12, 128, 9).transpose(1, 0, 2)))
    put("scb", fm_vec(inp["ssd_conv_b"][l]))
    w = inp["ml_conv_w"][l].reshape(9, 1024)
    put("mcw", np.ascontiguousarray(w.T.reshape(8, 128, 9).transpose(1, 0, 2)))
    put("mcb", fm_vec(inp["ml_conv_b"][l]))
    return p


def make_row(inp, l):
    r = np.zeros((NROW,), np.float32)

    def put(n, v):
        o, w = RO[n]
        r[o:o + w] = v.reshape(w)

    put("dtb", inp["ssd_dt_bias"][l])
    put("alog", inp["ssd_a_log"][l])
    put("dsk", inp["ssd_d"][l])
    put("snw", inp["ssd_norm_w"][l])
    put("mnw", inp["ml_norm_w"][l])
    put("gnw", inp["gla_norm_w"][l])
    put("mgb", inp["ml_gate_b"][l])
    return np.ascontiguousarray(np.broadcast_to(r[None, :], (128, NROW)))


def make_gw2(inp, l):
    g = np.zeros((33, 512), np.float32)
    g[0:16, 0:256] = inp["gla_w2"][l, 0]
    g[16:32, 256:512] = inp["gla_w2"][l, 1]
    g[32, :] = inp["gla_b2"][l].reshape(512)
    return g


def subs_for(c0, c1):
    out = []
    o = 0
    nb = (c1 - c0) * 128
    if c0 == 0:
        out.append((0, 256))
        o = 256
    rem = nb - o
    npieces = (rem + 511) // 512
    base = (rem // 128) // npieces
    extra = (rem // 128) % npieces
    for i in range(npieces):
        n = (base + (1 if i < extra else 0)) * 128
        out.append((o, n))
        o += n
    return out


class K:
    def __init__(self, stop_after=None, dump=None):
        self.stop_after = stop_after
        self.dump = dump
        nc = self.nc = bass.Bass("TRN2", target_bir_lowering=False)
        self.es = ExitStack()
        es = self.es
        P = self.P = Prog(nc, es)
        L = DEPTH
        ein = lambda n, s, dt=F32: DT(nc, n, s, dt, kind="ExternalInput")
        self.xT = ein("xT", [D, T])
        self.cT = ein("cT", [128, 32])
        self.SELF = ein("SELF", [128, 2])
        self.CONST = ein("CONST", [128, NCONST])
        self.PAR = ein("PAR", [L, 128, NPAR])
        self.HALO_OUT = DT(nc, "HALO_OUT", [2560, 64], F32)
        self.HALO_ALL = DT(nc, "HALO_ALL", [5120, 64], F32)
        self.HALO_REV = DT(nc, "HALO_REV", [2560, 64], F32)
        self.XOUT = DT(nc, "XOUT", [128, 2048], F32)
        self.XALL = DT(nc, "XALL", [256, 2048], F32)
        self.ROW = ein("ROW", [L, 128, NROW])
        self.GW2 = ein("GW2", [L, 33, 512])
        self.w_mod = ein("w_mod", [L, D, 9 * D])
        self.ffn_w_in = ein("ffn_w_in", [L, 2, D, 2 * DFF])
        self.ffn_w_out = ein("ffn_w_out", [L, 2, DFF, D])
        self.w_in = ein("w_in", [L, D, INTOT])
        self.w_br_ssd = ein("w_br_ssd", [L, 1024, D])
        self.w_br_ml = ein("w_br_ml", [L, 512, D])
        self.w_br_gla = ein("w_br_gla", [L, 512, D])
        self.w_out = ein("w_out", [L, D, D])
        self.outT = DT(nc, "outT", [D, NLT * 128], F32, kind="ExternalOutput")
        self.HT = DT(nc, "HT", [D, T], F32)
        self.UT = DT(nc, "UT", [D, T], BF16)
        self.ZT = DT(nc, "ZT", [D, T], F32)
        self.dumps = []
        sb = lambda n, s, dt=F32: es.enter_context(nc.sbuf_tensor(uniq(n), s, dt))
        self.ident = sb("ident", [128, 128]); self.identb = Buf()
        self.ones = sb("ones", [128, 128]); self.onesb = Buf()
        self.identh = sb("identh", [128, 128], BF16); self.identhb = Buf()
        self.par = sb("par", [128, NPAR]); self.parb = Buf()
        self.modv = sb("modv", [128, 144, 2]); self.modb = Buf()
        self.selt = sb("selt", [128, 2]); self.seltb = Buf()
        self.dummy = sb("ccdummy", [128, 4]); self.dummyb = Buf()
        self.banks = []
        for i in range(8):
            pt = es.enter_context(nc.psum_tensor("ps%d" % i, [128, 512], F32))
            self.banks.append((pt, Buf(ex=True)))
        self.bi = 0
        o, w = CO["ident"]
        P.dma("sp", self.ident[:, :], self.CONST.ap[:, o:o + w], writes=[self.identb])
        o, w = CO["ones"]
        P.dma("sp", self.ones[:, :], self.CONST.ap[:, o:o + w], writes=[self.onesb])
        P.op("dve", lambda e: e.tensor_copy(self.identh[:, :], self.ident[:, :]), reads=[self.identb], writes=[self.identhb])
        P.dma("sp", self.selt[:, :], self.SELF.ap[:, :], writes=[self.seltb])

    def add_dump(self, name, src, shape, dt=F32):
        dst = DT(self.nc, name, shape, dt, kind="ExternalOutput")
        self.dumps.append((src, dst))

    def bank(self):
        b = self.banks[self.bi % 8]
        self.bi += 1
        return b

    def mvec(self, k9, fc, v):
        return self.modv[:, k9 * 16 + fc, v:v + 1]

    def parv(self, name, idx):
        o, w = PO[name]
        return self.par[:, o + idx:o + idx + 1]

    def phase_mod(self, l):
        P, nc = self.P, self.nc
        P.dma("sp", self.par[:, :], self.PAR.ap[l], writes=[self.parb])
        with nc.sbuf_tensor(uniq("wm0"), [128, 16, 512], F32) as wm0, nc.sbuf_tensor(uniq("wm1"), [128, 16, 512], F32) as wm1, \
                nc.sbuf_tensor(uniq("sc"), [128, 32], F32) as sc:
            wbufs = [(wm0, Buf()), (wm1, Buf())]
            scb = Buf()
            P.dma("sp", sc[:, :], self.cT.ap[:, :], writes=[scb])
            P.op("act", lambda e: e.activation(sc[:, :], sc[:, :], AF.Silu), reads=[scb], writes=[scb])
            wsrc = self.w_mod.ap[l].rearrange("(kc p) n -> p kc n", p=128)
            ob, _ = PO["bmod"]
            for cb in range(36):
                wt, wb = wbufs[cb % 2]
                P.dma("sp", wt[:, :, :], wsrc[:, :, cb * 512:(cb + 1) * 512], writes=[wb])
                ps, pb = self.bank()
                for j in range(4):
                    for kc in range(16):
                        P.op("pe", lambda e: e.matmul(ps[:, j * 2:j * 2 + 2], wt[:, kc, j * 128:(j + 1) * 128],
                                                      sc[:, kc * 2:kc * 2 + 2], start=(kc == 0), stop=(kc == 15)),
                             reads=[wb, scb], writes=[pb], inc=(kc == 15))
                P.op("dve", lambda e: e.tensor_tensor(
                    self.modv[:, cb * 4:(cb + 1) * 4, :], ps[:, 0:8].rearrange("p (j v) -> p j v", v=2),
                    self.par[:, ob + cb * 4:ob + (cb + 1) * 4].unsqueeze(2).to_broadcast([128, 4, 2]), ALU.add),
                     reads=[pb, self.parb], writes=[self.modb])
            for k in range(3):
                a = self.modv[:, (3 * k + 1) * 16:(3 * k + 2) * 16, :]
                P.op("dve", lambda e: e.tensor_scalar_add(a, a, 1.0), reads=[self.modb], writes=[self.modb])
                g = self.modv[:, (3 * k + 2) * 16:(3 * k + 3) * 16, :]
                coef = (0.5 if k != 1 else 1.0) / ALPHA
                P.op("dve", lambda e: e.tensor_scalar_mul(g, g, coef), reads=[self.modb], writes=[self.modb])
            P.barrier()

    def phase_modapply(self, src, k):
        P, nc = self.P, self.nc
        with nc.sbuf_tensor(uniq("ma_x"), [128, 16, 512], F32) as xt, nc.sbuf_tensor(uniq("ma_u"), [128, 16, 512], BF16) as ut:
            xb, ub = Buf(), Buf()
            groups = [(0, 2)] + [(2 + 4 * i, 6 + 4 * i) for i in range(4)]
            for (c0, c1) in groups:
                n = (c1 - c0) * 128
                v = 1 if c0 == 0 else 0
                P.dma("sp", xt[:, :, :n], src.fm()[:, :, c0 * 128:c1 * 128], reads=src.bufs(c0, c1), writes=[xb])
                for fc in range(16):
                    P.op("dve", lambda e: e.tensor_scalar(ut[:, fc, :n], xt[:, fc, :n], self.mvec(3 * k + 1, fc, v),
                                                          self.mvec(3 * k, fc, v), ALU.mult, ALU.add),
                         reads=[xb, self.modb], writes=[ub])
                P.dma("sp", self.UT.fm()[:, :, c0 * 128:c1 * 128], ut[:, :, :n], reads=[ub], writes=self.UT.bufs(c0, c1))
            P.barrier()

    def phase_ffn(self, l, which, k, hsrc, blocks=((0, 9), (9, 18))):
        P, nc = self.P, self.nc
        W1 = self.ffn_w_in.ap[l, which].rearrange("(kc p) n -> p kc n", p=128)
        W2 = self.ffn_w_out.ap[l, which].rearrange("(kc p) n -> p kc n", p=128)
        with ExitStack() as es:
            sb = lambda n, s, dt=F32: es.enter_context(nc.sbuf_tensor(uniq(n), s, dt))
            U = sb("f_u", [128, 16, 1152], BF16); Ub = Buf()
            A = sb("f_a", [128, 44, 1152], BF16)
            WB = [(sb("f_w%d" % i, [128, 5632], BF16), Buf()) for i in range(3)]
            SG = [(sb("f_sg%d" % i, [128, 512], F32), Buf()) for i in range(2)]
            HB = [(sb("f_h%d" % i, [128, 1152], F32), Buf()) for i in range(2)]
            ZB = [(sb("f_z%d" % i, [128, 1152], F32), Buf()) for i in range(2)]
            wi = 0
            si = 0
            for (c0, c1) in blocks:
                nb = (c1 - c0) * 128
                t0 = c0 * 128
                subs = subs_for(c0, c1)
                Ab = [[Buf() for _ in subs] for _ in range(44)]
                P.dma("sp", U[:, :, :nb], self.UT.fm()[:, :, t0:t0 + nb], reads=self.UT.bufs(c0, c1), writes=[Ub])
                for j2 in range(22):
                    wa, wab = WB[wi % 3]; wi += 1
                    wg, wgb = WB[wi % 3]; wi += 1
                    wa3 = wa[:, 0:4096].rearrange("p (kc n) -> p kc n", n=256)
                    wg3 = wg[:, 0:4096].rearrange("p (kc n) -> p kc n", n=256)
                    P.dma("pool", wa3, W1[:, :, j2 * 256:(j2 + 1) * 256], writes=[wab])
                    P.dma("pool", wg3, W1[:, :, DFF + j2 * 256:DFF + (j2 + 1) * 256], writes=[wgb])
                    for jj in range(2):
                        j = j2 * 2 + jj
                        for si_, (o, n) in enumerate(subs):
                            psA, pbA = self.bank()
                            psG, pbG = self.bank()
                            for kc in range(16):
                                P.op("pe", lambda e: e.matmul(psA[:, :n], wa3[:, kc, jj * 128:(jj + 1) * 128],
                                                              U[:, kc, o:o + n], start=(kc == 0), stop=(kc == 15)),
                                     reads=[wab, Ub], writes=[pbA], inc=(kc == 15))
                            for kc in range(16):
                                P.op("pe", lambda e: e.matmul(psG[:, :n], wg3[:, kc, jj * 128:(jj + 1) * 128],
                                                              U[:, kc, o:o + n], start=(kc == 0), stop=(kc == 15)),
                                     reads=[wgb, Ub], writes=[pbG], inc=(kc == 15))
                            sg, sgb = SG[si % 2]; si += 1
                            P.op("act", lambda e: e.activation(sg[:, :n], psG[:, :n], AF.Silu), reads=[pbG], writes=[sgb])
                            P.op("dve", lambda e: e.tensor_tensor(A[:, j, o:o + n], sg[:, :n], psA[:, :n], ALU.mult),
                                 reads=[sgb, pbA], writes=[Ab[j][si_]])
                for fc in range(16):
                    w2, w2b = WB[wi % 3]; wi += 1
                    w23 = w2[:, :].rearrange("p (kc n) -> p kc n", n=128)
                    P.dma("pool", w23, W2[:, :, fc * 128:(fc + 1) * 128], writes=[w2b])
                    hb, hbb = HB[fc % 2]
                    zb, zbb = ZB[fc % 2]
                    P.dma("sp", hb[:, :nb], hsrc.ap[fc * 128:(fc + 1) * 128, t0:t0 + nb], reads=hsrc.bufs(c0, c1), writes=[hbb])
                    for si_, (o, n) in enumerate(subs):
                        ps, pb = self.bank()
                        for kc in range(44):
                            P.op("pe", lambda e: e.matmul(ps[:, :n], w23[:, kc, :], A[:, kc, o:o + n],
                                                          start=(kc == 0), stop=(kc == 43)),
                                 reads=[w2b, Ab[kc][si_]], writes=[pb], inc=(kc == 43))
                        v = 1 if (c0 == 0 and o == 0) else 0
                        P.op("dve", lambda e: e.scalar_tensor_tensor(zb[:, o:o + n], ps[:, :n], self.mvec(3 * k + 2, fc, v),
                                                                     hb[:, o:o + n], ALU.mult, ALU.add),
                             reads=[pb, hbb, self.modb], writes=[zbb])
                    P.dma("sp", self.ZT.ap[fc * 128:(fc + 1) * 128, t0:t0 + nb], zb[:, :nb], reads=[zbb],
                          writes=self.ZT.bufs(c0, c1))
            P.barrier()

    def phase_ln(self, lnk, nextk, final=False, skip_ctx=False):
        P, nc = self.P, self.nc
        eps = EPS / (ALPHA * ALPHA)
        with ExitStack() as es:
            sb = lambda n, s, dt=F32: es.enter_context(nc.sbuf_tensor(uniq(n), s, dt))
            Zs = [(sb("l_z%d" % i, [128, 16, 512]), Buf()) for i in range(2)]
            Hs = [(sb("l_h%d" % i, [128, 16, 512]), Buf()) for i in range(2)]
            Us = [(sb("l_u%d" % i, [128, 16, 512], BF16), Buf()) for i in range(2)]
            SQ = [(sb("l_sq%d" % i, [128, 512]), Buf()) for i in range(2)]
            mean = sb("l_mean", [128, 512]); meanb = Buf()
            var = sb("l_var", [128, 512]); varb = Buf()
            rstd = sb("l_rstd", [128, 512]); rstdb = Buf()
            nmr = sb("l_nmr", [128, 512]); nmrb = Buf()
            tmp = [(sb("l_t%d" % i, [128, 512]), Buf()) for i in range(2)]
            groups = [(0, 2)] + [(2 + 4 * i, 6 + 4 * i) for i in range(4)]
            glist = [g for g in groups if not ((final or skip_ctx) and g[0] == 0)]

            def ln_load(i):
                (a0, a1) = glist[i]
                zt_, zb_ = Zs[i % 2]
                P.dma("sp", zt_[:, :, :(a1 - a0) * 128], self.ZT.fm()[:, :, a0 * 128:a1 * 128], reads=self.ZT.bufs(a0, a1), writes=[zb_])

            ln_load(0)
            for gi_, (c0, c1) in enumerate(glist):
                if gi_ + 1 < len(glist):
                    ln_load(gi_ + 1)
                Z, Zb = Zs[gi_ % 2]; H, Hb = Hs[gi_ % 2]; Uo, Uob = Us[gi_ % 2]
                n = (c1 - c0) * 128
                v = 1 if c0 == 0 else 0
                ps1, pb1 = self.bank()
                ps2, pb2 = self.bank()
                for fc in range(16):
                    sq, sqb = SQ[fc % 2]
                    P.op("act", lambda e: e.activation(sq[:, :n], Z[:, fc, :n], AF.Square), reads=[Zb], writes=[sqb])
                    P.op("pe", lambda e: e.matmul(ps1[:, :n], self.ones[:, :], Z[:, fc, :n], start=(fc == 0), stop=(fc == 15)),
                         reads=[self.onesb, Zb], writes=[pb1], inc=(fc == 15))
                    P.op("pe", lambda e: e.matmul(ps2[:, :n], self.ones[:, :], sq[:, :n], start=(fc == 0), stop=(fc == 15)),
                         reads=[self.onesb, sqb], writes=[pb2], inc=True)
                P.op("act", lambda e: e.mul(mean[:, :n], ps1[:, :n], 1.0 / D), reads=[pb1], writes=[meanb])
                P.op("dve", lambda e: e.tensor_tensor(var[:, :n], mean[:, :n], mean[:, :n], ALU.mult), reads=[meanb], writes=[varb])
                P.op("dve", lambda e: e.scalar_tensor_tensor(var[:, :n], ps2[:, :n], 1.0 / D, var[:, :n], ALU.mult, ALU.subtract),
                     reads=[pb2, varb], writes=[varb])
                P.op("dve", lambda e: e.tensor_scalar_add(var[:, :n], var[:, :n], eps), reads=[varb], writes=[varb])
                P.op("act", lambda e: e.sqrt(var[:, :n], var[:, :n]), reads=[varb], writes=[varb])
                P.op("dve", lambda e: e.reciprocal(rstd[:, :n], var[:, :n]), reads=[varb], writes=[rstdb])
                P.op("dve", lambda e: e.scalar_tensor_tensor(nmr[:, :n], mean[:, :n], -1.0, rstd[:, :n], ALU.mult, ALU.mult),
                     reads=[meanb, rstdb], writes=[nmrb])
                og, _ = PO["lng"]
                obb, _ = PO["lnb"]
                for fc in range(16):
                    tt, ttb = tmp[fc % 2]
                    P.op("dve", lambda e: e.tensor_tensor(tt[:, :n], Z[:, fc, :n], rstd[:, :n], ALU.mult),
                         reads=[Zb, rstdb], writes=[ttb])
                    P.op("pool", lambda e: e.tensor_tensor(tt[:, :n], tt[:, :n], nmr[:, :n], ALU.add),
                         reads=[ttb, nmrb], writes=[ttb])
                    gcol = self.par[:, og + lnk * 16 + fc:og + lnk * 16 + fc + 1]
                    bcol = self.par[:, obb + lnk * 16 + fc:obb + lnk * 16 + fc + 1]
                    P.op("act", lambda e: e.activation(H[:, fc, :n], tt[:, :n], AF.Identity, bias=bcol, scale=gcol),
                         reads=[ttb, self.parb], writes=[Hb])
                    if nextk is not None:
                        P.op("dve", lambda e: e.tensor_scalar(Uo[:, fc, :n], H[:, fc, :n], self.mvec(3 * nextk + 1, fc, v),
                                                              self.mvec(3 * nextk, fc, v), ALU.mult, ALU.add),
                             reads=[Hb, self.modb], writes=[Uob])
                if final:
                    P.dma("sp", self.outT.fm()[:, :, (c0 - 2) * 128:(c1 - 2) * 128], H[:, :, :n], reads=[Hb],
                          writes=self.outT.bufs(c0, c1))
                else:
                    P.dma("act", self.HT.fm()[:, :, c0 * 128:c1 * 128], H[:, :, :n], reads=[Hb], writes=self.HT.bufs(c0, c1))
                if nextk is not None:
                    P.dma("act", self.UT.fm()[:, :, c0 * 128:c1 * 128], Uo[:, :, :n], reads=[Uob], writes=self.UT.bufs(c0, c1))
            P.barrier()

    def alloc_mixer_scratch(self):
        nc = self.nc
        mk = lambda n, s, dt=F32: DT(nc, n, s, dt)
        self.XBC = mk("XBC", [1536, T]); self.QKM = mk("QKM", [1024, T]); self.GQK = mk("GQK", [512, T])
        self.GP = mk("GP", [6144, T]); self.LR = mk("LR", [32, T]); self.TOK = mk("TOK", [T, 3120])
        self.XS_TOK = mk("XS_TOK", [T, 1024]); self.BCT = mk("BCT", [512, T], BF16); self.B_TOK = mk("B_TOK", [T, 256], BF16)
        self.QKC = mk("QKC", [1024, T], BF16); self.KM_TOK = mk("KM_TOK", [T, 512], BF16)
        self.FINT = mk("FINT", [D, T], BF16)
        self.DS = dict(ssd=mk("DS_ssd", [NCH, 2, 64, 1024]), ml=mk("DS_ml", [NCH, 2, 128, 512]),
                       mln=mk("DS_mln", [NCH, 2, 128, 4]), gla=mk("DS_gla", [NCH, 2, 128, 256]))
        self.DEC = dict(ssd=mk("DEC_ssd", [NCH, 2, 64, 16]), ml=mk("DEC_ml", [NCH, 2, 128, 4]),
                        gla=mk("DEC_gla", [NCH, 2, 128, 2]))
        self.ENT = dict(ssd=mk("ENT_ssd", [NCH, 2, 64, 1024], BF16), ml=mk("ENT_ml", [NCH, 2, 128, 512], BF16),
                        mln=mk("ENT_mln", [NCH, 2, 128, 4], BF16), gla=mk("ENT_gla", [NCH, 2, 128, 256], BF16))

    def phase_proj(self, l):
        P, nc = self.P, self.nc
        blocks = [(0, 9), (9, 18)]
        W = self.w_in.ap[l].rearrange("(kc p) n -> p kc n", p=128)
        fm_groups = [(self.XBC, OFF["xbc"], 1536), (self.QKM, OFF["mq"], 1024), (self.GQK, OFF["gq"], 512),
                     (self.GP, OFF["gate"], 6144), (self.LR, OFF["lr"], 32)]
        tm_groups = [(0, OFF["z"], 1024), (1024, OFF["dt"], 32), (1056, OFF["mv"], 1024), (2080, OFF["mg"], 16),
                     (2096, OFF["gv"], 1024)]
        with ExitStack() as es:
            sb = lambda n, s, dt=F32: es.enter_context(nc.sbuf_tensor(uniq(n), s, dt))
            U = sb("p_u", [128, 16, 1152], BF16); Ub = Buf()
            WB = [(sb("p_w%d" % i, [128, 16, 256], BF16), Buf()) for i in range(3)]
            OS = [(sb("p_o%d" % i, [128, 1152], F32), Buf()) for i in range(3)]
            TS = [(sb("p_t%d" % i, [128, 256], F32), Buf()) for i in range(3)]
            wi = 0; oi = 0; ti = 0
            for (c0, c1) in blocks:
                nb = (c1 - c0) * 128
                t0 = c0 * 128
                subs = subs_for(c0, c1)
                P.dma("sp", U[:, :, :nb], self.UT.fm()[:, :, t0:t0 + nb], reads=self.UT.bufs(c0, c1), writes=[Ub])
                for (dst, woff, ncol) in fm_groups:
                    for cb in range(0, ncol, 256):
                        w = min(256, ncol - cb)
                        wt, wb = WB[wi % 3]; wi += 1
                        P.dma("pool", wt[:, :, :w], W[:, :, woff + cb:woff + cb + w], writes=[wb])
                        for m0 in range(0, w, 128):
                            M = min(128, w - m0)
                            ot, ob = OS[oi % 3]; oi += 1
                            for (o, n) in subs:
                                ps, pb = self.bank()
                                for kc in range(16):
                                    P.op("pe", lambda e: e.matmul(ps[:M, :n], wt[:, kc, m0:m0 + M], U[:, kc, o:o + n],
                                                                  start=(kc == 0), stop=(kc == 15)),
                                         reads=[wb, Ub], writes=[pb], inc=(kc == 15))
                                P.op("act", lambda e: e.copy(ot[:M, o:o + n], ps[:M, :n]), reads=[pb], writes=[ob])
                            P.dma("sp", dst.ap[cb + m0:cb + m0 + M, t0:t0 + nb], ot[:M, :nb], reads=[ob],
                                  writes=dst.bufs(c0, c1))
                for (toff, woff, ncol) in tm_groups:
                    for cb in range(0, ncol, 256):
                        w = min(256, ncol - cb)
                        wt, wb = WB[wi % 3]; wi += 1
                        P.dma("pool", wt[:, :, :w], W[:, :, woff + cb:woff + cb + w], writes=[wb])
                        for c in range(c0, c1):
                            co = (c - c0) * 128
                            ps, pb = self.bank()
                            for kc in range(16):
                                P.op("pe", lambda e: e.matmul(ps[:, :w], U[:, kc, co:co + 128], wt[:, kc, :w],
                                                              start=(kc == 0), stop=(kc == 15)),
                                     reads=[wb, Ub], writes=[pb], inc=(kc == 15))
                            tt, tb = TS[ti % 3]; ti += 1
                            P.op("act", lambda e: e.copy(tt[:, :w], ps[:, :w]), reads=[pb], writes=[tb])
                            P.dma("sp", self.TOK.ap[c * 128:(c + 1) * 128, toff + cb:toff + cb + w], tt[:, :w], reads=[tb],
                                  writes=self.TOK.bufs(c, c + 1))
            P.barrier()

    def phase_conv(self, l):
        P, nc = self.P, self.nc
        with ExitStack() as es:
            sb = lambda n, s, dt=F32: es.enter_context(nc.sbuf_tensor(uniq(n), s, dt))
            X = [(sb("c_x%d" % i, [128, 10, 66]), Buf()) for i in range(2)]
            O = [(sb("c_o%d" % i, [128, 512]), Buf()) for i in range(2)]
            S = [(sb("c_s%d" % i, [128, 512]), Buf()) for i in range(2)]
            SH = [(sb("c_sh%d" % i, [128, 512], BF16), Buf()) for i in range(2)]
            TR = [(sb("c_tr%d" % i, [128, 128]), Buf()) for i in range(3)]
            TRH = [(sb("c_trh%d" % i, [128, 128], BF16), Buf()) for i in range(3)]
            it = 0; tri = 0
            for (src, nt, wname, bname, kind) in ((self.XBC, 12, "scw", "scb", "ssd"), (self.QKM, 8, "mcw", "mcb", "ml")):
                ow, _ = PO[wname]
                obn, _ = PO[bname]
                for ct in range(nt):
                    wcol = lambda tap: self.par[:, ow + ct * 9 + tap:ow + ct * 9 + tap + 1]
                    bcol = self.par[:, obn + ct:obn + ct + 1]
                    segs = [("ctx", 0)] + [("lat", rb) for rb in range(4)]
                    hrow0 = ct * 128 + (0 if kind == "ssd" else 1536)
                    for (sk, rb) in segs:
                        xt, xb = X[it % 2]; ot, ob = O[it % 2]; st, stb = S[it % 2]; sh, shb = SH[it % 2]; it += 1
                        P.op("pool", lambda e: e.memset(xt[:, :, :], 0.0), writes=[xb])
                        if sk == "ctx":
                            n = 256; c0 = 0; c1 = 2
                            xf = xt[:, :, :].rearrange("p a b -> p (a b)")
                            P.dma("sp", xf[:, 1:257], src.ap[ct * 128:(ct + 1) * 128, 0:256], reads=src.bufs(0, 2), writes=[xb])
                            taps = [(3, xf[:, 0:256]), (4, xf[:, 1:257]), (5, xf[:, 2:258])]
                            oview = ot[:, :n]
                        else:
                            n = 512; c0 = 2 + rb * 4; c1 = c0 + 4
                            r0 = rb * 8
                            ra = max(r0 - 1, 0); rz = min(r0 + 9, 32)
                            cc0 = 2 + (ra * 64) // 128; cc1 = 2 + (rz * 64 + 127) // 128
                            srcv = src.ap[ct * 128:(ct + 1) * 128, 256 + ra * 64:256 + rz * 64].rearrange("p (r c) -> p r c", c=64)
                            d0 = ra - (r0 - 1)
                            P.dma("sp", xt[:, d0:d0 + (rz - ra), 1:65], srcv, reads=src.bufs(cc0, cc1), writes=[xb])
                            if rb == 3:
                                P.dma("sp", xt[:, 9, 1:65], self.HALO_REV.ap[hrow0:hrow0 + 128, :], reads=self.HALO_REV.bufs(0, 1) + [xb], writes=[xb])
                            taps = [(i * 3 + j, xt[:, i:i + 8, j:j + 64]) for i in range(3) for j in range(3)]
                            oview = ot[:, :].rearrange("p (r c) -> p r c", c=64)
                        for ti_, (tap, xv) in enumerate(taps):
                            if ti_ == 0:
                                P.op("dve", lambda e: e.tensor_scalar(oview, xv, wcol(tap), bcol, ALU.mult, ALU.add),
                                     reads=[xb, self.parb], writes=[ob])
                            else:
                                P.op("dve", lambda e: e.scalar_tensor_tensor(oview, xv, wcol(tap), oview, ALU.mult, ALU.add),
                                     reads=[xb, ob, self.parb], writes=[ob])
                        P.op("act", lambda e: e.activation(st[:, :n], ot[:, :n], AF.Silu), reads=[ob], writes=[stb])
                        t0 = c0 * 128
                        fm_dst = None
                        if kind == "ssd" and ct >= 8:
                            fm_dst = (self.BCT, (ct - 8) * 128, 1.0)
                        if kind == "ml":
                            fm_dst = (self.QKC, ct * 128, (128.0 ** -0.5) if ct < 4 else 1.0)
                        if fm_dst is not None:
                            dstt, roff, scl = fm_dst
                            P.op("pool", lambda e: e.tensor_scalar_mul(sh[:, :n], st[:, :n], scl), reads=[stb], writes=[shb])
                            P.dma("act", dstt.ap[roff:roff + 128, t0:t0 + n], sh[:, :n], reads=[shb], writes=dstt.bufs(c0, c1))
                        tm_dst = None
                        if kind == "ssd" and ct < 8:
                            tm_dst = (self.XS_TOK, ct * 128, False)
                        if kind == "ssd" and ct in (8, 9):
                            tm_dst = (self.B_TOK, (ct - 8) * 128, True)
                        if kind == "ml" and ct >= 4:
                            tm_dst = (self.KM_TOK, (ct - 4) * 128, True)
                        if tm_dst is not None:
                            dstt, coff, half = tm_dst
                            for cc in range(n // 128):
                                ps, pb = self.bank()
                                P.op("pe", lambda e: e.transpose(ps[:, 0:128], st[:, cc * 128:(cc + 1) * 128], self.ident[:, :]),
                                     reads=[stb, self.identb], writes=[pb])
                                if half:
                                    tr, trb = TRH[tri % 3]
                                else:
                                    tr, trb = TR[tri % 3]
                                tri += 1
                                P.op("act", lambda e: e.copy(tr[:, :], ps[:, 0:128]), reads=[pb], writes=[trb])
                                c = c0 + cc
                                P.dma("act", dstt.ap[c * 128:(c + 1) * 128, coff:coff + 128], tr[:, :], reads=[trb],
                                      writes=dstt.bufs(c, c + 1))
            P.barrier()

    def scan_pass(self, l, mode, chunks):
        P, nc = self.P, self.nc
        AX = mybir.AxisListType.X
        with ExitStack() as es:
            sb = lambda n, s, dt=F32: es.enter_context(nc.sbuf_tensor(uniq(n), s, dt))
            cb_ = Buf()
            cst = {}
            for n_ in ("trif", "trib", "mbf", "mbb", "m01f", "m01b", "self", "selb"):
                cst[n_] = sb("k_" + n_, [128, 128])
                o, w = CO[n_]
                P.dma("sp", cst[n_][:, :], self.CONST.ap[:, o:o + w], writes=[cb_])
            esel = sb("k_esel", [16, 16, 128])
            o, w = CO["esel"]
            P.dma("sp", esel[:, :, :], self.CONST.ap[0:16, o:o + w].rearrange("p (h m) -> p h m", m=128), writes=[cb_])
            row = sb("k_row", [128, NROW])
            P.dma("sp", row[:, :], self.ROW.ap[l], writes=[cb_])
            gw2 = sb("k_gw2", [33, 512])
            P.dma("sp", gw2[:, :], self.GW2.ap[l], writes=[cb_])
            aneg = sb("k_aneg", [128, 32])
            oa, _ = RO["alog"]
            P.op("act", lambda e: e.activation(aneg[:, :], row[:, oa:oa + 32], AF.Exp), reads=[cb_], writes=[cb_])
            P.op("dve", lambda e: e.tensor_scalar_mul(aneg[:, :], aneg[:, :], -1.0), reads=[cb_], writes=[cb_])
            onesh = sb("k_onesh", [128, 1], BF16)
            P.op("dve", lambda e: e.memset(onesh[:, :], 1.0), writes=[cb_])
            one1 = self.ones[:, 0:1]
            TRI = [cst["trif"], cst["trib"]]; MB = [cst["mbf"], cst["mbb"]]; M01 = [cst["m01f"], cst["m01b"]]
            MBH4 = [sb("k_mbh%d" % i, [128, 512], BF16) for i in range(2)]
            for i_ in range(2):
                P.op("dve", lambda e: e.tensor_copy(MBH4[i_][:, :].rearrange("p (h m) -> p h m", m=128),
                                                    MB[i_][:, :].unsqueeze(1).to_broadcast([128, 4, 128])), reads=[cb_], writes=[cb_])
            ones16 = sb("k_ones16", [16, 128])
            P.op("dve", lambda e: e.memset(ones16[:, :], 1.0), writes=[cb_])
            RM = [(sb("k_rm%d" % i, [16, 4, 128]), Buf()) for i in range(2)]
            SEL = [cst["self"], cst["selb"]]
            ENDC = [127, 0]
            rowv = lambda n_: row[:, RO[n_][0]:RO[n_][0] + RO[n_][1]]
            INS = []
            for i_ in range(2):
                d_ = dict(TK=sb("k_tk%d" % i_, [128, 3120]), XS=sb("k_xs%d" % i_, [128, 1024]), BTK=sb("k_btk%d" % i_, [128, 256], BF16),
                          KMT=sb("k_kmt%d" % i_, [128, 512], BF16), BCT=sb("k_bct%d" % i_, [128, 4, 128], BF16),
                          QKC=sb("k_qkc%d" % i_, [128, 8, 128], BF16), GQ=sb("k_gq%d" % i_, [128, 4, 128]), LRA=sb("k_lra%d" % i_, [33, 128]))
                for n_ in list(d_.keys()):
                    d_[n_ + "b"] = Buf()
                P.op("dve", lambda e: e.memset(d_["LRA"][:, :], 1.0), writes=[d_["LRAb"]])
                INS.append(d_)

            def load_inputs(c_, d_):
                a0, a1 = c_ * 128, (c_ + 1) * 128
                P.dma("sp", d_["TK"][:, :], self.TOK.ap[a0:a1, :], reads=self.TOK.bufs(c_, c_ + 1), writes=[d_["TKb"]])
                P.dma("sp", d_["XS"][:, :], self.XS_TOK.ap[a0:a1, :], reads=self.XS_TOK.bufs(c_, c_ + 1), writes=[d_["XSb"]])
                P.dma("sp", d_["BTK"][:, :], self.B_TOK.ap[a0:a1, :], reads=self.B_TOK.bufs(c_, c_ + 1), writes=[d_["BTKb"]])
                P.dma("sp", d_["KMT"][:, :], self.KM_TOK.ap[a0:a1, :], reads=self.KM_TOK.bufs(c_, c_ + 1), writes=[d_["KMTb"]])
                P.dma("sp", d_["BCT"][:, :, :], self.BCT.fm()[:, :, a0:a1], reads=self.BCT.bufs(c_, c_ + 1), writes=[d_["BCTb"]])
                P.dma("sp", d_["QKC"][:, :, :], self.QKC.fm()[:, :, a0:a1], reads=self.QKC.bufs(c_, c_ + 1), writes=[d_["QKCb"]])
                P.dma("sp", d_["GQ"][:, :, :], self.GQK.fm()[:, :, a0:a1], reads=self.GQK.bufs(c_, c_ + 1), writes=[d_["GQb"]])
                P.dma("sp", d_["LRA"][0:32, :], self.LR.ap[:, a0:a1], reads=self.LR.bufs(c_, c_ + 1), writes=[d_["LRAb"]])
            dt = sb("k_dt", [128, 32]); dtb_ = Buf()
            lgs = sb("k_lgs", [128, 32]); lgsb = Buf()
            lgm = sb("k_lgm", [128, 8]); lgmb = Buf()
            igm = sb("k_igm", [128, 8]); igmb = Buf()
            sd = {}
            for nm, H in (("s", 16), ("m", 4)):
                sd[nm] = dict(H=H, nlg=sb("k_nlg" + nm, [128, 2 * H]), btok=sb("k_btok" + nm, [128, 2 * H]),
                              bT=sb("k_bT" + nm, [16, 2, 128]), nbT=sb("k_nbT" + nm, [16, 2, 128]),
                              cf=sb("k_cf" + nm, [128, 2 * H]), eb=sb("k_eb" + nm, [128, 2 * H]), nbk=sb("k_nbk" + nm, [128, 2 * H]),
                              ebend=sb("k_ebend" + nm, [128, 2 * H]), tmp=sb("k_tmp" + nm, [128, 2 * H]), b=Buf())
            big1 = sb("k_big1", [128, 1024]); big1b = Buf()
            VP = sb("k_vp", [128, 1024], BF16); VPb = Buf()
            KQ = sb("k_kq", [128, 4, 128]); KQb = Buf()
            DECt = [(sb("k_dec%d" % i, [128, 512]), Buf()) for i in range(2)]
            PT = [(sb("k_pt%d" % i, [128, 4, 128], BF16), Buf()) for i in range(2)]
            ST = sb("k_st", [128, 512], BF16); STb = Buf()
            STN = sb("k_stn", [128, 4], BF16); STNb = Buf()
            YI = sb("k_yi", [128, 2, 1024]); YIb = Buf()
            FIN = sb("k_fin", [128, 2048]); FINb = Buf()
            FT = sb("k_ft", [128, 16, 128], BF16); FTb = Buf()
            SM = sb("k_sm", [128, 64]); SMb = Buf()
            DSt = sb("k_dst", [128, 1024]); DStb = Buf()
            MVH = sb("k_mvh", [128, 512], BF16); MVHb = Buf()
            GVH = sb("k_gvh", [128, 512], BF16); GVHb = Buf()
            LA = sb("k_la", [128, 512]); LAb = Buf()
            EBg = sb("k_ebg", [128, 2, 2, 128]); EBgb = Buf()
            ENg = sb("k_eng", [128, 2, 2, 128]); ENgb = Buf()
            QTg = sb("k_qtg", [128, 2, 2, 128], BF16); QTgb = Buf()
            KTg = sb("k_ktg", [128, 2, 2, 128], BF16); KTgb = Buf()
            KTf = sb("k_ktf", [128, 2, 128]); KTfb = Buf()
            KTK = sb("k_ktk", [128, 2, 128], BF16); KTKb = Buf()
            AT = sb("k_at", [128, 4, 128], BF16); ATb = Buf()
            CFH = sb("k_cfh", [128, 8], BF16); CFHb = Buf()
            B = self.banks
            qi = [0]

            def mm(bank, out, lhsT, rhs, start, stop, reads, inc=None):
                P.op("pe", lambda e: e.matmul(out, lhsT, rhs, start=start, stop=stop), reads=reads, writes=[bank[1]],
                     inc=(stop if inc is None else inc))

            def sd_prep(nm, lg, lgb, ig, igb):
                t = sd[nm]; H = t["H"]; tb = t["b"]
                for d in range(2):
                    sl = slice(d * H, (d + 1) * H)
                    mm(B[6], B[6][0][:, d * H:(d + 1) * H], TRI[d][:, :], lg[:, sl], True, True, [cb_, lgb])
                    mm(B[7], B[7][0][:H, d * 128:(d + 1) * 128], lg[:, sl], TRI[d][:, :], True, True, [cb_, lgb])
                P.op("act", lambda e: e.copy(t["btok"][:, :], B[6][0][:, 0:2 * H]), reads=[B[6][1]], writes=[tb])
                P.op("act", lambda e: e.copy(t["bT"][:H, :, :], B[7][0][:H, 0:256].rearrange("p (d m) -> p d m", m=128)),
                     reads=[B[7][1]], writes=[tb])
                P.op("dve", lambda e: e.tensor_scalar_mul(t["nbk"][:, :], t["btok"][:, :], -1.0), reads=[tb], writes=[tb])
                if ig is not None:
                    P.op("dve", lambda e: e.tensor_tensor(t["nbk"][:, :], t["nbk"][:, :], ig, ALU.add), reads=[tb, igb], writes=[tb])
                for d in range(2):
                    mm(B[6], B[6][0][:, 32 + d * H:32 + (d + 1) * H], SEL[d][:, :], t["btok"][:, d * H:(d + 1) * H], True, True, [cb_, tb])
                P.op("dve", lambda e: e.tensor_tensor(t["tmp"][:, :], B[6][0][:, 32:32 + 2 * H], t["nbk"][:, :], ALU.add),
                     reads=[B[6][1], tb], writes=[tb])
                P.op("act", lambda e: e.activation(t["cf"][:, :], t["tmp"][:, :], AF.Exp), reads=[tb], writes=[tb])
                P.op("act", lambda e: e.activation(t["eb"][:, :], t["btok"][:, :], AF.Exp), reads=[tb], writes=[tb])
                P.op("act", lambda e: e.activation(t["ebend"][:, :], B[6][0][:, 32:32 + 2 * H], AF.Exp), reads=[B[6][1]], writes=[tb])

            def decay_quad(nm, d, heads):
                t = sd[nm]; H = t["H"]
                bk = B[1 + (qi[0] % 2)]
                dec, decb = DECt[qi[0] % 2]
                h0 = heads[0]
                rm, rmb = RM[qi[0] % 2]
                P.op("dve", lambda e: e.tensor_tensor(rm[:H, :, :], esel[:H, h0:h0 + 4, :],
                                                      t["bT"][:H, d, :].unsqueeze(1).to_broadcast([H, 4, 128]), ALU.mult),
                     reads=[cb_, t["b"]], writes=[rmb])
                mm(bk, bk[0][:, :], self.identh[:, :], MBH4[d][:, :], True, False, [cb_, self.identhb])
                mm(bk, bk[0][:, :], ones16[:H, :], rm[:H, :, :].rearrange("p h m -> p (h m)"), False, True, [cb_, rmb])
                qi[0] += 1
                for hh, h in enumerate(heads):
                    P.op("act", lambda e: e.activation(dec[:, hh * 128:(hh + 1) * 128], bk[0][:, hh * 128:(hh + 1) * 128], AF.Exp,
                                                       bias=t["nbk"][:, d * H + h:d * H + h + 1]), reads=[bk[1], t["b"]], writes=[decb])
                return dec, decb

            def gnorm(x, xb, G, Wd, wrow, center):
                xv = x.rearrange("p (g w) -> p g w", w=Wd)
                if center:
                    P.op("dve", lambda e: e.tensor_reduce(SM[:, 0:G], xv, AX, ALU.add), reads=[xb], writes=[SMb])
                    P.op("dve", lambda e: e.tensor_scalar_mul(SM[:, 0:G], SM[:, 0:G], 1.0 / Wd), reads=[SMb], writes=[SMb])
                    P.op("dve", lambda e: e.tensor_tensor(xv, xv, SM[:, 0:G].unsqueeze(2).to_broadcast([128, G, Wd]), ALU.subtract),
                         reads=[xb, SMb], writes=[xb])
                sq = big1[:, 0:G * Wd]
                P.op("dve", lambda e: e.tensor_tensor(sq, x, x, ALU.mult), reads=[xb], writes=[big1b])
                P.op("dve", lambda e: e.tensor_reduce(SM[:, 8:8 + G], sq.rearrange("p (g w) -> p g w", w=Wd), AX, ALU.add),
                     reads=[big1b], writes=[SMb])
                P.op("dve", lambda e: e.tensor_scalar(SM[:, 8:8 + G], SM[:, 8:8 + G], 1.0 / Wd, EPS, ALU.mult, ALU.add),
                     reads=[SMb], writes=[SMb])
                P.op("act", lambda e: e.sqrt(SM[:, 8:8 + G], SM[:, 8:8 + G]), reads=[SMb], writes=[SMb])
                P.op("dve", lambda e: e.reciprocal(SM[:, 8:8 + G], SM[:, 8:8 + G]), reads=[SMb], writes=[SMb])
                P.op("dve", lambda e: e.tensor_tensor(xv, xv, SM[:, 8:8 + G].unsqueeze(2).to_broadcast([128, G, Wd]), ALU.mult),
                     reads=[xb, SMb], writes=[xb])
                P.op("dve", lambda e: e.tensor_tensor(x, x, wrow, ALU.mult), reads=[xb, cb_], writes=[xb])

            if chunks:
                load_inputs(chunks[0], INS[0])
            for ci_, c in enumerate(chunks):
                r0, r1 = c * 128, (c + 1) * 128
                if ci_ + 1 < len(chunks):
                    load_inputs(chunks[ci_ + 1], INS[(ci_ + 1) % 2])
                d_ = INS[ci_ % 2]
                TK, TKb, XS, XSb, BTK, BTKb, KMT, KMTb = d_["TK"], d_["TKb"], d_["XS"], d_["XSb"], d_["BTK"], d_["BTKb"], d_["KMT"], d_["KMTb"]
                BCT, BCTb, QKC, QKCb, GQ, GQb, lra, lrb = d_["BCT"], d_["BCTb"], d_["QKC"], d_["QKCb"], d_["GQ"], d_["GQb"], d_["LRA"], d_["LRAb"]
                zraw = TK[:, 0:1024]; dtraw = TK[:, 1024:1056]; mv = TK[:, 1056:1568]; mo = TK[:, 1568:2080]
                mgr = TK[:, 2080:2096]; gv = TK[:, 2096:2608]; gr = TK[:, 2608:3120]
                P.op("dve", lambda e: e.tensor_tensor(dt[:, :], dtraw, rowv("dtb"), ALU.add), reads=[TKb, cb_], writes=[dtb_])
                P.op("act", lambda e: e.activation(dt[:, :], dt[:, :], AF.Exp), reads=[dtb_], writes=[dtb_])
                P.op("act", lambda e: e.activation(dt[:, :], dt[:, :], AF.Ln, bias=one1), reads=[dtb_, self.onesb], writes=[dtb_])
                P.op("dve", lambda e: e.tensor_tensor(lgs[:, :], dt[:, :], aneg[:, :], ALU.mult), reads=[dtb_, cb_], writes=[lgsb])
                sd_prep("s", lgs[:, :], lgsb, None, None)
                ts = sd["s"]
                xs3 = XS[:, :].rearrange("p (h w) -> p h w", w=64)
                vp3 = VP[:, :].rearrange("p (h w) -> p h w", w=64)
                if mode == "S":
                    for d in range(2):
                        P.op("dve", lambda e: e.tensor_tensor(ts["tmp"][:, d * 16:(d + 1) * 16], ts["cf"][:, d * 16:(d + 1) * 16],
                                                              dt[:, d * 16:(d + 1) * 16], ALU.mult), reads=[ts["b"], dtb_], writes=[ts["b"]])
                        P.op("dve", lambda e: e.tensor_tensor(vp3, xs3, ts["tmp"][:, d * 16:(d + 1) * 16].unsqueeze(2).to_broadcast([128, 16, 64]),
                                                              ALU.mult), reads=[XSb, ts["b"]], writes=[VPb])
                        for g in range(4):
                            bk = B[3 + g // 2]
                            mm(bk, bk[0][0:64, (g % 2) * 256:(g % 2) * 256 + 256], BTK[:, g * 64:(g + 1) * 64],
                               VP[:, g * 256:(g + 1) * 256], True, True, [BTKb, VPb])
                        P.op("act", lambda e: e.copy(DSt[0:64, 0:512], B[3][0][0:64, :]), reads=[B[3][1]], writes=[DStb])
                        P.op("act", lambda e: e.copy(DSt[0:64, 512:1024], B[4][0][0:64, :]), reads=[B[4][1]], writes=[DStb])
                        P.dma("act", self.DS["ssd"].ap[c, d], DSt[0:64, :], reads=[DStb], writes=self.DS["ssd"].bufs(c, c + 1))
                        P.dma("act", self.DEC["ssd"].ap[c, d], ts["ebend"][0:64, d * 16:(d + 1) * 16], reads=[ts["b"]],
                              writes=self.DEC["ssd"].bufs(c, c + 1))
                elif "s" in getattr(self, "obr", "smg"):
                    for g in range(4):
                        pb_ = (g % 2) * 64
                        bk_ = B[0] if g % 2 == 0 else B[5]
                        mm(bk_, bk_[0][:, (g // 2) * 128:(g // 2 + 1) * 128], BCT[pb_:pb_ + 64, g // 2, :], BCT[pb_:pb_ + 64, 2 + g // 2, :],
                           True, True, [BCTb])
                    for e_ in range(2):
                        bk_ = B[0] if e_ == 0 else B[5]
                        for t_ in range(2):
                            P.op("act", lambda e: e.copy(KQ[:, t_ * 2 + e_, :], bk_[0][:, t_ * 128:(t_ + 1) * 128]), reads=[bk_[1]], writes=[KQb])
                    LV = getattr(self, "olvl", 9)
                    for d in range(2 if LV >= 2 else 0):
                        P.op("dve", lambda e: e.tensor_tensor(vp3, xs3, dt[:, d * 16:(d + 1) * 16].unsqueeze(2).to_broadcast([128, 16, 64]),
                                                              ALU.mult), reads=[XSb, dtb_], writes=[VPb])
                        for g in range(4):
                            dec, decb = decay_quad("s", d, [g * 4 + i for i in range(4)])
                            pt, ptb = PT[g % 2]
                            P.op("dve", lambda e: e.tensor_tensor(pt[:, :, :], dec[:, :].rearrange("p (h m) -> p h m", m=128),
                                                                  KQ[:, g, :].unsqueeze(1).to_broadcast([128, 4, 128]), ALU.mult),
                                 reads=[decb, KQb], writes=[ptb])
                            for hh in range(4):
                                h = g * 4 + hh
                                bk = B[3 + h // 8]
                                mm(bk, bk[0][:, (h % 8) * 64:(h % 8) * 64 + 64], pt[:, hh, :], VP[:, h * 64:(h + 1) * 64],
                                   True, True, [ptb, VPb], inc=True)
                        if LV < 3:
                            continue
                        for e_ in range(2):
                            srcv = self.ENT["ssd"].ap[c, d].rearrange("n (t e w) -> n t e w", e=2, w=256)[:, :, e_, :]
                            P.dma("sp", ST[e_ * 64:(e_ + 1) * 64, :].rearrange("n (t w) -> n t w", w=256), srcv,
                                  reads=self.ENT["ssd"].bufs(c, c + 1), writes=[STb])
                        for g in range(4):
                            pb_ = (g % 2) * 64
                            bk_ = B[5] if g % 2 == 0 else B[0]
                            mm(bk_, bk_[0][:, (g // 2) * 256:(g // 2 + 1) * 256], BCT[pb_:pb_ + 64, 2 + g // 2, :],
                               ST[pb_:pb_ + 64, (g // 2) * 256:(g // 2) * 256 + 256], True, True, [BCTb, STb])
                        for g in range(4):
                            bk_ = B[5] if g % 2 == 0 else B[0]
                            P.op("dve", lambda e: e.tensor_tensor(
                                YI[:, d, g * 256:(g + 1) * 256].rearrange("p (h w) -> p h w", w=64),
                                bk_[0][:, (g // 2) * 256:(g // 2 + 1) * 256].rearrange("p (h w) -> p h w", w=64),
                                ts["eb"][:, d * 16 + g * 4:d * 16 + g * 4 + 4].unsqueeze(2).to_broadcast([128, 4, 64]), ALU.mult),
                                 reads=[bk_[1], ts["b"]], writes=[YIb])
                        for half in range(2):
                            P.op("dve", lambda e: e.tensor_tensor(YI[:, d, half * 512:(half + 1) * 512], YI[:, d, half * 512:(half + 1) * 512],
                                                                  B[3 + half][0][:, :], ALU.add), reads=[YIb, B[3 + half][1]], writes=[YIb])
                    fs = FIN[:, 0:1024]
                    if LV < 4:
                        continue
                    P.op("dve", lambda e: e.tensor_tensor(fs, YI[:, 0, :], YI[:, 1, :], ALU.add), reads=[YIb], writes=[FINb])
                    od, _ = RO["dsk"]
                    P.op("dve", lambda e: e.tensor_tensor(big1[:, :].rearrange("p (h w) -> p h w", w=64), xs3,
                                                          row[:, od:od + 16].unsqueeze(2).to_broadcast([128, 16, 64]), ALU.mult),
                         reads=[XSb, cb_], writes=[big1b])
                    P.op("dve", lambda e: e.tensor_tensor(fs, fs, big1[:, :], ALU.add), reads=[FINb, big1b], writes=[FINb])
                    P.op("act", lambda e: e.activation(big1[:, :], zraw, AF.Silu), reads=[TKb], writes=[big1b])
                    P.op("dve", lambda e: e.tensor_tensor(fs, fs, big1[:, :], ALU.mult), reads=[FINb, big1b], writes=[FINb])
                    gnorm(fs, FINb, 4, 256, rowv("snw"), False)
                og, _ = RO["mgb"]
                mg3 = big1[:, 0:16]
                P.op("dve", lambda e: e.tensor_tensor(mg3, mgr, row[:, og:og + 16], ALU.add), reads=[TKb, cb_], writes=[big1b])
                mgv = mg3.rearrange("p (d t h) -> p d t h", d=2, t=2)
                P.op("dve", lambda e: e.tensor_copy(igm[:, :].rearrange("p (d h) -> p d h", d=2), mgv[:, :, 0, :]), reads=[big1b], writes=[igmb])
                P.op("act", lambda e: e.activation(lgm[:, :].rearrange("p (d h) -> p d h", d=2), mgv[:, :, 1, :], AF.Exp, scale=-1.0),
                     reads=[big1b], writes=[lgmb])
                P.op("act", lambda e: e.activation(lgm[:, :], lgm[:, :], AF.Ln, bias=one1), reads=[lgmb, self.onesb], writes=[lgmb])
                P.op("dve", lambda e: e.tensor_scalar_mul(lgm[:, :], lgm[:, :], -1.0), reads=[lgmb], writes=[lgmb])
                sd_prep("m", lgm[:, :], lgmb, igm[:, :], igmb)
                tm = sd["m"]
                mv3 = mv.rearrange("p (h w) -> p h w", w=128)
                vpm = VP[:, 0:512].rearrange("p (h w) -> p h w", w=128)
                if mode == "S":
                    P.op("dve", lambda e: e.tensor_copy(CFH[:, :], tm["cf"][:, :]), reads=[tm["b"]], writes=[CFHb])
                    for d in range(2):
                        P.op("dve", lambda e: e.tensor_tensor(vpm, mv3, tm["cf"][:, d * 4:(d + 1) * 4].unsqueeze(2).to_broadcast([128, 4, 128]),
                                                              ALU.mult), reads=[TKb, tm["b"]], writes=[VPb])
                        for h in range(4):
                            mm(B[3], B[3][0][:, h * 128:(h + 1) * 128], KMT[:, h * 128:(h + 1) * 128], VP[:, h * 128:(h + 1) * 128],
                               True, True, [KMTb, VPb])
                            mm(B[6], B[6][0][:, 96 + h:97 + h], KMT[:, h * 128:(h + 1) * 128], CFH[:, d * 4 + h:d * 4 + h + 1],
                               True, True, [KMTb, CFHb])
                        P.op("act", lambda e: e.copy(DSt[:, 0:512], B[3][0][:, :]), reads=[B[3][1]], writes=[DStb])
                        P.op("act", lambda e: e.copy(DSt[:, 512:516], B[6][0][:, 96:100]), reads=[B[6][1]], writes=[DStb])
                        P.dma("act", self.DS["ml"].ap[c, d], DSt[:, 0:512], reads=[DStb], writes=self.DS["ml"].bufs(c, c + 1))
                        P.dma("act", self.DS["mln"].ap[c, d], DSt[:, 512:516], reads=[DStb], writes=self.DS["mln"].bufs(c, c + 1))
                        P.dma("act", self.DEC["ml"].ap[c, d], tm["ebend"][:, d * 4:(d + 1) * 4], reads=[tm["b"]],
                              writes=self.DEC["ml"].bufs(c, c + 1))
                elif "m" in getattr(self, "obr", "smg"):
                    P.op("dve", lambda e: e.tensor_copy(MVH[:, :], mv), reads=[TKb], writes=[MVHb])
                    for h in range(4):
                        mm(B[0], B[0][0][:, h * 128:(h + 1) * 128], QKC[:, 4 + h, :], QKC[:, h, :], True, True, [QKCb])
                    P.op("act", lambda e: e.copy(KQ[:, :, :], B[0][0][:, :].rearrange("p (g m) -> p g m", m=128)),
                         reads=[B[0][1]], writes=[KQb])
                    fm_ = FIN[:, 1024:1536]
                    for d in range(2):
                        dec, decb = decay_quad("m", d, [0, 1, 2, 3])
                        pt, ptb = PT[d % 2]
                        P.op("dve", lambda e: e.tensor_tensor(pt[:, :, :], dec[:, :].rearrange("p (h m) -> p h m", m=128), KQ[:, :, :], ALU.mult),
                             reads=[decb, KQb], writes=[ptb])
                        P.dma("sp", ST[:, :], self.ENT["ml"].ap[c, d], reads=self.ENT["ml"].bufs(c, c + 1), writes=[STb])
                        P.dma("sp", STN[:, :], self.ENT["mln"].ap[c, d], reads=self.ENT["mln"].bufs(c, c + 1), writes=[STNb])
                        for h in range(4):
                            mm(B[3 + d], B[3 + d][0][:, h * 128:(h + 1) * 128], pt[:, h, :], MVH[:, h * 128:(h + 1) * 128], True, True, [ptb, MVHb])
                            mm(B[6], B[6][0][:, 64 + d * 4 + h:65 + d * 4 + h], pt[:, h, :], onesh[:, :], True, True, [ptb, cb_])
                            mm(B[5], B[5][0][:, h * 128:(h + 1) * 128], QKC[:, h, :], ST[:, h * 128:(h + 1) * 128], True, True, [QKCb, STb])
                            mm(B[6], B[6][0][:, 80 + d * 4 + h:81 + d * 4 + h], QKC[:, h, :], STN[:, h:h + 1], True, True, [QKCb, STNb])
                        ebd = tm["eb"][:, d * 4:(d + 1) * 4]
                        num = YI[:, d, 0:512]
                        P.op("dve", lambda e: e.tensor_tensor(num.rearrange("p (h w) -> p h w", w=128),
                                                              B[5][0][:, :].rearrange("p (h w) -> p h w", w=128),
                                                              ebd.unsqueeze(2).to_broadcast([128, 4, 128]), ALU.mult),
                             reads=[B[5][1], tm["b"]], writes=[YIb])
                        P.op("dve", lambda e: e.tensor_tensor(num, num, B[3 + d][0][:, :], ALU.add), reads=[YIb, B[3 + d][1]], writes=[YIb])
                        den = SM[:, 16 + d * 4:20 + d * 4]
                        P.op("dve", lambda e: e.tensor_tensor(den, B[6][0][:, 80 + d * 4:84 + d * 4], ebd, ALU.mult),
                             reads=[B[6][1], tm["b"]], writes=[SMb])
                        P.op("dve", lambda e: e.tensor_tensor(den, den, B[6][0][:, 64 + d * 4:68 + d * 4], ALU.add),
                             reads=[SMb, B[6][1]], writes=[SMb])
                        P.op("act", lambda e: e.activation(den, den, AF.Abs), reads=[SMb], writes=[SMb])
                        P.op("dve", lambda e: e.tensor_scalar_max(den, den, 1.0), reads=[SMb], writes=[SMb])
                        P.op("dve", lambda e: e.reciprocal(den, den), reads=[SMb], writes=[SMb])
                        P.op("dve", lambda e: e.tensor_tensor(num.rearrange("p (h w) -> p h w", w=128), num.rearrange("p (h w) -> p h w", w=128),
                                                              den.unsqueeze(2).to_broadcast([128, 4, 128]), ALU.mult),
                             reads=[YIb, SMb], writes=[YIb])
                    P.op("dve", lambda e: e.tensor_tensor(fm_, YI[:, 0, 0:512], YI[:, 1, 0:512], ALU.add), reads=[YIb], writes=[FINb])
                    P.op("act", lambda e: e.activation(big1[:, 0:512], mo, AF.Sigmoid), reads=[TKb], writes=[big1b])
                    P.op("dve", lambda e: e.tensor_tensor(fm_, fm_, big1[:, 0:512], ALU.mult), reads=[FINb, big1b], writes=[FINb])
                    gnorm(fm_, FINb, 4, 128, rowv("mnw"), True)
                mm(B[0], B[0][0][:, :], lra[:, :], gw2[:, :], True, True, [lrb, cb_])
                P.op("act", lambda e: e.activation(LA[:, :], B[0][0][:, :], AF.Exp, scale=-1.0), reads=[B[0][1]], writes=[LAb])
                P.op("act", lambda e: e.activation(LA[:, :], LA[:, :], AF.Ln, bias=one1), reads=[LAb, self.onesb], writes=[LAb])
                P.op("dve", lambda e: e.tensor_scalar_mul(LA[:, :], LA[:, :], -1.0 / 16.0), reads=[LAb], writes=[LAb])
                for d in range(2):
                    for tt in range(2):
                        mm(B[7], B[7][0][:, (d * 2 + tt) * 128:(d * 2 + tt + 1) * 128], LA[:, d * 256 + tt * 128:d * 256 + (tt + 1) * 128],
                           TRI[d][:, :], True, True, [LAb, cb_])
                b7v = B[7][0][:, :].rearrange("p (d t m) -> p d t m", d=2, t=2)
                P.op("act", lambda e: e.activation(EBg[:, :, :, :], b7v, AF.Exp), reads=[B[7][1]], writes=[EBgb])
                P.op("act", lambda e: e.activation(ENg[:, :, :, :], b7v, AF.Exp, scale=-1.0), reads=[B[7][1]], writes=[ENgb])
                P.op("dve", lambda e: e.tensor_copy(GVH[:, :], gv), reads=[TKb], writes=[GVHb])
                for d in range(2):
                    P.op("dve", lambda e: e.scalar_tensor_tensor(QTg[:, d, :, :], GQ[:, 0:2, :], 0.125, EBg[:, d, :, :], ALU.mult, ALU.mult),
                         reads=[GQb, EBgb], writes=[QTgb])
                    P.op("dve", lambda e: e.tensor_tensor(KTf[:, :, :], GQ[:, 2:4, :], ENg[:, d, :, :], ALU.mult), reads=[GQb, ENgb], writes=[KTfb])
                    if mode == "S":
                        for tt in range(2):
                            bk = B[1 + tt]
                            P.op("pe", lambda e: e.transpose(bk[0][:, 0:128], KTf[:, tt, :], self.ident[:, :]), reads=[KTfb, self.identb], writes=[bk[1]])
                            P.op("act", lambda e: e.copy(KTK[:, tt, :], bk[0][:, 0:128]), reads=[bk[1]], writes=[KTKb])
                        for tt in range(2):
                            mm(B[3], B[3][0][:, tt * 256:(tt + 1) * 256], KTK[:, tt, :], GVH[:, tt * 256:(tt + 1) * 256], True, True, [KTKb, GVHb])
                        for tt in range(2):
                            decc = EBg[:, d, tt, ENDC[d]:ENDC[d] + 1]
                            for e_ in range(2):
                                P.op("dve", lambda e: e.tensor_scalar_mul(DSt[e_ * 64:(e_ + 1) * 64, tt * 128:(tt + 1) * 128],
                                                                          B[3][0][e_ * 64:(e_ + 1) * 64, tt * 256 + e_ * 128:tt * 256 + (e_ + 1) * 128],
                                                                          decc[e_ * 64:(e_ + 1) * 64, :]),
                                     reads=[B[3][1], EBgb], writes=[DStb])
                            P.op("dve", lambda e: e.tensor_copy(DSt[:, 256 + tt:257 + tt], decc), reads=[EBgb], writes=[DStb])
                        P.dma("act", self.DS["gla"].ap[c, d], DSt[:, 0:256], reads=[DStb], writes=self.DS["gla"].bufs(c, c + 1))
                        P.dma("act", self.DEC["gla"].ap[c, d], DSt[:, 256:258], reads=[DStb], writes=self.DEC["gla"].bufs(c, c + 1))
                    elif "g" in getattr(self, "obr", "smg"):
                        P.op("dve", lambda e: e.tensor_copy(KTg[:, d, :, :], KTf[:, :, :]), reads=[KTfb], writes=[KTgb])
                        for h in range(4):
                            pb_ = (h % 2) * 64
                            bk_ = B[0] if h % 2 == 0 else B[5]
                            mm(bk_, bk_[0][:, (h // 2) * 128:(h // 2 + 1) * 128], KTg[pb_:pb_ + 64, d, h // 2, :], QTg[pb_:pb_ + 64, d, h // 2, :],
                               True, True, [KTgb, QTgb])
                        for h in range(4):
                            bk_ = B[0] if h % 2 == 0 else B[5]
                            P.op("dve", lambda e: e.tensor_tensor(AT[:, h, :], bk_[0][:, (h // 2) * 128:(h // 2 + 1) * 128], M01[d][:, :], ALU.mult),
                                 reads=[bk_[1], cb_], writes=[ATb])
                        P.dma("sp", ST[:, 0:256], self.ENT["gla"].ap[c, d], reads=self.ENT["gla"].bufs(c, c + 1), writes=[STb])
                        for h in range(4):
                            pb_ = (h % 2) * 64
                            o_ = B[3][0][:, h * 128:(h + 1) * 128]
                            mm(B[3], o_, AT[:, h, :], GVH[:, h * 128:(h + 1) * 128], True, False, [ATb, GVHb], inc=False)
                            mm(B[3], o_, QTg[pb_:pb_ + 64, d, h // 2, :], ST[pb_:pb_ + 64, (h // 2) * 128:(h // 2 + 1) * 128], False, True,
                               [QTgb, STb], inc=True)
                        fg_ = FIN[:, 1536:2048]
                        if d == 0:
                            P.op("act", lambda e: e.copy(fg_, B[3][0][:, :]), reads=[B[3][1]], writes=[FINb])
                        else:
                            P.op("dve", lambda e: e.tensor_tensor(fg_, fg_, B[3][0][:, :], ALU.add), reads=[FINb, B[3][1]], writes=[FINb])
                if mode == "O":
                    fg = FIN[:, 1536:2048]
                    gnorm(fg, FINb, 4, 128, rowv("gnw"), True)
                    P.op("act", lambda e: e.activation(big1[:, 0:512], gr, AF.Silu), reads=[TKb], writes=[big1b])
                    P.op("dve", lambda e: e.tensor_tensor(fg, fg, big1[:, 0:512], ALU.mult), reads=[FINb, big1b], writes=[FINb])
                    for kt in range(16):
                        bk = B[1 + kt % 2]
                        P.op("pe", lambda e: e.transpose(bk[0][:, 0:128], FIN[:, kt * 128:(kt + 1) * 128], self.ident[:, :]),
                             reads=[FINb, self.identb], writes=[bk[1]])
                        P.op("act", lambda e: e.copy(FT[:, kt, :], bk[0][:, 0:128]), reads=[bk[1]], writes=[FTb])
                    P.dma("act", self.FINT.fm()[:, :, r0:r1], FT[:, :, :], reads=[FTb], writes=self.FINT.bufs(c, c + 1))
            P.barrier()

    def phase_recur(self, l):
        P, nc = self.P, self.nc
        KINDS = (("ssd", 64, 1024, "ssd", 16, 0), ("ml", 128, 512, "ml", 4, 1024), ("mln", 128, 4, "ml", 4, 1536),
                 ("gla", 128, 256, "gla", 2, 1540))
        with ExitStack() as es:
            sb = lambda n, s, dt=F32: es.enter_context(nc.sbuf_tensor(uniq(n), s, dt))
            T_ = {}
            for (nm, np_, w, dkey, dw, xo) in KINDS:
                T_[nm] = dict(S=sb("r_s" + nm, [128, w]), Sb=Buf(),
                              Sh=[(sb("r_sh%s%d" % (nm, i), [128, w], BF16), Buf()) for i in range(2)],
                              Dl=[(sb("r_d%s%d" % (nm, i), [128, w]), Buf()) for i in range(4)],
                              Dc=[(sb("r_c%s%d" % (nm, i), [128, 16]), Buf()) for i in range(4)],
                              X0=sb("r_x0" + nm, [128, w]), X1=sb("r_x1" + nm, [128, w]), Xb=Buf(), it=0)

            def run(nm, np_, w, dkey, dw, d, order):
                t = T_[nm]; S = t["S"]; Sb = t["Sb"]
                for c in order:
                    sh, shb = t["Sh"][t["it"] % 2]; dl, dlb = t["Dl"][t["it"] % 4]; dc, dcb = t["Dc"][t["it"] % 4]; t["it"] += 1
                    P.op("pool", lambda e: e.tensor_copy(sh[:np_, :], S[:np_, :]), reads=[Sb], writes=[shb])
                    P.dma("sp", self.ENT[nm].ap[c, d], sh[:np_, :], reads=[shb], writes=self.ENT[nm].bufs(c, c + 1))
                    P.dma("act", dl[:np_, :], self.DS[nm].ap[c, d], reads=self.DS[nm].bufs(c, c + 1), writes=[dlb])
                    P.dma("act", dc[:np_, 0:dw], self.DEC[dkey].ap[c, d], reads=self.DEC[dkey].bufs(c, c + 1), writes=[dcb])
                    if nm == "ssd":
                        sv = S[:np_, :].rearrange("p (h w) -> p h w", w=64)
                        P.op("dve", lambda e: e.tensor_tensor(sv, sv, dc[:np_, 0:16].unsqueeze(2).to_broadcast([np_, 16, 64]), ALU.mult),
                             reads=[Sb, dcb], writes=[Sb])
                    elif nm == "ml":
                        sv = S[:, :].rearrange("p (h w) -> p h w", w=128)
                        P.op("dve", lambda e: e.tensor_tensor(sv, sv, dc[:, 0:4].unsqueeze(2).to_broadcast([128, 4, 128]), ALU.mult),
                             reads=[Sb, dcb], writes=[Sb])
                    elif nm == "mln":
                        P.op("dve", lambda e: e.tensor_tensor(S[:, :], S[:, :], dc[:, 0:4], ALU.mult), reads=[Sb, dcb], writes=[Sb])
                    else:
                        sv = S[:, :].rearrange("p (t w) -> p t w", w=128)
                        P.op("dve", lambda e: e.tensor_tensor(sv, sv, dc[:, 0:2].unsqueeze(2).to_broadcast([128, 2, 128]), ALU.mult),
                             reads=[Sb, dcb], writes=[Sb])
                    P.op("dve", lambda e: e.tensor_tensor(S[:np_, :], S[:np_, :], dl[:np_, :], ALU.add), reads=[Sb, dlb], writes=[Sb])

            xob = self.XOUT.bufs(0, 1)
            for (nm, np_, w, dkey, dw, xo) in KINDS:
                t = T_[nm]
                P.op("dve", lambda e: e.memset(t["S"][:, :], 0.0), writes=[t["Sb"]])
                run(nm, np_, w, dkey, dw, 0, list(range(NCH)))
                P.dma("sp", self.XOUT.ap[0:np_, xo:xo + w], t["S"][:np_, :], reads=[t["Sb"]], writes=xob)
            P.collective(self.XOUT.ap, self.XALL.ap, reads=xob, writes=self.XALL.bufs(0, 1), dummy=self.dummy[:, :], dummyb=self.dummyb)
            for (nm, np_, w, dkey, dw, xo) in KINDS:
                t = T_[nm]
                P.op("dve", lambda e: e.memset(t["S"][:, :], 0.0), writes=[t["Sb"]])
                run(nm, np_, w, dkey, dw, 1, [1, 0])
                P.dma("sp", t["X0"][:np_, :], self.XALL.ap[0:np_, xo:xo + w], reads=self.XALL.bufs(0, 1) + [self.dummyb], writes=[t["Xb"]])
                P.dma("sp", t["X1"][:np_, :], self.XALL.ap[128:128 + np_, xo:xo + w], reads=self.XALL.bufs(0, 1) + [self.dummyb], writes=[t["Xb"]])
                P.op("dve", lambda e: e.tensor_scalar_mul(t["X0"][:np_, :], t["X0"][:np_, :], self.selt[:np_, 0:1]),
                     reads=[t["Xb"], self.seltb], writes=[t["Xb"]])
                P.op("dve", lambda e: e.scalar_tensor_tensor(t["S"][:np_, :], t["X1"][:np_, :], self.selt[:np_, 1:2], t["X0"][:np_, :],
                                                             ALU.mult, ALU.add), reads=[t["Xb"], self.seltb, t["Sb"]], writes=[t["Sb"]])
                run(nm, np_, w, dkey, dw, 1, list(range(NCH - 1, 1, -1)))
            P.barrier()

    def phase_halo(self, l):
        P, nc = self.P, self.nc
        hb = self.HALO_OUT.bufs(0, 1)
        P.dma("sp", self.HALO_OUT.ap[0:1536, :], self.XBC.ap[:, T - 64:T], reads=self.XBC.bufs(NCH - 1, NCH), writes=hb)
        P.dma("sp", self.HALO_OUT.ap[1536:2560, :], self.QKM.ap[:, T - 64:T], reads=self.QKM.bufs(NCH - 1, NCH), writes=hb)
        P.collective(self.HALO_OUT.ap, self.HALO_ALL.ap, reads=hb, writes=self.HALO_ALL.bufs(0, 1), dummy=self.dummy[:, :], dummyb=self.dummyb)
        o_anti, _ = CO["anti"]
        with ExitStack() as es:
            sb = lambda n, s, dt=F32: es.enter_context(nc.sbuf_tensor(uniq(n), s, dt))
            anti = sb("h_anti", [128, 128]); ab = Buf()
            P.dma("sp", anti[:, :], self.CONST.ap[:, o_anti:o_anti + 128], writes=[ab])
            H0 = sb("h_0", [128, 20, 64]); H1 = sb("h_1", [128, 20, 64]); Hb = Buf()
            TT = [(sb("h_t%d" % i, [64, 128]), Buf()) for i in range(2)]
            R = sb("h_r", [128, 20, 64]); Rb = Buf()
            rd = self.HALO_ALL.bufs(0, 1) + [self.dummyb]
            P.dma("sp", H0[:, :, :], self.HALO_ALL.ap[0:2560, :].rearrange("(c p) t -> p c t", p=128), reads=rd, writes=[Hb])
            P.dma("sp", H1[:, :, :], self.HALO_ALL.ap[2560:5120, :].rearrange("(c p) t -> p c t", p=128), reads=rd, writes=[Hb])
            P.op("dve", lambda e: e.tensor_scalar_mul(H0[:, :, :], H0[:, :, :], self.selt[:, 0:1]), reads=[Hb, self.seltb], writes=[Hb])
            P.op("dve", lambda e: e.scalar_tensor_tensor(H0[:, :, :], H1[:, :, :], self.selt[:, 1:2], H0[:, :, :], ALU.mult, ALU.add),
                 reads=[Hb, self.seltb], writes=[Hb])
            for ct in range(20):
                bk = self.banks[ct % 2]
                tt, ttb = TT[ct % 2]
                P.op("pe", lambda e: e.transpose(bk[0][0:64, 0:128], H0[:, ct, :], self.ident[:, :]), reads=[Hb, self.identb], writes=[bk[1]])
                P.op("act", lambda e: e.copy(tt[:, :], bk[0][0:64, 0:128]), reads=[bk[1]], writes=[ttb])
                bk2 = self.banks[2 + ct % 2]
                P.op("pe", lambda e: e.matmul(bk2[0][:, 0:64], tt[:, :], anti[0:64, 64:128], start=True, stop=True), reads=[ttb, ab], writes=[bk2[1]])
                P.op("act", lambda e: e.copy(R[:, ct, :], bk2[0][:, 0:64]), reads=[bk2[1]], writes=[Rb])
            P.dma("sp", self.HALO_REV.ap.rearrange("(c p) t -> p c t", p=128), R[:, :, :], reads=[Rb], writes=self.HALO_REV.bufs(0, 1))
            P.barrier()

    def phase_merge(self, l, blocks):
        P, nc = self.P, self.nc
        Wb = [self.w_br_ssd.ap[l].rearrange("(kc p) n -> p kc n", p=128), self.w_br_ml.ap[l].rearrange("(kc p) n -> p kc n", p=128),
              self.w_br_gla.ap[l].rearrange("(kc p) n -> p kc n", p=128)]
        Wo = self.w_out.ap[l].rearrange("(kc p) n -> p kc n", p=128)
        KR = [(0, 8), (8, 12), (12, 16)]
        om, _ = PO["mergeb"]
        with ExitStack() as es:
            sb = lambda n, s, dt=F32: es.enter_context(nc.sbuf_tensor(uniq(n), s, dt))
            F = sb("m_f", [128, 16, 1152], BF16); Fb = Buf()
            MG = sb("m_mg", [128, 16, 1152], BF16)
            WB = [(sb("m_w%d" % i, [128, 16, 128], BF16), Buf()) for i in range(3)]
            GPt2 = [[(sb("m_gp%d_%d" % (j, i), [128, 1152]), Buf()) for i in range(3)] for j in range(2)]
            Gs = [(sb("m_g%d" % i, [128, 512]), Buf()) for i in range(2)]
            ACC = [(sb("m_acc%d" % i, [128, 512]), Buf()) for i in range(2)]
            HB = [(sb("m_h%d" % i, [128, 1152]), Buf()) for i in range(2)]
            ZB = [(sb("m_z%d" % i, [128, 1152]), Buf()) for i in range(2)]
            wi = 0; gi = 0; ai = 0
            for (c0, c1) in blocks:
                nb = (c1 - c0) * 128
                t0 = c0 * 128
                subs = subs_for(c0, c1)
                MGb = [[Buf() for _ in subs] for _ in range(16)]
                P.dma("sp", F[:, :, :nb], self.FINT.fm()[:, :, t0:t0 + nb], reads=self.FINT.bufs(c0, c1), writes=[Fb])
                for fo in range(16):
                    wt, wb = WB[wi % 3]; wi += 1
                    GPt = GPt2[fo % 2]
                    for X in range(3):
                        P.dma("pool", wt[:, KR[X][0]:KR[X][1], :], Wb[X][:, :, fo * 128:(fo + 1) * 128], writes=[wb])
                        P.dma("sp", GPt[X][0][:, :nb], self.GP.ap[X * 2048 + fo * 128:X * 2048 + (fo + 1) * 128, t0:t0 + nb],
                              reads=self.GP.bufs(c0, c1), writes=[GPt[X][1]])
                    for si_, (o, n) in enumerate(subs):
                        acc, accb = ACC[ai % 2]; ai += 1
                        for X in range(3):
                            ps, pb = self.bank()
                            for kc in range(KR[X][0], KR[X][1]):
                                P.op("pe", lambda e: e.matmul(ps[:, :n], wt[:, kc, :], F[:, kc, o:o + n], start=(kc == KR[X][0]),
                                                              stop=(kc == KR[X][1] - 1)), reads=[wb, Fb], writes=[pb], inc=(kc == KR[X][1] - 1))
                            g, gb = Gs[gi % 2]; gi += 1
                            P.op("act", lambda e: e.activation(g[:, :n], GPt[X][0][:, o:o + n], AF.Sigmoid,
                                                               bias=self.par[:, om + X * 16 + fo:om + X * 16 + fo + 1]),
                                 reads=[GPt[X][1], self.parb], writes=[gb])
                            if X == 0:
                                P.op("dve", lambda e: e.tensor_tensor(acc[:, :n], ps[:, :n], g[:, :n], ALU.mult), reads=[pb, gb], writes=[accb])
                            else:
                                P.op("dve", lambda e: e.tensor_tensor(g[:, :n], ps[:, :n], g[:, :n], ALU.mult), reads=[pb, gb], writes=[gb])
                                if X == 1:
                                    P.op("dve", lambda e: e.tensor_tensor(acc[:, :n], acc[:, :n], g[:, :n], ALU.add), reads=[accb, gb], writes=[accb])
                                else:
                                    P.op("dve", lambda e: e.tensor_tensor(MG[:, fo, o:o + n], acc[:, :n], g[:, :n], ALU.add),
                                         reads=[accb, gb], writes=[MGb[fo][si_]])
                for fo in range(16):
                    wt, wb = WB[wi % 3]; wi += 1
                    P.dma("pool", wt[:, :, :], Wo[:, :, fo * 128:(fo + 1) * 128], writes=[wb])
                    hb, hbb = HB[fo % 2]
                    zb, zbb = ZB[fo % 2]
                    P.dma("sp", hb[:, :nb], self.HT.ap[fo * 128:(fo + 1) * 128, t0:t0 + nb], reads=self.HT.bufs(c0, c1), writes=[hbb])
                    for si_, (o, n) in enumerate(subs):
                        ps, pb = self.bank()
                        for kc in range(16):
                            P.op("pe", lambda e: e.matmul(ps[:, :n], wt[:, kc, :], MG[:, kc, o:o + n], start=(kc == 0), stop=(kc == 15)),
                                 reads=[wb, MGb[kc][si_]], writes=[pb], inc=(kc == 15))
                        v = 1 if (c0 == 0 and o == 0) else 0
                        P.op("dve", lambda e: e.scalar_tensor_tensor(zb[:, o:o + n], ps[:, :n], self.mvec(5, fo, v), hb[:, o:o + n],
                                                                     ALU.mult, ALU.add), reads=[pb, hbb, self.modb], writes=[zbb])
                    P.dma("sp", self.ZT.ap[fo * 128:(fo + 1) * 128, t0:t0 + nb], zb[:, :nb], reads=[zbb], writes=self.ZT.bufs(c0, c1))
            P.barrier()

    def phase_mixer(self, l):
        if not hasattr(self, "XBC"):
            self.alloc_mixer_scratch()
        self.phase_proj(l)
        self.phase_halo(l)
        self.phase_conv(l)
        self.scan_pass(l, "S", list(range(NCH)))
        self.phase_recur(l)
        if l < DEPTH - 1:
            self.scan_pass(l, "O", list(range(NCH)))
            self.phase_merge(l, [(0, 9), (9, 18)])
        else:
            self.scan_pass(l, "O", list(range(2, NCH)))
            self.phase_merge(l, [(2, 10), (10, 18)])

    def finish(self):
        P = self.P
        for i, (src, dst) in enumerate(self.dumps):
            P.dma("sp", dst.ap, src.ap, reads=list(src.b.values()), writes=[Buf()])
        P.barrier(engines=["sp"])
        self.es.close()
        return self.nc


def build_program(stop_after=None, dump=None):
    k = K(stop_after=stop_after, dump=dump)
    hsrc = k.xT
    for l in range(DEPTH):
        k.phase_mod(l)
        k.phase_modapply(hsrc, 0)
        k.phase_ffn(l, 0, 0, hsrc)
        hsrc = k.HT
        k.phase_ln(0, 1)
        if stop_after == "ffn0":
            break
        k.phase_mixer(l)
        last = (l == DEPTH - 1)
        k.phase_ln(1, 2, skip_ctx=last)
        k.phase_ffn(l, 1, 2, k.HT, blocks=(((2, 10), (10, 18)) if last else ((0, 9), (9, 18))))
        k.phase_ln(2, None, final=last)
    return k.finish()


def _swap_dirs(inp, l):
    d = {}
    d["ssd_conv_w"] = inp["ssd_conv_w"][l][::-1, ::-1, :]
    d["ml_conv_w"] = inp["ml_conv_w"][l][::-1, ::-1, :]
    d["ssd_dt_bias"] = inp["ssd_dt_bias"][l][::-1]
    d["ssd_a_log"] = inp["ssd_a_log"][l][::-1]
    d["ml_gate_b"] = inp["ml_gate_b"][l][::-1]
    d["gla_w2"] = inp["gla_w2"][l][::-1]
    d["gla_b2"] = inp["gla_b2"][l][::-1]
    return d


def host_inputs(inp, ncores=8):
    f = lambda a: np.ascontiguousarray(np.asarray(a, dtype=np.float32))
    inp = {k: f(v) for k, v in inp.items()}
    consts = make_consts()
    inp_odd = dict(inp)
    for k in ("ssd_conv_w", "ml_conv_w", "ssd_dt_bias", "ssd_a_log", "ml_gate_b", "gla_w2", "gla_b2"):
        inp_odd[k] = np.stack([_swap_dirs(inp, l)[k] for l in range(DEPTH)])
    perm = np.arange(INTOT)
    for (o, n) in ((OFF["dt"], 32), (OFF["mg"], 16), (OFF["lr"], 32)):
        h = n // 2
        perm[o:o + h] = np.arange(o + h, o + n)
        perm[o + h:o + n] = np.arange(o, o + h)
    w_in_odd = np.ascontiguousarray(inp["w_in"][:, :, perm])
    per = []
    for par, src, w_in in ((0, inp, inp["w_in"]), (1, inp_odd, w_in_odd)):
        per.append(dict(PAR=np.stack([make_par(src, l) for l in range(DEPTH)]), ROW=np.stack([make_row(src, l) for l in range(DEPTH)]),
                        GW2=np.stack([make_gw2(src, l) for l in range(DEPTH)]), w_in=w_in,
                        SELF=np.ascontiguousarray(np.broadcast_to(np.array([[0.0, 1.0]] if par == 0 else [[1.0, 0.0]], np.float32), (128, 2)))))
    shared = dict(CONST=consts, w_mod=inp["w_mod"], ffn_w_in=inp["ffn_w_in"], ffn_w_out=inp["ffn_w_out"],
                  w_br_ssd=inp["w_br_ssd"], w_br_ml=inp["w_br_ml"], w_br_gla=inp["w_br_gla"], w_out=inp["w_out"])
    maps = []
    for core in range(ncores):
        b, half = core // 2, core % 2
        cx = inp["ctx"][b]
        xx = inp["x"][b, half * NLT * 128:(half + 1) * NLT * 128]
        if half == 1:
            cx = cx[::-1]
            xx = xx[::-1]
        tok = np.concatenate([cx, xx], axis=0)
        xT = np.ascontiguousarray(tok.T)
        cv = np.stack([inp["c"][b], inp["c_ctx"]], axis=1)
        cT = np.ascontiguousarray(cv.reshape(16, 128, 2).transpose(1, 0, 2).reshape(128, 32))
        m = dict(shared)
        m.update(per[half])
        m["xT"] = xT
        m["cT"] = cT
        maps.append(m)
    return maps


def assemble(results, nb):
    out = []
    for b in range(nb):
        h0 = np.asarray(results[2 * b]["outT"]).T
        h1 = np.asarray(results[2 * b + 1]["outT"]).T[::-1]
        out.append(np.concatenate([h0, h1], axis=0))
    return np.stack(out, axis=0).astype(np.float32)


def kernel(**inputs):
    nc = build_program()
    maps = host_inputs(inputs)
    res = run_bass_kernel_spmd(nc, maps, core_ids=list(range(8)))
    return assemble(res.results, 4)
```

```python
import numpy as np
from contextlib import ExitStack
import concourse.bass as bass
import concourse.mybir as mybir
from concourse.bass_utils import run_bass_kernel_spmd

F32 = mybir.dt.float32
BF16 = mybir.dt.bfloat16
ALU = mybir.AluOpType
AF = mybir.ActivationFunctionType

D = 2048
DFF = 5632
NFC = 16
NCT = 2
NLT = 16
NCH = NCT + NLT
T = NCH * 128
DEPTH = 2
ALPHA = (2 * DEPTH) ** 0.25
EPS = 1e-5
INTOT = 12368
OFF = dict(z=0, xbc=1024, dt=2560, mq=2592, mk=3104, mv=3616, mo=4128, mg=4640,
           gq=4656, gk=4912, gv=5168, gr=5680, lr=6192, gate=6224)
NEG = -30000.0
_UQ = [0]


def uniq(n):
    _UQ[0] += 1
    return "%s_%d" % (n, _UQ[0])


class Buf:
    __slots__ = ("w", "r", "ex")

    def __init__(self, ex=False):
        self.w = None
        self.r = {}
        self.ex = ex


class DT:
    def __init__(self, nc, name, shape, dt, kind="Internal"):
        self.t = nc.dram_tensor(name, list(shape), dt, kind=kind)
        self.ap = self.t.ap()
        self.b = {}

    def bufs(self, c0, c1):
        return [self.b.setdefault(c, Buf()) for c in range(c0, c1)]

    def fm(self):
        return self.ap.rearrange("(c p) t -> p c t", p=128)


class Prog:
    def __init__(self, nc, es):
        self.nc = nc
        self.E = {"pe": nc.tensor, "act": nc.scalar, "dve": nc.vector, "pool": nc.gpsimd, "sp": nc.sync}
        self.R = 4
        self.EP = 700
        self.sem = {e: [es.enter_context(nc.semaphore("s_%s%d" % (e, i))) for i in range(self.R)] for e in self.E}
        self.cnt = {e: 0 for e in self.E}
        self.known = {e: {f: 0 for f in self.E} for e in self.E}
        self.ND = 24
        self.dsem = [es.enter_context(nc.semaphore("d%d" % i)) for i in range(self.ND)]
        self.dval = [0] * self.ND
        self.dseq = 0
        self.dknown = {e: [0] * self.ND for e in self.E}
        self.nins = 0
        self.csems = [es.enter_context(nc.semaphore("cc%d" % i)) for i in range(4)]

    def _wait(self, x, tok):
        if tok[0] == "c":
            _, e, k = tok
            if x == "pe" and e == "pe":
                return
            if self.known[x][e] >= k:
                return
            self.known[x][e] = k
            ep = (k - 1) // self.EP
            val = (ep // self.R) * self.EP + ((k - 1) % self.EP) + 1
            self.E[x].wait_ge(self.sem[e][ep % self.R], val)
        else:
            _, i, v = tok
            if self.dknown[x][i] >= v:
                return
            self.dknown[x][i] = v
            self.E[x].wait_ge(self.dsem[i], v)

    def _deps(self, reads, writes):
        deps = set()
        for b in reads:
            if b.w is not None:
                deps.add(b.w)
        for b in writes:
            if b.w is not None:
                deps.add(b.w)
            deps.update(b.r.values())
        return deps

    def op(self, e, fn, reads=(), writes=(), inc=True):
        if e != "pe":
            rx = [b for b in reads if b.ex]
            if rx:
                writes = list(writes) + rx
        for t in self._deps(reads, writes):
            self._wait(e, t)
        self.nins += 1
        if inc:
            self.cnt[e] += 1
            k = self.cnt[e]
            ep = (k - 1) // self.EP
            fn(self.E[e]).then_inc(self.sem[e][ep % self.R], 1)
        else:
            k = self.cnt[e] + 1
            fn(self.E[e])
        tok = ("c", e, k)
        for b in reads:
            b.r[e] = tok
        for b in writes:
            b.w = tok
            b.r = {}

    def dma(self, q, out, in_, reads=(), writes=()):
        i = self.dseq % self.ND
        self.dseq += 1
        if self.dval[i] > 0:
            self._wait(q, ("d", i, self.dval[i]))
        for t in self._deps(reads, writes):
            self._wait(q, t)
        self.dval[i] += 16
        tok = ("d", i, self.dval[i])
        self.nins += 1
        self.E[q].dma_start(out=out, in_=in_).then_inc(self.dsem[i], 16)
        for b in reads:
            b.r[("d", i)] = tok
        for b in writes:
            b.w = tok
            b.r = {}

    def collective(self, src, dst, reads, writes, dummy, dummyb):
        for t in self._deps(reads, writes):
            self._wait("pool", t)
        csem = self.csems.pop()
        self.nc.gpsimd.collective_compute("AllGather", ALU.bypass, replica_groups=REPLICA_GROUPS,
                                          ins=[src.opt()], outs=[dst.opt()]).then_inc(csem)
        self.nc.gpsimd.wait_ge(csem, 1)
        self.op("pool", lambda e: e.memset(dummy, 0.0), reads=reads, writes=list(writes) + [dummyb])

    def barrier(self, engines=None):
        for x in (engines or self.E):
            for e in self.E:
                if e != x and self.cnt[e] > 0:
                    self._wait(x, ("c", e, self.cnt[e]))
            for i in range(self.ND):
                if self.dval[i] > 0:
                    self._wait(x, ("d", i, self.dval[i]))


REPLICA_GROUPS = [[0, 1], [2, 3], [4, 5], [6, 7]]


def merge_into(dst, srcs):
    for s in srcs:
        if s.w is not None:
            dst.r[("m", id(s))] = s.w
        for k, v in s.r.items():
            dst.r[("m", id(s), k)] = v


CO = {}
_o = 0
for _n, _w in (("ident", 128), ("ones", 128), ("trif", 128), ("trib", 128), ("mbf", 128), ("mbb", 128),
               ("m01f", 128), ("m01b", 128), ("self", 128), ("selb", 128), ("esel", 16 * 128), ("anti", 128)):
    CO[_n] = (_o, _w)
    _o += _w
NCONST = _o


def make_consts():
    c = np.zeros((128, NCONST), np.float32)
    i = np.arange(128)
    J, Tt = np.meshgrid(i, i, indexing="ij")

    def put(n, m):
        o, w = CO[n]
        c[:, o:o + w] = m

    put("ident", (J == Tt).astype(np.float32))
    put("ones", np.ones((128, 128), np.float32))
    put("trif", (J <= Tt).astype(np.float32))
    put("trib", (J >= Tt).astype(np.float32))
    put("mbf", np.where(J <= Tt, 0.0, NEG).astype(np.float32))
    put("mbb", np.where(J >= Tt, 0.0, NEG).astype(np.float32))
    put("m01f", (J <= Tt).astype(np.float32))
    put("m01b", (J >= Tt).astype(np.float32))
    sf = np.zeros((128, 128), np.float32); sf[127, :] = 1.0
    sb = np.zeros((128, 128), np.float32); sb[0, :] = 1.0
    put("self", sf)
    put("selb", sb)
    es = np.zeros((128, 16, 128), np.float32)
    for h in range(16):
        es[h, h, :] = 1.0
    put("esel", es.reshape(128, 16 * 128))
    put("anti", (J + Tt == 127).astype(np.float32))
    return c


PO = {}
_o = 0
for _n, _w in (("bmod", 144), ("lng", 48), ("lnb", 48), ("mergeb", 48), ("scw", 12 * 9), ("scb", 12),
               ("mcw", 8 * 9), ("mcb", 8)):
    PO[_n] = (_o, _w)
    _o += _w
NPAR = _o
RO = {}
_o = 0
for _n, _w in (("dtb", 32), ("alog", 32), ("dsk", 16), ("snw", 1024), ("mnw", 512), ("gnw", 512), ("mgb", 16)):
    RO[_n] = (_o, _w)
    _o += _w
NROW = _o


def fm_vec(v):
    return np.ascontiguousarray(v.reshape(-1, 128).T)


def make_par(inp, l):
    p = np.zeros((128, NPAR), np.float32)

    def put(n, m):
        o, w = PO[n]
        p[:, o:o + w] = m.reshape(128, w)

    put("bmod", fm_vec(inp["b_mod"][l]))
    put("lng", np.concatenate([fm_vec(inp["ln_g"][l, k]) for k in range(3)], axis=1))
    put("lnb", np.concatenate([fm_vec(inp["ln_b"][l, k]) for k in range(3)], axis=1))
    put("mergeb", np.concatenate([fm_vec(inp["merge_b"][l, k]) for k in range(3)], axis=1))
    w = inp["ssd_conv_w"][l].reshape(9, 1536)
    put("scw", np.ascontiguousarray(w.T.reshape(12, 128, 9).transpose(1, 0, 2)))
    put("scb", fm_vec(inp["ssd_conv_b"][l]))
    w = inp["ml_conv_w"][l].reshape(9, 1024)
    put("mcw", np.ascontiguousarray(w.T.reshape(8, 128, 9).transpose(1, 0, 2)))
    put("mcb", fm_vec(inp["ml_conv_b"][l]))
    return p


def make_row(inp, l):
    r = np.zeros((NROW,), np.float32)

    def put(n, v):
        o, w = RO[n]
        r[o:o + w] = v.reshape(w)

    put("dtb", inp["ssd_dt_bias"][l])
    put("alog", inp["ssd_a_log"][l])
    put("dsk", inp["ssd_d"][l])
    put("snw", inp["ssd_norm_w"][l])
    put("mnw", inp["ml_norm_w"][l])
    put("gnw", inp["gla_norm_w"][l])
    put("mgb", inp["ml_gate_b"][l])
    return np.ascontiguousarray(np.broadcast_to(r[None, :], (128, NROW)))


def make_gw2(inp, l):
    g = np.zeros((33, 512), np.float32)
    g[0:16, 0:256] = inp["gla_w2"][l, 0]
    g[16:32, 256:512] = inp["gla_w2"][l, 1]
    g[32, :] = inp["gla_b2"][l].reshape(512)
    return g


def subs_for(c0, c1):
    out = []
    o = 0
    nb = (c1 - c0) * 128
    if c0 == 0:
        out.append((0, 256))
        o = 256
    rem = nb - o
    npieces = (rem + 511) // 512
    base = (rem // 128) // npieces
    extra = (rem // 128) % npieces
    for i in range(npieces):
        n = (base + (1 if i < extra else 0)) * 128
        out.append((o, n))
        o += n
    return out


class K:
    def __init__(self, stop_after=None, dump=None):
        self.stop_after = stop_after
        self.dump = dump
        nc = self.nc = bass.Bass("TRN2", target_bir_lowering=False)
        self.es = ExitStack()
        es = self.es
        P = self.P = Prog(nc, es)
        L = DEPTH
        ein = lambda n, s, dt=F32: DT(nc, n, s, dt, kind="ExternalInput")
        self.xT = ein("xT", [D, T])
        self.cT = ein("cT", [128, 32])
        self.SELF = ein("SELF", [128, 2])
        self.CONST = ein("CONST", [128, NCONST])
        self.PAR = ein("PAR", [L, 128, NPAR])
        self.HALO_OUT = DT(nc, "HALO_OUT", [2560, 64], F32)
        self.HALO_ALL = DT(nc, "HALO_ALL", [5120, 64], F32)
        self.HALO_REV = DT(nc, "HALO_REV", [2560, 64], F32)
        self.XOUT = DT(nc, "XOUT", [128, 2048], F32)
        self.XALL = DT(nc, "XALL", [256, 2048], F32)
        self.ROW = ein("ROW", [L, 128, NROW])
        self.GW2 = ein("GW2", [L, 33, 512])
        self.w_mod = ein("w_mod", [L, D, 9 * D])
        self.ffn_w_in = ein("ffn_w_in", [L, 2, D, 2 * DFF])
        self.ffn_w_out = ein("ffn_w_out", [L, 2, DFF, D])
        self.w_in = ein("w_in", [L, D, INTOT])
        self.w_br_ssd = ein("w_br_ssd", [L, 1024, D])
        self.w_br_ml = ein("w_br_ml", [L, 512, D])
        self.w_br_gla = ein("w_br_gla", [L, 512, D])
        self.w_out = ein("w_out", [L, D, D])
        self.outT = DT(nc, "outT", [D, NLT * 128], F32, kind="ExternalOutput")
        self.HT = DT(nc, "HT", [D, T], F32)
        self.UT = DT(nc, "UT", [D, T], BF16)
        self.ZT = DT(nc, "ZT", [D, T], F32)
        self.dumps = []
        sb = lambda n, s, dt=F32: es.enter_context(nc.sbuf_tensor(uniq(n), s, dt))
        self.ident = sb("ident", [128, 128]); self.identb = Buf()
        self.ones = sb("ones", [128, 128]); self.onesb = Buf()
        self.identh = sb("identh", [128, 128], BF16); self.identhb = Buf()
        self.par = sb("par", [128, NPAR]); self.parb = Buf()
        self.modv = sb("modv", [128, 144, 2]); self.modb = Buf()
        self.selt = sb("selt", [128, 2]); self.seltb = Buf()
        self.dummy = sb("ccdummy", [128, 4]); self.dummyb = Buf()
        self.banks = []
        for i in range(8):
            pt = es.enter_context(nc.psum_tensor("ps%d" % i, [128, 512], F32))
            self.banks.append((pt, Buf(ex=True)))
        self.bi = 0
        o, w = CO["ident"]
        P.dma("sp", self.ident[:, :], self.CONST.ap[:, o:o + w], writes=[self.identb])
        o, w = CO["ones"]
        P.dma("sp", self.ones[:, :], self.CONST.ap[:, o:o + w], writes=[self.onesb])
        P.op("dve", lambda e: e.tensor_copy(self.identh[:, :], self.ident[:, :]), reads=[self.identb], writes=[self.identhb])
        P.dma("sp", self.selt[:, :], self.SELF.ap[:, :], writes=[self.seltb])

    def add_dump(self, name, src, shape, dt=F32):
        dst = DT(self.nc, name, shape, dt, kind="ExternalOutput")
        self.dumps.append((src, dst))

    def bank(self):
        b = self.banks[self.bi % 8]
        self.bi += 1
        return b

    def mvec(self, k9, fc, v):
        return self.modv[:, k9 * 16 + fc, v:v + 1]

    def parv(self, name, idx):
        o, w = PO[name]
        return self.par[:, o + idx:o + idx + 1]

    def phase_mod(self, l):
        P, nc = self.P, self.nc
        P.dma("sp", self.par[:, :], self.PAR.ap[l], writes=[self.parb])
        with nc.sbuf_tensor(uniq("wm0"), [128, 16, 512], F32) as wm0, nc.sbuf_tensor(uniq("wm1"), [128, 16, 512], F32) as wm1, \
                nc.sbuf_tensor(uniq("sc"), [128, 32], F32) as sc:
            wbufs = [(wm0, Buf()), (wm1, Buf())]
            scb = Buf()
            P.dma("sp", sc[:, :], self.cT.ap[:, :], writes=[scb])
            P.op("act", lambda e: e.activation(sc[:, :], sc[:, :], AF.Silu), reads=[scb], writes=[scb])
            wsrc = self.w_mod.ap[l].rearrange("(kc p) n -> p kc n", p=128)
            ob, _ = PO["bmod"]
            for cb in range(36):
                wt, wb = wbufs[cb % 2]
                P.dma("sp", wt[:, :, :], wsrc[:, :, cb * 512:(cb + 1) * 512], writes=[wb])
                ps, pb = self.bank()
                for j in range(4):
                    for kc in range(16):
                        P.op("pe", lambda e: e.matmul(ps[:, j * 2:j * 2 + 2], wt[:, kc, j * 128:(j + 1) * 128],
                                                      sc[:, kc * 2:kc * 2 + 2], start=(kc == 0), stop=(kc == 15)),
                             reads=[wb, scb], writes=[pb], inc=(kc == 15))
                P.op("dve", lambda e: e.tensor_tensor(
                    self.modv[:, cb * 4:(cb + 1) * 4, :], ps[:, 0:8].rearrange("p (j v) -> p j v", v=2),
                    self.par[:, ob + cb * 4:ob + (cb + 1) * 4].unsqueeze(2).to_broadcast([128, 4, 2]), ALU.add),
                     reads=[pb, self.parb], writes=[self.modb])
            for k in range(3):
                a = self.modv[:, (3 * k + 1) * 16:(3 * k + 2) * 16, :]
                P.op("dve", lambda e: e.tensor_scalar_add(a, a, 1.0), reads=[self.modb], writes=[self.modb])
                g = self.modv[:, (3 * k + 2) * 16:(3 * k + 3) * 16, :]
                coef = (0.5 if k != 1 else 1.0) / ALPHA
                P.op("dve", lambda e: e.tensor_scalar_mul(g, g, coef), reads=[self.modb], writes=[self.modb])
            P.barrier()

    def phase_modapply(self, src, k):
        P, nc = self.P, self.nc
        with nc.sbuf_tensor(uniq("ma_x"), [128, 16, 512], F32) as xt, nc.sbuf_tensor(uniq("ma_u"), [128, 16, 512], BF16) as ut:
            xb, ub = Buf(), Buf()
            groups = [(0, 2)] + [(2 + 4 * i, 6 + 4 * i) for i in range(4)]
            for (c0, c1) in groups:
                n = (c1 - c0) * 128
                v = 1 if c0 == 0 else 0
                P.dma("sp", xt[:, :, :n], src.fm()[:, :, c0 * 128:c1 * 128], reads=src.bufs(c0, c1), writes=[xb])
                for fc in range(16):
                    P.op("dve", lambda e: e.tensor_scalar(ut[:, fc, :n], xt[:, fc, :n], self.mvec(3 * k + 1, fc, v),
                                                          self.mvec(3 * k, fc, v), ALU.mult, ALU.add),
                         reads=[xb, self.modb], writes=[ub])
                P.dma("sp", self.UT.fm()[:, :, c0 * 128:c1 * 128], ut[:, :, :n], reads=[ub], writes=self.UT.bufs(c0, c1))
            P.barrier()

    def phase_ffn(self, l, which, k, hsrc, blocks=((0, 9), (9, 18))):
        P, nc = self.P, self.nc
        W1 = self.ffn_w_in.ap[l, which].rearrange("(kc p) n -> p kc n", p=128)
        W2 = self.ffn_w_out.ap[l, which].rearrange("(kc p) n -> p kc n", p=128)
        with ExitStack() as es:
            sb = lambda n, s, dt=F32: es.enter_context(nc.sbuf_tensor(uniq(n), s, dt))
            U = sb("f_u", [128, 16, 1152], BF16); Ub = Buf()
            A = sb("f_a", [128, 44, 1152], BF16)
            WB = [(sb("f_w%d" % i, [128, 5632], BF16), Buf()) for i in range(3)]
            SG = [(sb("f_sg%d" % i, [128, 512], F32), Buf()) for i in range(2)]
            HB = [(sb("f_h%d" % i, [128, 1152], F32), Buf()) for i in range(2)]
            ZB = [(sb("f_z%d" % i, [128, 1152], F32), Buf()) for i in range(2)]
            wi = 0
            si = 0
            for (c0, c1) in blocks:
                nb = (c1 - c0) * 128
                t0 = c0 * 128
                subs = subs_for(c0, c1)
                Ab = [[Buf() for _ in subs] for _ in range(44)]
                P.dma("sp", U[:, :, :nb], self.UT.fm()[:, :, t0:t0 + nb], reads=self.UT.bufs(c0, c1), writes=[Ub])
                for j2 in range(22):
                    wa, wab = WB[wi % 3]; wi += 1
                    wg, wgb = WB[wi % 3]; wi += 1
                    wa3 = wa[:, 0:4096].rearrange("p (kc n) -> p kc n", n=256)
                    wg3 = wg[:, 0:4096].rearrange("p (kc n) -> p kc n", n=256)
                    P.dma("pool", wa3, W1[:, :, j2 * 256:(j2 + 1) * 256], writes=[wab])
                    P.dma("pool", wg3, W1[:, :, DFF + j2 * 256:DFF + (j2 + 1) * 256], writes=[wgb])
                    for jj in range(2):
                        j = j2 * 2 + jj
                        for si_, (o, n) in enumerate(subs):
                            psA, pbA = self.bank()
                            psG, pbG = self.bank()
                            for kc in range(16):
                                P.op("pe", lambda e: e.matmul(psA[:, :n], wa3[:, kc, jj * 128:(jj + 1) * 128],
                                                              U[:, kc, o:o + n], start=(kc == 0), stop=(kc == 15)),
                                     reads=[wab, Ub], writes=[pbA], inc=(kc == 15))
                            for kc in range(16):
                                P.op("pe", lambda e: e.matmul(psG[:, :n], wg3[:, kc, jj * 128:(jj + 1) * 128],
                                                              U[:, kc, o:o + n], start=(kc == 0), stop=(kc == 15)),
                                     reads=[wgb, Ub], writes=[pbG], inc=(kc == 15))
                            sg, sgb = SG[si % 2]; si += 1
                            P.op("act", lambda e: e.activation(sg[:, :n], psG[:, :n], AF.Silu), reads=[pbG], writes=[sgb])
                            P.op("dve", lambda e: e.tensor_tensor(A[:, j, o:o + n], sg[:, :n], psA[:, :n], ALU.mult),
                                 reads=[sgb, pbA], writes=[Ab[j][si_]])
                for fc in range(16):
                    w2, w2b = WB[wi % 3]; wi += 1
                    w23 = w2[:, :].rearrange("p (kc n) -> p kc n", n=128)
                    P.dma("pool", w23, W2[:, :, fc * 128:(fc + 1) * 128], writes=[w2b])
                    hb, hbb = HB[fc % 2]
                    zb, zbb = ZB[fc % 2]
                    P.dma("sp", hb[:, :nb], hsrc.ap[fc * 128:(fc + 1) * 128, t0:t0 + nb], reads=hsrc.bufs(c0, c1), writes=[hbb])
                    for si_, (o, n) in enumerate(subs):
                        ps, pb = self.bank()
                        for kc in range(44):
                            P.op("pe", lambda e: e.matmul(ps[:, :n], w23[:, kc, :], A[:, kc, o:o + n],
                                                          start=(kc == 0), stop=(kc == 43)),
                                 reads=[w2b, Ab[kc][si_]], writes=[pb], inc=(kc == 43))
                        v = 1 if (c0 == 0 and o == 0) else 0
                        P.op("dve", lambda e: e.scalar_tensor_tensor(zb[:, o:o + n], ps[:, :n], self.mvec(3 * k + 2, fc, v),
                                                                     hb[:, o:o + n], ALU.mult, ALU.add),
                             reads=[pb, hbb, self.modb], writes=[zbb])
                    P.dma("sp", self.ZT.ap[fc * 128:(fc + 1) * 128, t0:t0 + nb], zb[:, :nb], reads=[zbb],
                          writes=self.ZT.bufs(c0, c1))
            P.barrier()

    def phase_ln(self, lnk, nextk, final=False, skip_ctx=False):
        P, nc = self.P, self.nc
        eps = EPS / (ALPHA * ALPHA)
        with ExitStack() as es:
            sb = lambda n, s, dt=F32: es.enter_context(nc.sbuf_tensor(uniq(n), s, dt))
            Zs = [(sb("l_z%d" % i, [128, 16, 512]), Buf()) for i in range(2)]
            Hs = [(sb("l_h%d" % i, [128, 16, 512]), Buf()) for i in range(2)]
            Us = [(sb("l_u%d" % i, [128, 16, 512], BF16), Buf()) for i in range(2)]
            SQ = [(sb("l_sq%d" % i, [128, 512]), Buf()) for i in range(2)]
            mean = sb("l_mean", [128, 512]); meanb = Buf()
            var = sb("l_var", [128, 512]); varb = Buf()
            rstd = sb("l_rstd", [128, 512]); rstdb = Buf()
            nmr = sb("l_nmr", [128, 512]); nmrb = Buf()
            tmp = [(sb("l_t%d" % i, [128, 512]), Buf()) for i in range(2)]
            groups = [(0, 2)] + [(2 + 4 * i, 6 + 4 * i) for i in range(4)]
            glist = [g for g in groups if not ((final or skip_ctx) and g[0] == 0)]

            def ln_load(i):
                (a0, a1) = glist[i]
                zt_, zb_ = Zs[i % 2]
                P.dma("sp", zt_[:, :, :(a1 - a0) * 128], self.ZT.fm()[:, :, a0 * 128:a1 * 128], reads=self.ZT.bufs(a0, a1), writes=[zb_])

            ln_load(0)
            for gi_, (c0, c1) in enumerate(glist):
                if gi_ + 1 < len(glist):
                    ln_load(gi_ + 1)
                Z, Zb = Zs[gi_ % 2]; H, Hb = Hs[gi_ % 2]; Uo, Uob = Us[gi_ % 2]
                n = (c1 - c0) * 128
                v = 1 if c0 == 0 else 0
                ps1, pb1 = self.bank()
                ps2, pb2 = self.bank()
                for fc in range(16):
                    sq, sqb = SQ[fc % 2]
                    P.op("act", lambda e: e.activation(sq[:, :n], Z[:, fc, :n], AF.Square), reads=[Zb], writes=[sqb])
                    P.op("pe", lambda e: e.matmul(ps1[:, :n], self.ones[:, :], Z[:, fc, :n], start=(fc == 0), stop=(fc == 15)),
                         reads=[self.onesb, Zb], writes=[pb1], inc=(fc == 15))
                    P.op("pe", lambda e: e.matmul(ps2[:, :n], self.ones[:, :], sq[:, :n], start=(fc == 0), stop=(fc == 15)),
                         reads=[self.onesb, sqb], writes=[pb2], inc=True)
                P.op("act", lambda e: e.mul(mean[:, :n], ps1[:, :n], 1.0 / D), reads=[pb1], writes=[meanb])
                P.op("dve", lambda e: e.tensor_tensor(var[:, :n], mean[:, :n], mean[:, :n], ALU.mult), reads=[meanb], writes=[varb])
                P.op("dve", lambda e: e.scalar_tensor_tensor(var[:, :n], ps2[:, :n], 1.0 / D, var[:, :n], ALU.mult, ALU.subtract),
                     reads=[pb2, varb], writes=[varb])
                P.op("dve", lambda e: e.tensor_scalar_add(var[:, :n], var[:, :n], eps), reads=[varb], writes=[varb])
                P.op("act", lambda e: e.sqrt(var[:, :n], var[:, :n]), reads=[varb], writes=[varb])
                P.op("dve", lambda e: e.reciprocal(rstd[:, :n], var[:, :n]), reads=[varb], writes=[rstdb])
                P.op("dve", lambda e: e.scalar_tensor_tensor(nmr[:, :n], mean[:, :n], -1.0, rstd[:, :n], ALU.mult, ALU.mult),
                     reads=[meanb, rstdb], writes=[nmrb])
                og, _ = PO["lng"]
                obb, _ = PO["lnb"]
                for fc in range(16):
                    tt, ttb = tmp[fc % 2]
                    P.op("dve", lambda e: e.tensor_tensor(tt[:, :n], Z[:, fc, :n], rstd[:, :n], ALU.mult),
                         reads=[Zb, rstdb], writes=[ttb])
                    P.op("pool", lambda e: e.tensor_tensor(tt[:, :n], tt[:, :n], nmr[:, :n], ALU.add),
                         reads=[ttb, nmrb], writes=[ttb])
                    gcol = self.par[:, og + lnk * 16 + fc:og + lnk * 16 + fc + 1]
                    bcol = self.par[:, obb + lnk * 16 + fc:obb + lnk * 16 + fc + 1]
                    P.op("act", lambda e: e.activation(H[:, fc, :n], tt[:, :n], AF.Identity, bias=bcol, scale=gcol),
                         reads=[ttb, self.parb], writes=[Hb])
                    if nextk is not None:
                        P.op("dve", lambda e: e.tensor_scalar(Uo[:, fc, :n], H[:, fc, :n], self.mvec(3 * nextk + 1, fc, v),
                                                              self.mvec(3 * nextk, fc, v), ALU.mult, ALU.add),
                             reads=[Hb, self.modb], writes=[Uob])
                if final:
                    P.dma("sp", self.outT.fm()[:, :, (c0 - 2) * 128:(c1 - 2) * 128], H[:, :, :n], reads=[Hb],
                          writes=self.outT.bufs(c0, c1))
                else:
                    P.dma("act", self.HT.fm()[:, :, c0 * 128:c1 * 128], H[:, :, :n], reads=[Hb], writes=self.HT.bufs(c0, c1))
                if nextk is not None:
                    P.dma("act", self.UT.fm()[:, :, c0 * 128:c1 * 128], Uo[:, :, :n], reads=[Uob], writes=self.UT.bufs(c0, c1))
            P.barrier()

    def alloc_mixer_scratch(self):
        nc = self.nc
        mk = lambda n, s, dt=F32: DT(nc, n, s, dt)
        self.XBC = mk("XBC", [1536, T]); self.QKM = mk("QKM", [1024, T]); self.GQK = mk("GQK", [512, T])
        self.GP = mk("GP", [6144, T]); self.LR = mk("LR", [32, T]); self.TOK = mk("TOK", [T, 3120])
        self.XS_TOK = mk("XS_TOK", [T, 1024]); self.BCT = mk("BCT", [512, T], BF16); self.B_TOK = mk("B_TOK", [T, 256], BF16)
        self.QKC = mk("QKC", [1024, T], BF16); self.KM_TOK = mk("KM_TOK", [T, 512], BF16)
        self.FINT = mk("FINT", [D, T], BF16)
        self.DS = dict(ssd=mk("DS_ssd", [NCH, 2, 64, 1024]), ml=mk("DS_ml", [NCH, 2, 128, 512]),
                       mln=mk("DS_mln", [NCH, 2, 128, 4]), gla=mk("DS_gla", [NCH, 2, 128, 256]))
        self.DEC = dict(ssd=mk("DEC_ssd", [NCH, 2, 64, 16]), ml=mk("DEC_ml", [NCH, 2, 128, 4]),
                        gla=mk("DEC_gla", [NCH, 2, 128, 2]))
        self.ENT = dict(ssd=mk("ENT_ssd", [NCH, 2, 64, 1024], BF16), ml=mk("ENT_ml", [NCH, 2, 128, 512], BF16),
                        mln=mk("ENT_mln", [NCH, 2, 128, 4], BF16), gla=mk("ENT_gla", [NCH, 2, 128, 256], BF16))

    def phase_proj(self, l):
        P, nc = self.P, self.nc
        blocks = [(0, 9), (9, 18)]
        W = self.w_in.ap[l].rearrange("(kc p) n -> p kc n", p=128)
        fm_groups = [(self.XBC, OFF["xbc"], 1536), (self.QKM, OFF["mq"], 1024), (self.GQK, OFF["gq"], 512),
                     (self.GP, OFF["gate"], 6144), (self.LR, OFF["lr"], 32)]
        tm_groups = [(0, OFF["z"], 1024), (1024, OFF["dt"], 32), (1056, OFF["mv"], 1024), (2080, OFF["mg"], 16),
                     (2096, OFF["gv"], 1024)]
        with ExitStack() as es:
            sb = lambda n, s, dt=F32: es.enter_context(nc.sbuf_tensor(uniq(n), s, dt))
            U = sb("p_u", [128, 16, 1152], BF16); Ub = Buf()
            WB = [(sb("p_w%d" % i, [128, 16, 256], BF16), Buf()) for i in range(3)]
            OS = [(sb("p_o%d" % i, [128, 1152], F32), Buf()) for i in range(3)]
            TS = [(sb("p_t%d" % i, [128, 256], F32), Buf()) for i in range(3)]
            wi = 0; oi = 0; ti = 0
            for (c0, c1) in blocks:
                nb = (c1 - c0) * 128
                t0 = c0 * 128
                subs = subs_for(c0, c1)
                P.dma("sp", U[:, :, :nb], self.UT.fm()[:, :, t0:t0 + nb], reads=self.UT.bufs(c0, c1), writes=[Ub])
                for (dst, woff, ncol) in fm_groups:
                    for cb in range(0, ncol, 256):
                        w = min(256, ncol - cb)
                        wt, wb = WB[wi % 3]; wi += 1
                        P.dma("pool", wt[:, :, :w], W[:, :, woff + cb:woff + cb + w], writes=[wb])
                        for m0 in range(0, w, 128):
                            M = min(128, w - m0)
                            ot, ob = OS[oi % 3]; oi += 1
                            for (o, n) in subs:
                                ps, pb = self.bank()
                                for kc in range(16):
                                    P.op("pe", lambda e: e.matmul(ps[:M, :n], wt[:, kc, m0:m0 + M], U[:, kc, o:o + n],
                                                                  start=(kc == 0), stop=(kc == 15)),
                                         reads=[wb, Ub], writes=[pb], inc=(kc == 15))
                                P.op("act", lambda e: e.copy(ot[:M, o:o + n], ps[:M, :n]), reads=[pb], writes=[ob])
                            P.dma("sp", dst.ap[cb + m0:cb + m0 + M, t0:t0 + nb], ot[:M, :nb], reads=[ob],
                                  writes=dst.bufs(c0, c1))
                for (toff, woff, ncol) in tm_groups:
                    for cb in range(0, ncol, 256):
                        w = min(256, ncol - cb)
                        wt, wb = WB[wi % 3]; wi += 1
                        P.dma("pool", wt[:, :, :w], W[:, :, woff + cb:woff + cb + w], writes=[wb])
                        for c in range(c0, c1):
                            co = (c - c0) * 128
                            ps, pb = self.bank()
                            for kc in range(16):
                                P.op("pe", lambda e: e.matmul(ps[:, :w], U[:, kc, co:co + 128], wt[:, kc, :w],
                                                              start=(kc == 0), stop=(kc == 15)),
                                     reads=[wb, Ub], writes=[pb], inc=(kc == 15))
                            tt, tb = TS[ti % 3]; ti += 1
                            P.op("act", lambda e: e.copy(tt[:, :w], ps[:, :w]), reads=[pb], writes=[tb])
                            P.dma("sp", self.TOK.ap[c * 128:(c + 1) * 128, toff + cb:toff + cb + w], tt[:, :w], reads=[tb],
                                  writes=self.TOK.bufs(c, c + 1))
            P.barrier()

    def phase_conv(self, l):
        P, nc = self.P, self.nc
        with ExitStack() as es:
            sb = lambda n, s, dt=F32: es.enter_context(nc.sbuf_tensor(uniq(n), s, dt))
            X = [(sb("c_x%d" % i, [128, 10, 66]), Buf()) for i in range(2)]
            O = [(sb("c_o%d" % i, [128, 512]), Buf()) for i in range(2)]
            S = [(sb("c_s%d" % i, [128, 512]), Buf()) for i in range(2)]
            SH = [(sb("c_sh%d" % i, [128, 512], BF16), Buf()) for i in range(2)]
            TR = [(sb("c_tr%d" % i, [128, 128]), Buf()) for i in range(3)]
            TRH = [(sb("c_trh%d" % i, [128, 128], BF16), Buf()) for i in range(3)]
            it = 0; tri = 0
            for (src, nt, wname, bname, kind) in ((self.XBC, 12, "scw", "scb", "ssd"), (self.QKM, 8, "mcw", "mcb", "ml")):
                ow, _ = PO[wname]
                obn, _ = PO[bname]
                for ct in range(nt):
                    wcol = lambda tap: self.par[:, ow + ct * 9 + tap:ow + ct * 9 + tap + 1]
                    bcol = self.par[:, obn + ct:obn + ct + 1]
                    segs = [("ctx", 0)] + [("lat", rb) for rb in range(4)]
                    hrow0 = ct * 128 + (0 if kind == "ssd" else 1536)
                    for (sk, rb) in segs:
                        xt, xb = X[it % 2]; ot, ob = O[it % 2]; st, stb = S[it % 2]; sh, shb = SH[it % 2]; it += 1
                        P.op("pool", lambda e: e.memset(xt[:, :, :], 0.0), writes=[xb])
                        if sk == "ctx":
                            n = 256; c0 = 0; c1 = 2
                            xf = xt[:, :, :].rearrange("p a b -> p (a b)")
                            P.dma("sp", xf[:, 1:257], src.ap[ct * 128:(ct + 1) * 128, 0:256], reads=src.bufs(0, 2), writes=[xb])
                            taps = [(3, xf[:, 0:256]), (4, xf[:, 1:257]), (5, xf[:, 2:258])]
                            oview = ot[:, :n]
                        else:
                            n = 512; c0 = 2 + rb * 4; c1 = c0 + 4
                            r0 = rb * 8
                            ra = max(r0 - 1, 0); rz = min(r0 + 9, 32)
                            cc0 = 2 + (ra * 64) // 128; cc1 = 2 + (rz * 64 + 127) // 128
                            srcv = src.ap[ct * 128:(ct + 1) * 128, 256 + ra * 64:256 + rz * 64].rearrange("p (r c) -> p r c", c=64)
                            d0 = ra - (r0 - 1)
                            P.dma("sp", xt[:, d0:d0 + (rz - ra), 1:65], srcv, reads=src.bufs(cc0, cc1), writes=[xb])
                            if rb == 3:
                                P.dma("sp", xt[:, 9, 1:65], self.HALO_REV.ap[hrow0:hrow0 + 128, :], reads=self.HALO_REV.bufs(0, 1) + [xb], writes=[xb])
                            taps = [(i * 3 + j, xt[:, i:i + 8, j:j + 64]) for i in range(3) for j in range(3)]
                            oview = ot[:, :].rearrange("p (r c) -> p r c", c=64)
                        for ti_, (tap, xv) in enumerate(taps):
                            if ti_ == 0:
                                P.op("dve", lambda e: e.tensor_scalar(oview, xv, wcol(tap), bcol, ALU.mult, ALU.add),
                                     reads=[xb, self.parb], writes=[ob])
                            else:
                                P.op("dve", lambda e: e.scalar_tensor_tensor(oview, xv, wcol(tap), oview, ALU.mult, ALU.add),
                                     reads=[xb, ob, self.parb], writes=[ob])
                        P.op("act", lambda e: e.activation(st[:, :n], ot[:, :n], AF.Silu), reads=[ob], writes=[stb])
                        t0 = c0 * 128
                        fm_dst = None
                        if kind == "ssd" and ct >= 8:
                            fm_dst = (self.BCT, (ct - 8) * 128, 1.0)
                        if kind == "ml":
                            fm_dst = (self.QKC, ct * 128, (128.0 ** -0.5) if ct < 4 else 1.0)
                        if fm_dst is not None:
                            dstt, roff, scl = fm_dst
                            P.op("pool", lambda e: e.tensor_scalar_mul(sh[:, :n], st[:, :n], scl), reads=[stb], writes=[shb])
                            P.dma("act", dstt.ap[roff:roff + 128, t0:t0 + n], sh[:, :n], reads=[shb], writes=dstt.bufs(c0, c1))
                        tm_dst = None
                        if kind == "ssd" and ct < 8:
                            tm_dst = (self.XS_TOK, ct * 128, False)
                        if kind == "ssd" and ct in (8, 9):
                            tm_dst = (self.B_TOK, (ct - 8) * 128, True)
                        if kind == "ml" and ct >= 4:
                            tm_dst = (self.KM_TOK, (ct - 4) * 128, True)
                        if tm_dst is not None:
                            dstt, coff, half = tm_dst
                            for cc in range(n // 128):
                                ps, pb = self.bank()
                                P.op("pe", lambda e: e.transpose(ps[:, 0:128], st[:, cc * 128:(cc + 1) * 128], self.ident[:, :]),
                                     reads=[stb, self.identb], writes=[pb])
                                if half:
                                    tr, trb = TRH[tri % 3]
                                else:
                                    tr, trb = TR[tri % 3]
                                tri += 1
                                P.op("act", lambda e: e.copy(tr[:, :], ps[:, 0:128]), reads=[pb], writes=[trb])
                                c = c0 + cc
                                P.dma("act", dstt.ap[c * 128:(c + 1) * 128, coff:coff + 128], tr[:, :], reads=[trb],
                                      writes=dstt.bufs(c, c + 1))
            P.barrier()

    def scan_pass(self, l, mode, chunks):
        P, nc = self.P, self.nc
        AX = mybir.AxisListType.X
        with ExitStack() as es:
            sb = lambda n, s, dt=F32: es.enter_context(nc.sbuf_tensor(uniq(n), s, dt))
            cb_ = Buf()
            cst = {}
            for n_ in ("trif", "trib", "mbf", "mbb", "m01f", "m01b", "self", "selb"):
                cst[n_] = sb("k_" + n_, [128, 128])
                o, w = CO[n_]
                P.dma("sp", cst[n_][:, :], self.CONST.ap[:, o:o + w], writes=[cb_])
            esel = sb("k_esel", [16, 16, 128])
            o, w = CO["esel"]
            P.dma("sp", esel[:, :, :], self.CONST.ap[0:16, o:o + w].rearrange("p (h m) -> p h m", m=128), writes=[cb_])
            row = sb("k_row", [128, NROW])
            P.dma("sp", row[:, :], self.ROW.ap[l], writes=[cb_])
            gw2 = sb("k_gw2", [33, 512])
            P.dma("sp", gw2[:, :], self.GW2.ap[l], writes=[cb_])
            aneg = sb("k_aneg", [128, 32])
            oa, _ = RO["alog"]
            P.op("act", lambda e: e.activation(aneg[:, :], row[:, oa:oa + 32], AF.Exp), reads=[cb_], writes=[cb_])
            P.op("dve", lambda e: e.tensor_scalar_mul(aneg[:, :], aneg[:, :], -1.0), reads=[cb_], writes=[cb_])
            onesh = sb("k_onesh", [128, 1], BF16)
            P.op("dve", lambda e: e.memset(onesh[:, :], 1.0), writes=[cb_])
            one1 = self.ones[:, 0:1]
            TRI = [cst["trif"], cst["trib"]]; MB = [cst["mbf"], cst["mbb"]]; M01 = [cst["m01f"], cst["m01b"]]
            MBH4 = [sb("k_mbh%d" % i, [128, 512], BF16) for i in range(2)]
            for i_ in range(2):
                P.op("dve", lambda e: e.tensor_copy(MBH4[i_][:, :].rearrange("p (h m) -> p h m", m=128),
                                                    MB[i_][:, :].unsqueeze(1).to_broadcast([128, 4, 128])), reads=[cb_], writes=[cb_])
            ones16 = sb("k_ones16", [16, 128])
            P.op("dve", lambda e: e.memset(ones16[:, :], 1.0), writes=[cb_])
            RM = [(sb("k_rm%d" % i, [16, 4, 128]), Buf()) for i in range(2)]
            SEL = [cst["self"], cst["selb"]]
            ENDC = [127, 0]
            rowv = lambda n_: row[:, RO[n_][0]:RO[n_][0] + RO[n_][1]]
            INS = []
            for i_ in range(2):
                d_ = dict(TK=sb("k_tk%d" % i_, [128, 3120]), XS=sb("k_xs%d" % i_, [128, 1024]), BTK=sb("k_btk%d" % i_, [128, 256], BF16),
                          KMT=sb("k_kmt%d" % i_, [128, 512], BF16), BCT=sb("k_bct%d" % i_, [128, 4, 128], BF16),
                          QKC=sb("k_qkc%d" % i_, [128, 8, 128], BF16), GQ=sb("k_gq%d" % i_, [128, 4, 128]), LRA=sb("k_lra%d" % i_, [33, 128]))
                for n_ in list(d_.keys()):
                    d_[n_ + "b"] = Buf()
                P.op("dve", lambda e: e.memset(d_["LRA"][:, :], 1.0), writes=[d_["LRAb"]])
                INS.append(d_)

            def load_inputs(c_, d_):
                a0, a1 = c_ * 128, (c_ + 1) * 128
                P.dma("sp", d_["TK"][:, :], self.TOK.ap[a0:a1, :], reads=self.TOK.bufs(c_, c_ + 1), writes=[d_["TKb"]])
                P.dma("sp", d_["XS"][:, :], self.XS_TOK.ap[a0:a1, :], reads=self.XS_TOK.bufs(c_, c_ + 1), writes=[d_["XSb"]])
                P.dma("sp", d_["BTK"][:, :], self.B_TOK.ap[a0:a1, :], reads=self.B_TOK.bufs(c_, c_ + 1), writes=[d_["BTKb"]])
                P.dma("sp", d_["KMT"][:, :], self.KM_TOK.ap[a0:a1, :], reads=self.KM_TOK.bufs(c_, c_ + 1), writes=[d_["KMTb"]])
                P.dma("sp", d_["BCT"][:, :, :], self.BCT.fm()[:, :, a0:a1], reads=self.BCT.bufs(c_, c_ + 1), writes=[d_["BCTb"]])
                P.dma("sp", d_["QKC"][:, :, :], self.QKC.fm()[:, :, a0:a1], reads=self.QKC.bufs(c_, c_ + 1), writes=[d_["QKCb"]])
                P.dma("sp", d_["GQ"][:, :, :], self.GQK.fm()[:, :, a0:a1], reads=self.GQK.bufs(c_, c_ + 1), writes=[d_["GQb"]])
                P.dma("sp", d_["LRA"][0:32, :], self.LR.ap[:, a0:a1], reads=self.LR.bufs(c_, c_ + 1), writes=[d_["LRAb"]])
            dt = sb("k_dt", [128, 32]); dtb_ = Buf()
            lgs = sb("k_lgs", [128, 32]); lgsb = Buf()
            lgm = sb("k_lgm", [128, 8]); lgmb = Buf()
            igm = sb("k_igm", [128, 8]); igmb = Buf()
            sd = {}
            for nm, H in (("s", 16), ("m", 4)):
                sd[nm] = dict(H=H, nlg=sb("k_nlg" + nm, [128, 2 * H]), btok=sb("k_btok" + nm, [128, 2 * H]),
                              bT=sb("k_bT" + nm, [16, 2, 128]), nbT=sb("k_nbT" + nm, [16, 2, 128]),
                              cf=sb("k_cf" + nm, [128, 2 * H]), eb=sb("k_eb" + nm, [128, 2 * H]), nbk=sb("k_nbk" + nm, [128, 2 * H]),
                              ebend=sb("k_ebend" + nm, [128, 2 * H]), tmp=sb("k_tmp" + nm, [128, 2 * H]), b=Buf())
            big1 = sb("k_big1", [128, 1024]); big1b = Buf()
            VP = sb("k_vp", [128, 1024], BF16); VPb = Buf()
            KQ = sb("k_kq", [128, 4, 128]); KQb = Buf()
            DECt = [(sb("k_dec%d" % i, [128, 512]), Buf()) for i in range(2)]
            PT = [(sb("k_pt%d" % i, [128, 4, 128], BF16), Buf()) for i in range(2)]
            ST = sb("k_st", [128, 512], BF16); STb = Buf()
            STN = sb("k_stn", [128, 4], BF16); STNb = Buf()
            YI = sb("k_yi", [128, 2, 1024]); YIb = Buf()
            FIN = sb("k_fin", [128, 2048]); FINb = Buf()
            FT = sb("k_ft", [128, 16, 128], BF16); FTb = Buf()
            SM = sb("k_sm", [128, 64]); SMb = Buf()
            DSt = sb("k_dst", [128, 1024]); DStb = Buf()
            MVH = sb("k_mvh", [128, 512], BF16); MVHb = Buf()
            GVH = sb("k_gvh", [128, 512], BF16); GVHb = Buf()
            LA = sb("k_la", [128, 512]); LAb = Buf()
            EBg = sb("k_ebg", [128, 2, 2, 128]); EBgb = Buf()
            ENg = sb("k_eng", [128, 2, 2, 128]); ENgb = Buf()
            QTg = sb("k_qtg", [128, 2, 2, 128], BF16); QTgb = Buf()
            KTg = sb("k_ktg", [128, 2, 2, 128], BF16); KTgb = Buf()
            KTf = sb("k_ktf", [128, 2, 128]); KTfb = Buf()
            KTK = sb("k_ktk", [128, 2, 128], BF16); KTKb = Buf()
            AT = sb("k_at", [128, 4, 128], BF16); ATb = Buf()
            CFH = sb("k_cfh", [128, 8], BF16); CFHb = Buf()
            B = self.banks
            qi = [0]

            def mm(bank, out, lhsT, rhs, start, stop, reads, inc=None):
                P.op("pe", lambda e: e.matmul(out, lhsT, rhs, start=start, stop=stop), reads=reads, writes=[bank[1]],
                     inc=(stop if inc is None else inc))

            def sd_prep(nm, lg, lgb, ig, igb):
                t = sd[nm]; H = t["H"]; tb = t["b"]
                for d in range(2):
                    sl = slice(d * H, (d + 1) * H)
                    mm(B[6], B[6][0][:, d * H:(d + 1) * H], TRI[d][:, :], lg[:, sl], True, True, [cb_, lgb])
                    mm(B[7], B[7][0][:H, d * 128:(d + 1) * 128], lg[:, sl], TRI[d][:, :], True, True, [cb_, lgb])
                P.op("act", lambda e: e.copy(t["btok"][:, :], B[6][0][:, 0:2 * H]), reads=[B[6][1]], writes=[tb])
                P.op("act", lambda e: e.copy(t["bT"][:H, :, :], B[7][0][:H, 0:256].rearrange("p (d m) -> p d m", m=128)),
                     reads=[B[7][1]], writes=[tb])
                P.op("dve", lambda e: e.tensor_scalar_mul(t["nbk"][:, :], t["btok"][:, :], -1.0), reads=[tb], writes=[tb])
                if ig is not None:
                    P.op("dve", lambda e: e.tensor_tensor(t["nbk"][:, :], t["nbk"][:, :], ig, ALU.add), reads=[tb, igb], writes=[tb])
                for d in range(2):
                    mm(B[6], B[6][0][:, 32 + d * H:32 + (d + 1) * H], SEL[d][:, :], t["btok"][:, d * H:(d + 1) * H], True, True, [cb_, tb])
                P.op("dve", lambda e: e.tensor_tensor(t["tmp"][:, :], B[6][0][:, 32:32 + 2 * H], t["nbk"][:, :], ALU.add),
                     reads=[B[6][1], tb], writes=[tb])
                P.op("act", lambda e: e.activation(t["cf"][:, :], t["tmp"][:, :], AF.Exp), reads=[tb], writes=[tb])
                P.op("act", lambda e: e.activation(t["eb"][:, :], t["btok"][:, :], AF.Exp), reads=[tb], writes=[tb])
                P.op("act", lambda e: e.activation(t["ebend"][:, :], B[6][0][:, 32:32 + 2 * H], AF.Exp), reads=[B[6][1]], writes=[tb])

            def decay_quad(nm, d, heads):
                t = sd[nm]; H = t["H"]
                bk = B[1 + (qi[0] % 2)]
                dec, decb = DECt[qi[0] % 2]
                h0 = heads[0]
                rm, rmb = RM[qi[0] % 2]
                P.op("dve", lambda e: e.tensor_tensor(rm[:H, :, :], esel[:H, h0:h0 + 4, :],
                                                      t["bT"][:H, d, :].unsqueeze(1).to_broadcast([H, 4, 128]), ALU.mult),
                     reads=[cb_, t["b"]], writes=[rmb])
                mm(bk, bk[0][:, :], self.identh[:, :], MBH4[d][:, :], True, False, [cb_, self.identhb])
                mm(bk, bk[0][:, :], ones16[:H, :], rm[:H, :, :].rearrange("p h m -> p (h m)"), False, True, [cb_, rmb])
                qi[0] += 1
                for hh, h in enumerate(heads):
                    P.op("act", lambda e: e.activation(dec[:, hh * 128:(hh + 1) * 128], bk[0][:, hh * 128:(hh + 1) * 128], AF.Exp,
                                                       bias=t["nbk"][:, d * H + h:d * H + h + 1]), reads=[bk[1], t["b"]], writes=[decb])
                return dec, decb

            def gnorm(x, xb, G, Wd, wrow, center):
                xv = x.rearrange("p (g w) -> p g w", w=Wd)
                if center:
                    P.op("dve", lambda e: e.tensor_reduce(SM[:, 0:G], xv, AX, ALU.add), reads=[xb], writes=[SMb])
                    P.op("dve", lambda e: e.tensor_scalar_mul(SM[:, 0:G], SM[:, 0:G], 1.0 / Wd), reads=[SMb], writes=[SMb])
                    P.op("dve", lambda e: e.tensor_tensor(xv, xv, SM[:, 0:G].unsqueeze(2).to_broadcast([128, G, Wd]), ALU.subtract),
                         reads=[xb, SMb], writes=[xb])
                sq = big1[:, 0:G * Wd]
                P.op("dve", lambda e: e.tensor_tensor(sq, x, x, ALU.mult), reads=[xb], writes=[big1b])
                P.op("dve", lambda e: e.tensor_reduce(SM[:, 8:8 + G], sq.rearrange("p (g w) -> p g w", w=Wd), AX, ALU.add),
                     reads=[big1b], writes=[SMb])
                P.op("dve", lambda e: e.tensor_scalar(SM[:, 8:8 + G], SM[:, 8:8 + G], 1.0 / Wd, EPS, ALU.mult, ALU.add),
                     reads=[SMb], writes=[SMb])
                P.op("act", lambda e: e.activation(SM[:, 8:8 + G], SM[:, 8:8 + G], AF.Ln), reads=[SMb], writes=[SMb])
                P.op("act", lambda e: e.activation(SM[:, 8:8 + G], SM[:, 8:8 + G], AF.Exp, scale=-0.5), reads=[SMb], writes=[SMb])
                P.op("dve", lambda e: e.tensor_tensor(xv, xv, SM[:, 8:8 + G].unsqueeze(2).to_broadcast([128, G, Wd]), ALU.mult),
                     reads=[xb, SMb], writes=[xb])
                P.op("dve", lambda e: e.tensor_tensor(x, x, wrow, ALU.mult), reads=[xb, cb_], writes=[xb])

            def sig_exp(dst, src, srcb, dstb):
                P.op("act", lambda e: e.activation(dst, src, AF.Exp, scale=-1.0), reads=[srcb], writes=[dstb])
                P.op("dve", lambda e: e.tensor_scalar_add(dst, dst, 1.0), reads=[dstb], writes=[dstb])
                P.op("dve", lambda e: e.reciprocal(dst, dst), reads=[dstb], writes=[dstb])

            if chunks:
                load_inputs(chunks[0], INS[0])
            for ci_, c in enumerate(chunks):
                r0, r1 = c * 128, (c + 1) * 128
                if ci_ + 1 < len(chunks):
                    load_inputs(chunks[ci_ + 1], INS[(ci_ + 1) % 2])
                d_ = INS[ci_ % 2]
                TK, TKb, XS, XSb, BTK, BTKb, KMT, KMTb = d_["TK"], d_["TKb"], d_["XS"], d_["XSb"], d_["BTK"], d_["BTKb"], d_["KMT"], d_["KMTb"]
                BCT, BCTb, QKC, QKCb, GQ, GQb, lra, lrb = d_["BCT"], d_["BCTb"], d_["QKC"], d_["QKCb"], d_["GQ"], d_["GQb"], d_["LRA"], d_["LRAb"]
                zraw = TK[:, 0:1024]; dtraw = TK[:, 1024:1056]; mv = TK[:, 1056:1568]; mo = TK[:, 1568:2080]
                mgr = TK[:, 2080:2096]; gv = TK[:, 2096:2608]; gr = TK[:, 2608:3120]
                P.op("dve", lambda e: e.tensor_tensor(dt[:, :], dtraw, rowv("dtb"), ALU.add), reads=[TKb, cb_], writes=[dtb_])
                P.op("act", lambda e: e.activation(dt[:, :], dt[:, :], AF.Exp), reads=[dtb_], writes=[dtb_])
                P.op("act", lambda e: e.activation(dt[:, :], dt[:, :], AF.Ln, bias=one1), reads=[dtb_, self.onesb], writes=[dtb_])
                P.op("dve", lambda e: e.tensor_tensor(lgs[:, :], dt[:, :], aneg[:, :], ALU.mult), reads=[dtb_, cb_], writes=[lgsb])
                sd_prep("s", lgs[:, :], lgsb, None, None)
                ts = sd["s"]
                xs3 = XS[:, :].rearrange("p (h w) -> p h w", w=64)
                vp3 = VP[:, :].rearrange("p (h w) -> p h w", w=64)
                if mode == "S":
                    for d in range(2):
                        P.op("dve", lambda e: e.tensor_tensor(ts["tmp"][:, d * 16:(d + 1) * 16], ts["cf"][:, d * 16:(d + 1) * 16],
                                                              dt[:, d * 16:(d + 1) * 16], ALU.mult), reads=[ts["b"], dtb_], writes=[ts["b"]])
                        P.op("dve", lambda e: e.tensor_tensor(vp3, xs3, ts["tmp"][:, d * 16:(d + 1) * 16].unsqueeze(2).to_broadcast([128, 16, 64]),
                                                              ALU.mult), reads=[XSb, ts["b"]], writes=[VPb])
                        for g in range(4):
                            bk = B[3 + g // 2]
                            mm(bk, bk[0][0:64, (g % 2) * 256:(g % 2) * 256 + 256], BTK[:, g * 64:(g + 1) * 64],
                               VP[:, g * 256:(g + 1) * 256], True, True, [BTKb, VPb])
                        P.op("act", lambda e: e.copy(DSt[0:64, 0:512], B[3][0][0:64, :]), reads=[B[3][1]], writes=[DStb])
                        P.op("act", lambda e: e.copy(DSt[0:64, 512:1024], B[4][0][0:64, :]), reads=[B[4][1]], writes=[DStb])
                        P.dma("act", self.DS["ssd"].ap[c, d], DSt[0:64, :], reads=[DStb], writes=self.DS["ssd"].bufs(c, c + 1))
                        P.dma("act", self.DEC["ssd"].ap[c, d], ts["ebend"][0:64, d * 16:(d + 1) * 16], reads=[ts["b"]],
                              writes=self.DEC["ssd"].bufs(c, c + 1))
                elif "s" in getattr(self, "obr", "smg"):
                    for g in range(4):
                        pb_ = (g % 2) * 64
                        bk_ = B[0] if g % 2 == 0 else B[5]
                        mm(bk_, bk_[0][:, (g // 2) * 128:(g // 2 + 1) * 128], BCT[pb_:pb_ + 64, g // 2, :], BCT[pb_:pb_ + 64, 2 + g // 2, :],
                           True, True, [BCTb])
                    for e_ in range(2):
                        bk_ = B[0] if e_ == 0 else B[5]
                        for t_ in range(2):
                            P.op("act", lambda e: e.copy(KQ[:, t_ * 2 + e_, :], bk_[0][:, t_ * 128:(t_ + 1) * 128]), reads=[bk_[1]], writes=[KQb])
                    LV = getattr(self, "olvl", 9)
                    for d in range(2 if LV >= 2 else 0):
                        P.op("dve", lambda e: e.tensor_tensor(vp3, xs3, dt[:, d * 16:(d + 1) * 16].unsqueeze(2).to_broadcast([128, 16, 64]),
                                                              ALU.mult), reads=[XSb, dtb_], writes=[VPb])
                        for g in range(4):
                            dec, decb = decay_quad("s", d, [g * 4 + i for i in range(4)])
                            pt, ptb = PT[g % 2]
                            P.op("dve", lambda e: e.tensor_tensor(pt[:, :, :], dec[:, :].rearrange("p (h m) -> p h m", m=128),
                                                                  KQ[:, g, :].unsqueeze(1).to_broadcast([128, 4, 128]), ALU.mult),
                                 reads=[decb, KQb], writes=[ptb])
                            for hh in range(4):
                                h = g * 4 + hh
                                bk = B[3 + h // 8]
                                mm(bk, bk[0][:, (h % 8) * 64:(h % 8) * 64 + 64], pt[:, hh, :], VP[:, h * 64:(h + 1) * 64],
                                   True, True, [ptb, VPb], inc=True)
                        if LV < 3:
                            continue
                        for e_ in range(2):
                            srcv = self.ENT["ssd"].ap[c, d].rearrange("n (t e w) -> n t e w", e=2, w=256)[:, :, e_, :]
                            P.dma("sp", ST[e_ * 64:(e_ + 1) * 64, :].rearrange("n (t w) -> n t w", w=256), srcv,
                                  reads=self.ENT["ssd"].bufs(c, c + 1), writes=[STb])
                        for g in range(4):
                            pb_ = (g % 2) * 64
                            bk_ = B[5] if g % 2 == 0 else B[0]
                            mm(bk_, bk_[0][:, (g // 2) * 256:(g // 2 + 1) * 256], BCT[pb_:pb_ + 64, 2 + g // 2, :],
                               ST[pb_:pb_ + 64, (g // 2) * 256:(g // 2) * 256 + 256], True, True, [BCTb, STb])
                        for g in range(4):
                            bk_ = B[5] if g % 2 == 0 else B[0]
                            P.op("dve", lambda e: e.tensor_tensor(
                                YI[:, d, g * 256:(g + 1) * 256].rearrange("p (h w) -> p h w", w=64),
                                bk_[0][:, (g // 2) * 256:(g // 2 + 1) * 256].rearrange("p (h w) -> p h w", w=64),
                                ts["eb"][:, d * 16 + g * 4:d * 16 + g * 4 + 4].unsqueeze(2).to_broadcast([128, 4, 64]), ALU.mult),
                                 reads=[bk_[1], ts["b"]], writes=[YIb])
                        for half in range(2):
                            P.op("dve", lambda e: e.tensor_tensor(YI[:, d, half * 512:(half + 1) * 512], YI[:, d, half * 512:(half + 1) * 512],
                                                                  B[3 + half][0][:, :], ALU.add), reads=[YIb, B[3 + half][1]], writes=[YIb])
                    fs = FIN[:, 0:1024]
                    if LV < 4:
                        continue
                    P.op("dve", lambda e: e.tensor_tensor(fs, YI[:, 0, :], YI[:, 1, :], ALU.add), reads=[YIb], writes=[FINb])
                    od, _ = RO["dsk"]
                    P.op("dve", lambda e: e.tensor_tensor(big1[:, :].rearrange("p (h w) -> p h w", w=64), xs3,
                                                          row[:, od:od + 16].unsqueeze(2).to_broadcast([128, 16, 64]), ALU.mult),
                         reads=[XSb, cb_], writes=[big1b])
                    P.op("dve", lambda e: e.tensor_tensor(fs, fs, big1[:, :], ALU.add), reads=[FINb, big1b], writes=[FINb])
                    sig_exp(big1[:, :], zraw, TKb, big1b)
                    P.op("dve", lambda e: e.tensor_tensor(fs, fs, zraw, ALU.mult), reads=[FINb, TKb], writes=[FINb])
                    P.op("dve", lambda e: e.tensor_tensor(fs, fs, big1[:, :], ALU.mult), reads=[FINb, big1b], writes=[FINb])
                    gnorm(fs, FINb, 4, 256, rowv("snw"), False)
                og, _ = RO["mgb"]
                mg3 = big1[:, 0:16]
                P.op("dve", lambda e: e.tensor_tensor(mg3, mgr, row[:, og:og + 16], ALU.add), reads=[TKb, cb_], writes=[big1b])
                mgv = mg3.rearrange("p (d t h) -> p d t h", d=2, t=2)
                P.op("dve", lambda e: e.tensor_copy(igm[:, :].rearrange("p (d h) -> p d h", d=2), mgv[:, :, 0, :]), reads=[big1b], writes=[igmb])
                P.op("act", lambda e: e.activation(lgm[:, :].rearrange("p (d h) -> p d h", d=2), mgv[:, :, 1, :], AF.Exp, scale=-1.0),
                     reads=[big1b], writes=[lgmb])
                P.op("act", lambda e: e.activation(lgm[:, :], lgm[:, :], AF.Ln, bias=one1), reads=[lgmb, self.onesb], writes=[lgmb])
                P.op("dve", lambda e: e.tensor_scalar_mul(lgm[:, :], lgm[:, :], -1.0), reads=[lgmb], writes=[lgmb])
                sd_prep("m", lgm[:, :], lgmb, igm[:, :], igmb)
                tm = sd["m"]
                mv3 = mv.rearrange("p (h w) -> p h w", w=128)
                vpm = VP[:, 0:512].rearrange("p (h w) -> p h w", w=128)
                if mode == "S":
                    P.op("dve", lambda e: e.tensor_copy(CFH[:, :], tm["cf"][:, :]), reads=[tm["b"]], writes=[CFHb])
                    for d in range(2):
                        P.op("dve", lambda e: e.tensor_tensor(vpm, mv3, tm["cf"][:, d * 4:(d + 1) * 4].unsqueeze(2).to_broadcast([128, 4, 128]),
                                                              ALU.mult), reads=[TKb, tm["b"]], writes=[VPb])
                        for h in range(4):
                            mm(B[3], B[3][0][:, h * 128:(h + 1) * 128], KMT[:, h * 128:(h + 1) * 128], VP[:, h * 128:(h + 1) * 128],
                               True, True, [KMTb, VPb])
                            mm(B[6], B[6][0][:, 96 + h:97 + h], KMT[:, h * 128:(h + 1) * 128], CFH[:, d * 4 + h:d * 4 + h + 1],
                               True, True, [KMTb, CFHb])
                        P.op("act", lambda e: e.copy(DSt[:, 0:512], B[3][0][:, :]), reads=[B[3][1]], writes=[DStb])
                        P.op("act", lambda e: e.copy(DSt[:, 512:516], B[6][0][:, 96:100]), reads=[B[6][1]], writes=[DStb])
                        P.dma("act", self.DS["ml"].ap[c, d], DSt[:, 0:512], reads=[DStb], writes=self.DS["ml"].bufs(c, c + 1))
                        P.dma("act", self.DS["mln"].ap[c, d], DSt[:, 512:516], reads=[DStb], writes=self.DS["mln"].bufs(c, c + 1))
                        P.dma("act", self.DEC["ml"].ap[c, d], tm["ebend"][:, d * 4:(d + 1) * 4], reads=[tm["b"]],
                              writes=self.DEC["ml"].bufs(c, c + 1))
                elif "m" in getattr(self, "obr", "smg"):
                    P.op("dve", lambda e: e.tensor_copy(MVH[:, :], mv), reads=[TKb], writes=[MVHb])
                    for h in range(4):
                        mm(B[0], B[0][0][:, h * 128:(h + 1) * 128], QKC[:, 4 + h, :], QKC[:, h, :], True, True, [QKCb])
                    P.op("act", lambda e: e.copy(KQ[:, :, :], B[0][0][:, :].rearrange("p (g m) -> p g m", m=128)),
                         reads=[B[0][1]], writes=[KQb])
                    fm_ = FIN[:, 1024:1536]
                    for d in range(2):
                        dec, decb = decay_quad("m", d, [0, 1, 2, 3])
                        pt, ptb = PT[d % 2]
                        P.op("dve", lambda e: e.tensor_tensor(pt[:, :, :], dec[:, :].rearrange("p (h m) -> p h m", m=128), KQ[:, :, :], ALU.mult),
                             reads=[decb, KQb], writes=[ptb])
                        P.dma("sp", ST[:, :], self.ENT["ml"].ap[c, d], reads=self.ENT["ml"].bufs(c, c + 1), writes=[STb])
                        P.dma("sp", STN[:, :], self.ENT["mln"].ap[c, d], reads=self.ENT["mln"].bufs(c, c + 1), writes=[STNb])
                        for h in range(4):
                            mm(B[3 + d], B[3 + d][0][:, h * 128:(h + 1) * 128], pt[:, h, :], MVH[:, h * 128:(h + 1) * 128], True, True, [ptb, MVHb])
                            mm(B[6], B[6][0][:, 64 + d * 4 + h:65 + d * 4 + h], pt[:, h, :], onesh[:, :], True, True, [ptb, cb_])
                            mm(B[5], B[5][0][:, h * 128:(h + 1) * 128], QKC[:, h, :], ST[:, h * 128:(h + 1) * 128], True, True, [QKCb, STb])
                            mm(B[6], B[6][0][:, 80 + d * 4 + h:81 + d * 4 + h], QKC[:, h, :], STN[:, h:h + 1], True, True, [QKCb, STNb])
                        ebd = tm["eb"][:, d * 4:(d + 1) * 4]
                        num = YI[:, d, 0:512]
                        P.op("dve", lambda e: e.tensor_tensor(num.rearrange("p (h w) -> p h w", w=128),
                                                              B[5][0][:, :].rearrange("p (h w) -> p h w", w=128),
                                                              ebd.unsqueeze(2).to_broadcast([128, 4, 128]), ALU.mult),
                             reads=[B[5][1], tm["b"]], writes=[YIb])
                        P.op("dve", lambda e: e.tensor_tensor(num, num, B[3 + d][0][:, :], ALU.add), reads=[YIb, B[3 + d][1]], writes=[YIb])
                        den = SM[:, 16 + d * 4:20 + d * 4]
                        P.op("dve", lambda e: e.tensor_tensor(den, B[6][0][:, 80 + d * 4:84 + d * 4], ebd, ALU.mult),
                             reads=[B[6][1], tm["b"]], writes=[SMb])
                        P.op("dve", lambda e: e.tensor_tensor(den, den, B[6][0][:, 64 + d * 4:68 + d * 4], ALU.add),
                             reads=[SMb, B[6][1]], writes=[SMb])
                        nden = SM[:, 24 + d * 4:28 + d * 4]
                        P.op("dve", lambda e: e.tensor_scalar_mul(nden, den, -1.0), reads=[SMb], writes=[SMb])
                        P.op("dve", lambda e: e.tensor_tensor(den, den, nden, ALU.max), reads=[SMb], writes=[SMb])
                        P.op("dve", lambda e: e.tensor_scalar_max(den, den, 1.0), reads=[SMb], writes=[SMb])
                        P.op("dve", lambda e: e.reciprocal(den, den), reads=[SMb], writes=[SMb])
                        P.op("dve", lambda e: e.tensor_tensor(num.rearrange("p (h w) -> p h w", w=128), num.rearrange("p (h w) -> p h w", w=128),
                                                              den.unsqueeze(2).to_broadcast([128, 4, 128]), ALU.mult),
                             reads=[YIb, SMb], writes=[YIb])
                    P.op("dve", lambda e: e.tensor_tensor(fm_, YI[:, 0, 0:512], YI[:, 1, 0:512], ALU.add), reads=[YIb], writes=[FINb])
                    sig_exp(big1[:, 0:512], mo, TKb, big1b)
                    P.op("dve", lambda e: e.tensor_tensor(fm_, fm_, big1[:, 0:512], ALU.mult), reads=[FINb, big1b], writes=[FINb])
                    gnorm(fm_, FINb, 4, 128, rowv("mnw"), True)
                mm(B[0], B[0][0][:, :], lra[:, :], gw2[:, :], True, True, [lrb, cb_])
                P.op("act", lambda e: e.activation(LA[:, :], B[0][0][:, :], AF.Exp, scale=-1.0), reads=[B[0][1]], writes=[LAb])
                P.op("act", lambda e: e.activation(LA[:, :], LA[:, :], AF.Ln, bias=one1), reads=[LAb, self.onesb], writes=[LAb])
                P.op("dve", lambda e: e.tensor_scalar_mul(LA[:, :], LA[:, :], -1.0 / 16.0), reads=[LAb], writes=[LAb])
                for d in range(2):
                    for tt in range(2):
                        mm(B[7], B[7][0][:, (d * 2 + tt) * 128:(d * 2 + tt + 1) * 128], LA[:, d * 256 + tt * 128:d * 256 + (tt + 1) * 128],
                           TRI[d][:, :], True, True, [LAb, cb_])
                b7v = B[7][0][:, :].rearrange("p (d t m) -> p d t m", d=2, t=2)
                P.op("act", lambda e: e.activation(EBg[:, :, :, :], b7v, AF.Exp), reads=[B[7][1]], writes=[EBgb])
                P.op("act", lambda e: e.activation(ENg[:, :, :, :], b7v, AF.Exp, scale=-1.0), reads=[B[7][1]], writes=[ENgb])
                P.op("dve", lambda e: e.tensor_copy(GVH[:, :], gv), reads=[TKb], writes=[GVHb])
                for d in range(2):
                    P.op("dve", lambda e: e.scalar_tensor_tensor(QTg[:, d, :, :], GQ[:, 0:2, :], 0.125, EBg[:, d, :, :], ALU.mult, ALU.mult),
                         reads=[GQb, EBgb], writes=[QTgb])
                    P.op("dve", lambda e: e.tensor_tensor(KTf[:, :, :], GQ[:, 2:4, :], ENg[:, d, :, :], ALU.mult), reads=[GQb, ENgb], writes=[KTfb])
                    if mode == "S":
                        for tt in range(2):
                            bk = B[1 + tt]
                            P.op("pe", lambda e: e.transpose(bk[0][:, 0:128], KTf[:, tt, :], self.ident[:, :]), reads=[KTfb, self.identb], writes=[bk[1]])
                            P.op("act", lambda e: e.copy(KTK[:, tt, :], bk[0][:, 0:128]), reads=[bk[1]], writes=[KTKb])
                        for tt in range(2):
                            mm(B[3], B[3][0][:, tt * 256:(tt + 1) * 256], KTK[:, tt, :], GVH[:, tt * 256:(tt + 1) * 256], True, True, [KTKb, GVHb])
                        for tt in range(2):
                            decc = EBg[:, d, tt, ENDC[d]:ENDC[d] + 1]
                            for e_ in range(2):
                                P.op("dve", lambda e: e.tensor_scalar_mul(DSt[e_ * 64:(e_ + 1) * 64, tt * 128:(tt + 1) * 128],
                                                                          B[3][0][e_ * 64:(e_ + 1) * 64, tt * 256 + e_ * 128:tt * 256 + (e_ + 1) * 128],
                                                                          decc[e_ * 64:(e_ + 1) * 64, :]),
                                     reads=[B[3][1], EBgb], writes=[DStb])
                            P.op("dve", lambda e: e.tensor_copy(DSt[:, 256 + tt:257 + tt], decc), reads=[EBgb], writes=[DStb])
                        P.dma("act", self.DS["gla"].ap[c, d], DSt[:, 0:256], reads=[DStb], writes=self.DS["gla"].bufs(c, c + 1))
                        P.dma("act", self.DEC["gla"].ap[c, d], DSt[:, 256:258], reads=[DStb], writes=self.DEC["gla"].bufs(c, c + 1))
                    elif "g" in getattr(self, "obr", "smg"):
                        P.op("dve", lambda e: e.tensor_copy(KTg[:, d, :, :], KTf[:, :, :]), reads=[KTfb], writes=[KTgb])
                        for h in range(4):
                            pb_ = (h % 2) * 64
                            bk_ = B[0] if h % 2 == 0 else B[5]
                            mm(bk_, bk_[0][:, (h // 2) * 128:(h // 2 + 1) * 128], KTg[pb_:pb_ + 64, d, h // 2, :], QTg[pb_:pb_ + 64, d, h // 2, :],
                               True, True, [KTgb, QTgb])
                        for h in range(4):
                            bk_ = B[0] if h % 2 == 0 else B[5]
                            P.op("dve", lambda e: e.tensor_tensor(AT[:, h, :], bk_[0][:, (h // 2) * 128:(h // 2 + 1) * 128], M01[d][:, :], ALU.mult),
                                 reads=[bk_[1], cb_], writes=[ATb])
                        P.dma("sp", ST[:, 0:256], self.ENT["gla"].ap[c, d], reads=self.ENT["gla"].bufs(c, c + 1), writes=[STb])
                        for h in range(4):
                            pb_ = (h % 2) * 64
                            o_ = B[3][0][:, h * 128:(h + 1) * 128]
                            mm(B[3], o_, AT[:, h, :], GVH[:, h * 128:(h + 1) * 128], True, False, [ATb, GVHb], inc=False)
                            mm(B[3], o_, QTg[pb_:pb_ + 64, d, h // 2, :], ST[pb_:pb_ + 64, (h // 2) * 128:(h // 2 + 1) * 128], False, True,
                               [QTgb, STb], inc=True)
                        fg_ = FIN[:, 1536:2048]
                        if d == 0:
                            P.op("act", lambda e: e.copy(fg_, B[3][0][:, :]), reads=[B[3][1]], writes=[FINb])
                        else:
                            P.op("dve", lambda e: e.tensor_tensor(fg_, fg_, B[3][0][:, :], ALU.add), reads=[FINb, B[3][1]], writes=[FINb])
                if mode == "O":
                    fg = FIN[:, 1536:2048]
                    gnorm(fg, FINb, 4, 128, rowv("gnw"), True)
                    sig_exp(big1[:, 0:512], gr, TKb, big1b)
                    P.op("dve", lambda e: e.tensor_tensor(fg, fg, gr, ALU.mult), reads=[FINb, TKb], writes=[FINb])
                    P.op("dve", lambda e: e.tensor_tensor(fg, fg, big1[:, 0:512], ALU.mult), reads=[FINb, big1b], writes=[FINb])
                    for kt in range(16):
                        bk = B[1 + kt % 2]
                        P.op("pe", lambda e: e.transpose(bk[0][:, 0:128], FIN[:, kt * 128:(kt + 1) * 128], self.ident[:, :]),
                             reads=[FINb, self.identb], writes=[bk[1]])
                        P.op("act", lambda e: e.copy(FT[:, kt, :], bk[0][:, 0:128]), reads=[bk[1]], writes=[FTb])
                    P.dma("act", self.FINT.fm()[:, :, r0:r1], FT[:, :, :], reads=[FTb], writes=self.FINT.bufs(c, c + 1))
            P.barrier()

    def phase_recur(self, l):
        P, nc = self.P, self.nc
        KINDS = (("ssd", 64, 1024, "ssd", 16, 0), ("ml", 128, 512, "ml", 4, 1024), ("mln", 128, 4, "ml", 4, 1536),
                 ("gla", 128, 256, "gla", 2, 1540))
        with ExitStack() as es:
            sb = lambda n, s, dt=F32: es.enter_context(nc.sbuf_tensor(uniq(n), s, dt))
            T_ = {}
            for (nm, np_, w, dkey, dw, xo) in KINDS:
                T_[nm] = dict(S=sb("r_s" + nm, [128, w]), Sb=Buf(),
                              Sh=[(sb("r_sh%s%d" % (nm, i), [128, w], BF16), Buf()) for i in range(2)],
                              Dl=[(sb("r_d%s%d" % (nm, i), [128, w]), Buf()) for i in range(4)],
                              Dc=[(sb("r_c%s%d" % (nm, i), [128, 16]), Buf()) for i in range(4)],
                              X0=sb("r_x0" + nm, [128, w]), X1=sb("r_x1" + nm, [128, w]), Xb=Buf(), it=0)

            def run(nm, np_, w, dkey, dw, d, order):
                t = T_[nm]; S = t["S"]; Sb = t["Sb"]
                for c in order:
                    sh, shb = t["Sh"][t["it"] % 2]; dl, dlb = t["Dl"][t["it"] % 4]; dc, dcb = t["Dc"][t["it"] % 4]; t["it"] += 1
                    P.op("pool", lambda e: e.tensor_copy(sh[:np_, :], S[:np_, :]), reads=[Sb], writes=[shb])
                    P.dma("sp", self.ENT[nm].ap[c, d], sh[:np_, :], reads=[shb], writes=self.ENT[nm].bufs(c, c + 1))
                    P.dma("act", dl[:np_, :], self.DS[nm].ap[c, d], reads=self.DS[nm].bufs(c, c + 1), writes=[dlb])
                    P.dma("act", dc[:np_, 0:dw], self.DEC[dkey].ap[c, d], reads=self.DEC[dkey].bufs(c, c + 1), writes=[dcb])
                    if nm == "ssd":
                        sv = S[:np_, :].rearrange("p (h w) -> p h w", w=64)
                        P.op("dve", lambda e: e.tensor_tensor(sv, sv, dc[:np_, 0:16].unsqueeze(2).to_broadcast([np_, 16, 64]), ALU.mult),
                             reads=[Sb, dcb], writes=[Sb])
                    elif nm == "ml":
                        sv = S[:, :].rearrange("p (h w) -> p h w", w=128)
                        P.op("dve", lambda e: e.tensor_tensor(sv, sv, dc[:, 0:4].unsqueeze(2).to_broadcast([128, 4, 128]), ALU.mult),
                             reads=[Sb, dcb], writes=[Sb])
                    elif nm == "mln":
                        P.op("dve", lambda e: e.tensor_tensor(S[:, :], S[:, :], dc[:, 0:4], ALU.mult), reads=[Sb, dcb], writes=[Sb])
                    else:
                        sv = S[:, :].rearrange("p (t w) -> p t w", w=128)
                        P.op("dve", lambda e: e.tensor_tensor(sv, sv, dc[:, 0:2].unsqueeze(2).to_broadcast([128, 2, 128]), ALU.mult),
                             reads=[Sb, dcb], writes=[Sb])
                    P.op("dve", lambda e: e.tensor_tensor(S[:np_, :], S[:np_, :], dl[:np_, :], ALU.add), reads=[Sb, dlb], writes=[Sb])

            xob = self.XOUT.bufs(0, 1)
            for (nm, np_, w, dkey, dw, xo) in KINDS:
                t = T_[nm]
                P.op("dve", lambda e: e.memset(t["S"][:, :], 0.0), writes=[t["Sb"]])
                run(nm, np_, w, dkey, dw, 0, list(range(NCH)))
                P.dma("sp", self.XOUT.ap[0:np_, xo:xo + w], t["S"][:np_, :], reads=[t["Sb"]], writes=xob)
            P.collective(self.XOUT.ap, self.XALL.ap, reads=xob, writes=self.XALL.bufs(0, 1), dummy=self.dummy[:, :], dummyb=self.dummyb)
            for (nm, np_, w, dkey, dw, xo) in KINDS:
                t = T_[nm]
                P.op("dve", lambda e: e.memset(t["S"][:, :], 0.0), writes=[t["Sb"]])
                run(nm, np_, w, dkey, dw, 1, [1, 0])
                P.dma("sp", t["X0"][:np_, :], self.XALL.ap[0:np_, xo:xo + w], reads=self.XALL.bufs(0, 1) + [self.dummyb], writes=[t["Xb"]])
                P.dma("sp", t["X1"][:np_, :], self.XALL.ap[128:128 + np_, xo:xo + w], reads=self.XALL.bufs(0, 1) + [self.dummyb], writes=[t["Xb"]])
                P.op("dve", lambda e: e.tensor_scalar_mul(t["X0"][:np_, :], t["X0"][:np_, :], self.selt[:np_, 0:1]),
                     reads=[t["Xb"], self.seltb], writes=[t["Xb"]])
                P.op("dve", lambda e: e.scalar_tensor_tensor(t["S"][:np_, :], t["X1"][:np_, :], self.selt[:np_, 1:2], t["X0"][:np_, :],
                                                             ALU.mult, ALU.add), reads=[t["Xb"], self.seltb, t["Sb"]], writes=[t["Sb"]])
                run(nm, np_, w, dkey, dw, 1, list(range(NCH - 1, 1, -1)))
            P.barrier()

    def phase_halo(self, l):
        P, nc = self.P, self.nc
        hb = self.HALO_OUT.bufs(0, 1)
        P.dma("sp", self.HALO_OUT.ap[0:1536, :], self.XBC.ap[:, T - 64:T], reads=self.XBC.bufs(NCH - 1, NCH), writes=hb)
        P.dma("sp", self.HALO_OUT.ap[1536:2560, :], self.QKM.ap[:, T - 64:T], reads=self.QKM.bufs(NCH - 1, NCH), writes=hb)
        P.collective(self.HALO_OUT.ap, self.HALO_ALL.ap, reads=hb, writes=self.HALO_ALL.bufs(0, 1), dummy=self.dummy[:, :], dummyb=self.dummyb)
        o_anti, _ = CO["anti"]
        with ExitStack() as es:
            sb = lambda n, s, dt=F32: es.enter_context(nc.sbuf_tensor(uniq(n), s, dt))
            anti = sb("h_anti", [128, 128]); ab = Buf()
            P.dma("sp", anti[:, :], self.CONST.ap[:, o_anti:o_anti + 128], writes=[ab])
            H0 = sb("h_0", [128, 20, 64]); H1 = sb("h_1", [128, 20, 64]); Hb = Buf()
            TT = [(sb("h_t%d" % i, [64, 128]), Buf()) for i in range(2)]
            R = sb("h_r", [128, 20, 64]); Rb = Buf()
            rd = self.HALO_ALL.bufs(0, 1) + [self.dummyb]
            P.dma("sp", H0[:, :, :], self.HALO_ALL.ap[0:2560, :].rearrange("(c p) t -> p c t", p=128), reads=rd, writes=[Hb])
            P.dma("sp", H1[:, :, :], self.HALO_ALL.ap[2560:5120, :].rearrange("(c p) t -> p c t", p=128), reads=rd, writes=[Hb])
            P.op("dve", lambda e: e.tensor_scalar_mul(H0[:, :, :], H0[:, :, :], self.selt[:, 0:1]), reads=[Hb, self.seltb], writes=[Hb])
            P.op("dve", lambda e: e.scalar_tensor_tensor(H0[:, :, :], H1[:, :, :], self.selt[:, 1:2], H0[:, :, :], ALU.mult, ALU.add),
                 reads=[Hb, self.seltb], writes=[Hb])
            for ct in range(20):
                bk = self.banks[ct % 2]
                tt, ttb = TT[ct % 2]
                P.op("pe", lambda e: e.transpose(bk[0][0:64, 0:128], H0[:, ct, :], self.ident[:, :]), reads=[Hb, self.identb], writes=[bk[1]])
                P.op("act", lambda e: e.copy(tt[:, :], bk[0][0:64, 0:128]), reads=[bk[1]], writes=[ttb])
                bk2 = self.banks[2 + ct % 2]
                P.op("pe", lambda e: e.matmul(bk2[0][:, 0:64], tt[:, :], anti[0:64, 64:128], start=True, stop=True), reads=[ttb, ab], writes=[bk2[1]])
                P.op("act", lambda e: e.copy(R[:, ct, :], bk2[0][:, 0:64]), reads=[bk2[1]], writes=[Rb])
            P.dma("sp", self.HALO_REV.ap.rearrange("(c p) t -> p c t", p=128), R[:, :, :], reads=[Rb], writes=self.HALO_REV.bufs(0, 1))
            P.barrier()

    def phase_merge(self, l, blocks):
        P, nc = self.P, self.nc
        Wb = [self.w_br_ssd.ap[l].rearrange("(kc p) n -> p kc n", p=128), self.w_br_ml.ap[l].rearrange("(kc p) n -> p kc n", p=128),
              self.w_br_gla.ap[l].rearrange("(kc p) n -> p kc n", p=128)]
        Wo = self.w_out.ap[l].rearrange("(kc p) n -> p kc n", p=128)
        KR = [(0, 8), (8, 12), (12, 16)]
        om, _ = PO["mergeb"]
        with ExitStack() as es:
            sb = lambda n, s, dt=F32: es.enter_context(nc.sbuf_tensor(uniq(n), s, dt))
            F = sb("m_f", [128, 16, 1152], BF16); Fb = Buf()
            MG = sb("m_mg", [128, 16, 1152], BF16)
            WB = [(sb("m_w%d" % i, [128, 16, 128], BF16), Buf()) for i in range(3)]
            GPt2 = [[(sb("m_gp%d_%d" % (j, i), [128, 1152]), Buf()) for i in range(3)] for j in range(2)]
            Gs = [(sb("m_g%d" % i, [128, 512]), Buf()) for i in range(2)]
            ACC = [(sb("m_acc%d" % i, [128, 512]), Buf()) for i in range(2)]
            HB = [(sb("m_h%d" % i, [128, 1152]), Buf()) for i in range(2)]
            ZB = [(sb("m_z%d" % i, [128, 1152]), Buf()) for i in range(2)]
            wi = 0; gi = 0; ai = 0
            for (c0, c1) in blocks:
                nb = (c1 - c0) * 128
                t0 = c0 * 128
                subs = subs_for(c0, c1)
                MGb = [[Buf() for _ in subs] for _ in range(16)]
                P.dma("sp", F[:, :, :nb], self.FINT.fm()[:, :, t0:t0 + nb], reads=self.FINT.bufs(c0, c1), writes=[Fb])
                for fo in range(16):
                    wt, wb = WB[wi % 3]; wi += 1
                    GPt = GPt2[fo % 2]
                    for X in range(3):
                        P.dma("pool", wt[:, KR[X][0]:KR[X][1], :], Wb[X][:, :, fo * 128:(fo + 1) * 128], writes=[wb])
                        P.dma("sp", GPt[X][0][:, :nb], self.GP.ap[X * 2048 + fo * 128:X * 2048 + (fo + 1) * 128, t0:t0 + nb],
                              reads=self.GP.bufs(c0, c1), writes=[GPt[X][1]])
                    for si_, (o, n) in enumerate(subs):
                        acc, accb = ACC[ai % 2]; ai += 1
                        for X in range(3):
                            ps, pb = self.bank()
                            for kc in range(KR[X][0], KR[X][1]):
                                P.op("pe", lambda e: e.matmul(ps[:, :n], wt[:, kc, :], F[:, kc, o:o + n], start=(kc == KR[X][0]),
                                                              stop=(kc == KR[X][1] - 1)), reads=[wb, Fb], writes=[pb], inc=(kc == KR[X][1] - 1))
                            g, gb = Gs[gi % 2]; gi += 1
                            P.op("act", lambda e: e.activation(g[:, :n], GPt[X][0][:, o:o + n], AF.Sigmoid,
                                                               bias=self.par[:, om + X * 16 + fo:om + X * 16 + fo + 1]),
                                 reads=[GPt[X][1], self.parb], writes=[gb])
                            if X == 0:
                                P.op("dve", lambda e: e.tensor_tensor(acc[:, :n], ps[:, :n], g[:, :n], ALU.mult), reads=[pb, gb], writes=[accb])
                            else:
                                P.op("dve", lambda e: e.tensor_tensor(g[:, :n], ps[:, :n], g[:, :n], ALU.mult), reads=[pb, gb], writes=[gb])
                                if X == 1:
                                    P.op("dve", lambda e: e.tensor_tensor(acc[:, :n], acc[:, :n], g[:, :n], ALU.add), reads=[accb, gb], writes=[accb])
                                else:
                                    P.op("dve", lambda e: e.tensor_tensor(MG[:, fo, o:o + n], acc[:, :n], g[:, :n], ALU.add),
                                         reads=[accb, gb], writes=[MGb[fo][si_]])
                for fo in range(16):
                    wt, wb = WB[wi % 3]; wi += 1
                    P.dma("pool", wt[:, :, :], Wo[:, :, fo * 128:(fo + 1) * 128], writes=[wb])
                    hb, hbb = HB[fo % 2]
                    zb, zbb = ZB[fo % 2]
                    P.dma("sp", hb[:, :nb], self.HT.ap[fo * 128:(fo + 1) * 128, t0:t0 + nb], reads=self.HT.bufs(c0, c1), writes=[hbb])
                    for si_, (o, n) in enumerate(subs):
                        ps, pb = self.bank()
                        for kc in range(16):
                            P.op("pe", lambda e: e.matmul(ps[:, :n], wt[:, kc, :], MG[:, kc, o:o + n], start=(kc == 0), stop=(kc == 15)),
                                 reads=[wb, MGb[kc][si_]], writes=[pb], inc=(kc == 15))
                        v = 1 if (c0 == 0 and o == 0) else 0
                        P.op("dve", lambda e: e.scalar_tensor_tensor(zb[:, o:o + n], ps[:, :n], self.mvec(5, fo, v), hb[:, o:o + n],
                                                                     ALU.mult, ALU.add), reads=[pb, hbb, self.modb], writes=[zbb])
                    P.dma("sp", self.ZT.ap[fo * 128:(fo + 1) * 128, t0:t0 + nb], zb[:, :nb], reads=[zbb], writes=self.ZT.bufs(c0, c1))
            P.barrier()

    def phase_mixer(self, l):
        if not hasattr(self, "XBC"):
            self.alloc_mixer_scratch()
        self.phase_proj(l)
        self.phase_halo(l)
        self.phase_conv(l)
        self.scan_pass(l, "S", list(range(NCH)))
        self.phase_recur(l)
        if l < DEPTH - 1:
            self.scan_pass(l, "O", list(range(NCH)))
            self.phase_merge(l, [(0, 9), (9, 18)])
        else:
            self.scan_pass(l, "O", list(range(2, NCH)))
            self.phase_merge(l, [(2, 10), (10, 18)])

    def finish(self):
        P = self.P
        for i, (src, dst) in enumerate(self.dumps):
            P.dma("sp", dst.ap, src.ap, reads=list(src.b.values()), writes=[Buf()])
        P.barrier(engines=["sp"])
        self.es.close()
        return self.nc


def build_program(stop_after=None, dump=None):
    k = K(stop_after=stop_after, dump=dump)
    hsrc = k.xT
    for l in range(DEPTH):
        k.phase_mod(l)
        k.phase_modapply(hsrc, 0)
        k.phase_ffn(l, 0, 0, hsrc)
        hsrc = k.HT
        k.phase_ln(0, 1)
        if stop_after == "ffn0":
            break
        k.phase_mixer(l)
        last = (l == DEPTH - 1)
        k.phase_ln(1, 2, skip_ctx=last)
        k.phase_ffn(l, 1, 2, k.HT, blocks=(((2, 10), (10, 18)) if last else ((0, 9), (9, 18))))
        k.phase_ln(2, None, final=last)
    return k.finish()


def _swap_dirs(inp, l):
    d = {}
    d["ssd_conv_w"] = inp["ssd_conv_w"][l][::-1, ::-1, :]
    d["ml_conv_w"] = inp["ml_conv_w"][l][::-1, ::-1, :]
    d["ssd_dt_bias"] = inp["ssd_dt_bias"][l][::-1]
    d["ssd_a_log"] = inp["ssd_a_log"][l][::-1]
    d["ml_gate_b"] = inp["ml_gate_b"][l][::-1]
    d["gla_w2"] = inp["gla_w2"][l][::-1]
    d["gla_b2"] = inp["gla_b2"][l][::-1]
    return d


def host_inputs(inp, ncores=8):
    f = lambda a: np.ascontiguousarray(np.asarray(a, dtype=np.float32))
    inp = {k: f(v) for k, v in inp.items()}
    consts = make_consts()
    inp_odd = dict(inp)
    for k in ("ssd_conv_w", "ml_conv_w", "ssd_dt_bias", "ssd_a_log", "ml_gate_b", "gla_w2", "gla_b2"):
        inp_odd[k] = np.stack([_swap_dirs(inp, l)[k] for l in range(DEPTH)])
    perm = np.arange(INTOT)
    for (o, n) in ((OFF["dt"], 32), (OFF["mg"], 16), (OFF["lr"], 32)):
        h = n // 2
        perm[o:o + h] = np.arange(o + h, o + n)
        perm[o + h:o + n] = np.arange(o, o + h)
    w_in_odd = np.ascontiguousarray(inp["w_in"][:, :, perm])
    per = []
    for par, src, w_in in ((0, inp, inp["w_in"]), (1, inp_odd, w_in_odd)):
        per.append(dict(PAR=np.stack([make_par(src, l) for l in range(DEPTH)]), ROW=np.stack([make_row(src, l) for l in range(DEPTH)]),
                        GW2=np.stack([make_gw2(src, l) for l in range(DEPTH)]), w_in=w_in,
                        SELF=np.ascontiguousarray(np.broadcast_to(np.array([[0.0, 1.0]] if par == 0 else [[1.0, 0.0]], np.float32), (128, 2)))))
    shared = dict(CONST=consts, w_mod=inp["w_mod"], ffn_w_in=inp["ffn_w_in"], ffn_w_out=inp["ffn_w_out"],
                  w_br_ssd=inp["w_br_ssd"], w_br_ml=inp["w_br_ml"], w_br_gla=inp["w_br_gla"], w_out=inp["w_out"])
    maps = []
    for core in range(ncores):
        b, half = core // 2, core % 2
        cx = inp["ctx"][b]
        xx = inp["x"][b, half * NLT * 128:(half + 1) * NLT * 128]
        if half == 1:
            cx = cx[::-1]
            xx = xx[::-1]
        tok = np.concatenate([cx, xx], axis=0)
        xT = np.ascontiguousarray(tok.T)
        cv = np.stack([inp["c"][b], inp["c_ctx"]], axis=1)
        cT = np.ascontiguousarray(cv.reshape(16, 128, 2).transpose(1, 0, 2).reshape(128, 32))
        m = dict(shared)
        m.update(per[half])
        m["xT"] = xT
        m["cT"] = cT
        maps.append(m)
    return maps


def assemble(results, nb):
    out = []
    for b in range(nb):
        h0 = np.asarray(results[2 * b]["outT"]).T
        h1 = np.asarray(results[2 * b + 1]["outT"]).T[::-1]
        out.append(np.concatenate([h0, h1], axis=0))
    return np.stack(out, axis=0).astype(np.float32)


def kernel(**inputs):
    nc = build_program()
    maps = host_inputs(inputs)
    res = run_bass_kernel_spmd(nc, maps, core_ids=list(range(8)))
    return assemble(res.results, 4)
```
